# Optimizing a Trainium2 kernel written in Bass

```python
import math
import jax, jax.numpy as jnp
from jax import lax
import numpy as np

D_MODEL = 2048
BATCH = 4
SEQ = 2048
DEPTH = 1
DEC_BATCH = 128
DEC_SEQ = 4
PAST_LEN = 16384
PAGE_SIZE = 128

MIX_WIDTH = D_MODEL
GLA_WIDTH = MIX_WIDTH // 2
S5_WIDTH = MIX_WIDTH - GLA_WIDTH
GLA_HEADS = 4
GLA_DV = GLA_WIDTH // GLA_HEADS
GLA_DK = GLA_DV // 2
GLA_KW = GLA_HEADS * GLA_DK
GLA_RANK = 16
GLA_TAU = 16.0
GLA_CHUNK = 64
S5_CH = 16
S5_GROUPS = S5_WIDTH // S5_CH
S5_STATE = 64
S5_DT_MIN = 1e-3
S5_DT_MAX = 1e-1
D_FF = 5632
CONV_W = 3
NORM_EPS = 1e-6
IN_WIDTH = 2 * GLA_KW + 2 * GLA_WIDTH + GLA_RANK + S5_WIDTH

kernel_name = "hybrid_gla_s5_convffn_adaln_step"


def rmsnorm(x, gain):
    xf = x.astype(jnp.float32)
    y = xf * lax.rsqrt(jnp.mean(xf * xf, axis=-1, keepdims=True) + NORM_EPS)
    return (y * gain.astype(jnp.float32)).astype(x.dtype)


def modulate(h, shift, scale):
    return h * (1 + scale[:, None, :]) + shift[:, None, :]


def gla_chunked(q, k, v, log_a, s0):
    f32 = jnp.float32
    b, l = q.shape[0], q.shape[1]
    c = math.gcd(l, GLA_CHUNK)
    n = l // c

    def to_chunks(t):
        return jnp.moveaxis(t.astype(f32).reshape(b, n, c, t.shape[2], t.shape[3]), 1, 0).swapaxes(2, 3)

    qs = to_chunks(q) * (GLA_DK ** -0.5)
    ks, vs, gs = to_chunks(k), to_chunks(v), to_chunks(log_a)
    causal = jnp.tril(jnp.ones((c, c), dtype=bool))[:, :, None]

    def step(s, inp):
        qc, kc, vc, gc = inp
        cum = jnp.cumsum(gc, axis=2)
        rel = cum[:, :, :, None, :] - cum[:, :, None, :, :]
        decay = jnp.exp(jnp.where(causal, rel, -jnp.inf))
        att = jnp.einsum('bhtd,bhsd,bhtsd->bhts', qc, kc, decay)
        o = (jnp.einsum('bhts,bhsv->bhtv', att, vc)
             + jnp.einsum('bhtd,bhdv->bhtv', qc * jnp.exp(cum), s))
        last = cum[:, :, -1:, :]
        s_new = (jnp.exp(last[:, :, 0, :])[..., None] * s
                 + jnp.einsum('bhsd,bhsv->bhdv', kc * jnp.exp(last - cum), vc))
        return s_new, o

    s_fin, os_ = lax.scan(step, s0.astype(f32), (qs, ks, vs, gs))
    o = jnp.moveaxis(os_.swapaxes(2, 3), 0, 1).reshape(b, l, GLA_HEADS, GLA_DV)
    return o, s_fin


def s5_scan(u, lam_re, lam_im, log_step, b_re, b_im, c_re, c_im, d_skip, h0_re, h0_im):
    f32 = jnp.float32
    lam = lax.complex(lam_re.astype(f32), lam_im.astype(f32))
    dt = jnp.exp(log_step.astype(f32))[:, None]
    lam_bar = jnp.exp(lam * dt)
    bmat = lax.complex(b_re.astype(f32), b_im.astype(f32))
    b_bar = ((lam_bar - 1) / lam)[..., None] * bmat
    cmat = lax.complex(c_re.astype(f32), c_im.astype(f32))
    uf = u.astype(f32)
    bu = jnp.einsum('gpk,blgk->blgp', b_bar, uf.astype(jnp.complex64))
    h0 = lax.complex(h0_re.astype(f32), h0_im.astype(f32))
    bu = bu.at[:, 0].add(lam_bar * h0)
    a = jnp.broadcast_to(lam_bar, bu.shape)

    def combine(e1, e2):
        a1, b1 = e1
        a2, b2 = e2
        return a1 * a2, a2 * b1 + b2

    _, h = lax.associative_scan(combine, (a, bu), axis=1)
    y = jnp.real(jnp.einsum('gkp,blgp->blgk', cmat, h)) + d_skip.astype(f32) * uf
    h_last = h[:, -1]
    return y, jnp.real(h_last), jnp.imag(h_last)


def layer(x, c, gla_s, s5_re, s5_im, conv_s, w):
    bsz, l = x.shape[0], x.shape[1]
    mod = jnp.einsum('bd,de->be', jax.nn.silu(c), w['w_ada']) + w['b_ada']
    sh1, sc1, g1, sh2, sc2, g2 = jnp.split(mod, 6, axis=-1)

    h = modulate(rmsnorm(x, w['norm1']), sh1, sc1)
    proj = jnp.einsum('bld,de->ble', h, w['w_in'])
    cuts = [GLA_KW, 2 * GLA_KW, 2 * GLA_KW + GLA_WIDTH, 2 * GLA_KW + 2 * GLA_WIDTH,
            2 * GLA_KW + 2 * GLA_WIDTH + GLA_RANK]
    q, k, v, g, a_lr, u = jnp.split(proj, cuts, axis=-1)
    q = q.reshape(bsz, l, GLA_HEADS, GLA_DK)
    k = k.reshape(bsz, l, GLA_HEADS, GLA_DK)
    v = v.reshape(bsz, l, GLA_HEADS, GLA_DV)
    log_a = jax.nn.log_sigmoid((a_lr @ w['w_a2'] + w['b_a2']).astype(jnp.float32)) / GLA_TAU
    log_a = log_a.reshape(bsz, l, GLA_HEADS, GLA_DK)
    o, gla_new = gla_chunked(q, k, v, log_a, gla_s)
    o = rmsnorm(o.astype(x.dtype), w['gla_norm']).reshape(bsz, l, GLA_WIDTH) * jax.nn.silu(g)

    y5, re_new, im_new = s5_scan(u.reshape(bsz, l, S5_GROUPS, S5_CH), w['s5_lam_re'], w['s5_lam_im'],
                                 w['s5_log_step'], w['s5_b_re'], w['s5_b_im'], w['s5_c_re'],
                                 w['s5_c_im'], w['s5_d'], s5_re, s5_im)
    y5 = jax.nn.gelu(y5.reshape(bsz, l, S5_WIDTH).astype(x.dtype))
    y5 = y5 * jax.nn.sigmoid(y5 @ w['w_glu'] + w['b_glu'])

    mix = jnp.concatenate([o, y5], axis=-1) @ w['w_out']
    x = x + g1[:, None, :] * mix

    h = modulate(rmsnorm(x, w['norm2']), sh2, sc2)
    up = jnp.einsum('bld,df->blf', h, w['w_up'])
    xpad = jnp.concatenate([conv_s.astype(up.dtype), up], axis=1)
    conv = lax.conv_general_dilated(xpad, w['conv_w'][:, None, :].astype(up.dtype), (1,), 'VALID',
                                    dimension_numbers=('NWC', 'WIO', 'NWC'),
                                    feature_group_count=2 * D_FF) + w['conv_b']
    a_, gt = jnp.split(conv, 2, axis=-1)
    ff = (jax.nn.gelu(a_) * gt) @ w['w_down']
    x = x + g2[:, None, :] * ff
    return x, gla_new, re_new, im_new, xpad[:, -(CONV_W - 1):]


def setup_inputs(seed: int = 0) -> dict:
    key = jax.random.key(seed)
    ks = jax.random.split(key, 40)
    f32 = jnp.float32
    nrm = lambda i, shape, s: jax.random.normal(ks[i], shape, f32) * s
    n_idx = jnp.arange(S5_STATE, dtype=f32)
    lam_re = -0.5 + nrm(10, (DEPTH, S5_GROUPS, S5_STATE), 1e-3)
    lam_im = jnp.broadcast_to(math.pi * n_idx, (DEPTH, S5_GROUPS, S5_STATE)).astype(f32)
    log_step = jax.random.uniform(ks[11], (DEPTH, S5_GROUPS), f32,
                                  math.log(S5_DT_MIN), math.log(S5_DT_MAX))
    return {
        "x_prompt": nrm(0, (BATCH, SEQ, D_MODEL), 1.0),
        "x_sample": nrm(1, (DEC_BATCH, DEC_SEQ, D_MODEL), 1.0),
        "c_prompt": nrm(2, (BATCH, D_MODEL), 1.0),
        "c_sample": nrm(3, (DEC_BATCH, D_MODEL), 1.0),
        "state_gla": nrm(4, (DEPTH, DEC_BATCH, GLA_HEADS, GLA_DK, GLA_DV), 0.5),
        "state_s5_re": nrm(5, (DEPTH, DEC_BATCH, S5_GROUPS, S5_STATE), 0.1),
        "state_s5_im": nrm(6, (DEPTH, DEC_BATCH, S5_GROUPS, S5_STATE), 0.1),
        "state_conv": nrm(7, (DEPTH, DEC_BATCH, CONV_W - 1, 2 * D_FF), 1.0),
        "w_ada": nrm(8, (DEPTH, D_MODEL, 6 * D_MODEL), 0.5 * D_MODEL ** -0.5),
        "b_ada": nrm(9, (DEPTH, 6 * D_MODEL), 0.02),
        "norm1": 1.0 + nrm(12, (DEPTH, D_MODEL), 0.01),
        "w_in": nrm(13, (DEPTH, D_MODEL, IN_WIDTH), D_MODEL ** -0.5),
        "w_a2": nrm(14, (DEPTH, GLA_RANK, GLA_KW), GLA_RANK ** -0.5),
        "b_a2": nrm(15, (DEPTH, GLA_KW), 0.1),
        "gla_norm": 1.0 + nrm(16, (DEPTH, GLA_DV), 0.01),
        "s5_lam_re": lam_re,
        "s5_lam_im": lam_im,
        "s5_log_step": log_step,
        "s5_b_re": nrm(17, (DEPTH, S5_GROUPS, S5_STATE, S5_CH), (2 * S5_CH) ** -0.5),
        "s5_b_im": nrm(18, (DEPTH, S5_GROUPS, S5_STATE, S5_CH), (2 * S5_CH) ** -0.5),
        "s5_c_re": nrm(19, (DEPTH, S5_GROUPS, S5_CH, S5_STATE), S5_STATE ** -0.5),
        "s5_c_im": nrm(20, (DEPTH, S5_GROUPS, S5_CH, S5_STATE), S5_STATE ** -0.5),
        "s5_d": nrm(21, (DEPTH, S5_GROUPS, S5_CH), 1.0),
        "w_glu": nrm(22, (DEPTH, S5_WIDTH, S5_WIDTH), S5_WIDTH ** -0.5),
        "b_glu": nrm(23, (DEPTH, S5_WIDTH), 0.02),
        "w_out": nrm(24, (DEPTH, MIX_WIDTH, D_MODEL), MIX_WIDTH ** -0.5),
        "norm2": 1.0 + nrm(25, (DEPTH, D_MODEL), 0.01),
        "w_up": nrm(26, (DEPTH, D_MODEL, 2 * D_FF), D_MODEL ** -0.5),
        "conv_w": nrm(27, (DEPTH, CONV_W, 2 * D_FF), CONV_W ** -0.5),
        "conv_b": nrm(28, (DEPTH, 2 * D_FF), 0.02),
        "w_down": nrm(29, (DEPTH, D_FF, D_MODEL), D_FF ** -0.5),
        "final_norm": 1.0 + nrm(30, (D_MODEL,), 0.01),
    }


def reference(x_prompt, x_sample, c_prompt, c_sample, state_gla, state_s5_re, state_s5_im, state_conv,
              w_ada, b_ada, norm1, w_in, w_a2, b_a2, gla_norm, s5_lam_re, s5_lam_im, s5_log_step,
              s5_b_re, s5_b_im, s5_c_re, s5_c_im, s5_d, w_glu, b_glu, w_out, norm2, w_up, conv_w,
              conv_b, w_down, final_norm):
    f32 = jnp.float32
    bp = x_prompt.shape[0]
    yp, ys = x_prompt, x_sample
    gla_p, re_p, im_p, conv_p = [], [], [], []
    gla_s, re_s, im_s, conv_s = [], [], [], []
    for i in range(DEPTH):
        w = {
            'w_ada': w_ada[i], 'b_ada': b_ada[i], 'norm1': norm1[i], 'w_in': w_in[i],
            'w_a2': w_a2[i], 'b_a2': b_a2[i], 'gla_norm': gla_norm[i],
            's5_lam_re': s5_lam_re[i], 's5_lam_im': s5_lam_im[i], 's5_log_step': s5_log_step[i],
            's5_b_re': s5_b_re[i], 's5_b_im': s5_b_im[i], 's5_c_re': s5_c_re[i], 's5_c_im': s5_c_im[i],
            's5_d': s5_d[i], 'w_glu': w_glu[i], 'b_glu': b_glu[i], 'w_out': w_out[i],
            'norm2': norm2[i], 'w_up': w_up[i], 'conv_w': conv_w[i], 'conv_b': conv_b[i],
            'w_down': w_down[i],
        }
        yp, g_new, r_new, m_new, c_new = layer(
            yp, c_prompt,
            jnp.zeros((bp, GLA_HEADS, GLA_DK, GLA_DV), f32),
            jnp.zeros((bp, S5_GROUPS, S5_STATE), f32),
            jnp.zeros((bp, S5_GROUPS, S5_STATE), f32),
            jnp.zeros((bp, CONV_W - 1, 2 * D_FF), yp.dtype), w)
        gla_p.append(g_new); re_p.append(r_new); im_p.append(m_new); conv_p.append(c_new)
        ys, g_new, r_new, m_new, c_new = layer(
            ys, c_sample, state_gla[i], state_s5_re[i], state_s5_im[i], state_conv[i], w)
        gla_s.append(g_new); re_s.append(r_new); im_s.append(m_new); conv_s.append(c_new)
    y_prompt = rmsnorm(yp, final_norm)
    y_sample = rmsnorm(ys, final_norm)
    return (y_prompt, y_sample,
            jnp.stack(gla_p), jnp.stack(re_p), jnp.stack(im_p), jnp.stack(conv_p),
            jnp.stack(gla_s), jnp.stack(re_s), jnp.stack(im_s), jnp.stack(conv_s))
```

```python
import math
from contextlib import ExitStack
import numpy as np
import concourse.bass as bass
import concourse.mybir as mybir
from concourse.bass_utils import run_bass_kernel_spmd

F32 = mybir.dt.float32
BF16 = mybir.dt.bfloat16
I32 = mybir.dt.int32
AF = mybir.ActivationFunctionType
ALU = mybir.AluOpType

D = 2048
KC = 16
TM = 1024
TS = 64
TF = TM + TS
TT = TF + 128
NTILE = 10
NTO = 9
TILES = [(i * 128, 128) for i in range(8)] + [(1024, 64), (1088, 128)]
CGRP = [(0, 512), (512, 512), (1024, 64), (1088, 128)]
NCH = 160
NCP = 112
DFF = 5632
EPS = 1e-6
NCORES = 8


class Prog:
    ENGS = ['pe', 'act', 'dve', 'pool', 'sp']

    def __init__(self, nc, es):
        self.nc = nc
        self.es = es
        self.sem = {e: es.enter_context(nc.semaphore("s_" + e)) for e in ['pe', 'act', 'dve', 'pool']}
        self.cnt = {e: 0 for e in ['pe', 'act', 'dve', 'pool']}
        self.dsem = {}
        self.dcnt = {}
        self.dpool = [es.enter_context(nc.semaphore("dq%d" % i)) for i in range(64)]
        self.seen = {e: {} for e in self.ENGS}
        self.psi = 0
        self.nph = 0
        self.reset()

    def reset(self):
        self.q = {e: [] for e in self.ENGS}
        self.lw = {k: v for k, v in getattr(self, 'lw', {}).items() if v[0].startswith('d:')}
        self.rd = {k: [t for t in v if t[0].startswith('d:')] for k, v in getattr(self, 'rd', {}).items()}

    def _waits(self, e, r, w):
        deps = {}
        def add(tok):
            k, v = tok
            if k == 'pe' and e == 'pe':
                return
            if deps.get(k, 0) < v:
                deps[k] = v
        for k in r:
            if k in self.lw:
                add(self.lw[k])
        for k in w:
            if k in self.lw:
                add(self.lw[k])
            for t in self.rd.get(k, ()):
                add(t)
        out = []
        for k, v in deps.items():
            if self.seen[e].get(k, 0) < v:
                self.seen[e][k] = v
                out.append((k, v))
        return out

    def _commit(self, tok, r, w):
        for k in w:
            self.lw[k] = tok
            self.rd[k] = []
        for k in r:
            self.rd.setdefault(k, []).append(tok)

    def op(self, e, fn, r=(), w=(), inc=True):
        assert inc or e == 'pe'
        waits = self._waits(e, r, w)
        if inc:
            self.cnt[e] += 1
            tok = (e, self.cnt[e])
        else:
            tok = (e, self.cnt[e] + 1)
        self.q[e].append((waits, fn, e if inc else None))
        self._commit(tok, r, w)

    def dma(self, e, fn, r=(), w=(), sem='x'):
        waits = self._waits(e, r, w)
        if sem not in self.dsem:
            self.dsem[sem] = self.dpool.pop()
            self.dcnt[sem] = 0
        self.dcnt[sem] += 16
        tok = ('d:' + sem, self.dcnt[sem])
        self.q[e].append((waits, fn, 'd:' + sem))
        self._commit(tok, r, w)

    def settle(self, keys, sem):
        tok = ('d:' + sem, self.dcnt[sem])
        for k in keys:
            self.lw[k] = tok

    def semobj(self, k):
        return self.dsem[k[2:]] if k.startswith('d:') else self.sem[k]

    def emit(self, final=False):
        nc = self.nc
        if final:
            for s, v in self.dcnt.items():
                k = 'd:' + s
                if self.seen['sp'].get(k, 0) < v:
                    self.seen['sp'][k] = v
                    self.q['sp'].append(([(k, v)], None, None))
        q = self.q
        me = self

        def replay(eng, items):
            for waits, fn, kind in items:
                for k, v in waits:
                    eng.wait_ge(me.semobj(k), v)
                if fn is None:
                    continue
                ins = fn(eng)
                if kind is None:
                    continue
                if kind.startswith('d:'):
                    ins.then_inc(me.dsem[kind[2:]], 16)
                else:
                    ins.then_inc(me.sem[kind], 1)

        self.nph += 1
        with nc.Block() as block:
            @block.tensor
            def _(eng):
                replay(eng, q['pe'])

            @block.scalar
            def _(eng):
                replay(eng, q['act'])

            @block.vector
            def _(eng):
                replay(eng, q['dve'])

            @block.gpsimd
            def _(eng):
                replay(eng, q['pool'])

            @block.sync
            def _(eng):
                replay(eng, q['sp'])
        self.reset()


import os
STOP = int(os.environ.get('KSTOP', '99'))


def build_program(debug=False):
    nc = bass.Bass("TRN2", target_bir_lowering=False)

    def din(name, shape, dt=F32):
        return nc.dram_tensor(name, list(shape), dt, kind="ExternalInput").ap()

    def dout(name, shape):
        return nc.dram_tensor(name, list(shape), F32, kind="ExternalOutput").ap()

    xm = din("xm", [TM, D]); xs = din("xs", [TS, D]); cc = din("cc", [17, D])
    xpre = din("xpre", [TM, D]); c_mask = din("c_mask", [128, 1])
    sgla = din("sgla", [16, 4, 128, 256]); s5in = din("s5in", [2, 16, 64, 64]); sconv = din("sconv", [32, 2 * DFF])
    w_ada = din("w_ada", [D, 6 * D]); b_ada = din("b_ada", [1, 6 * D]); nrm = din("nrm", [3, D])
    w_in = din("w_in", [D, 4112]); w_a2 = din("w_a2", [16, 512]); b_a2 = din("b_a2", [1, 512])
    gla_norm = din("gla_norm", [1, 256])
    lam = din("lam", [2, 64, 64]); lstep = din("lstep", [1, 64])
    sb = din("sb", [2, 64, 64, 16]); sc = din("sc", [2, 64, 16, 64]); sd = din("sd", [64, 16])
    w_glu = din("w_glu", [1024, 1024]); b_glu = din("b_glu", [1, 1024])
    w_out = din("w_out", [D, D]); w_up = din("w_up", [D, 2 * DFF]); cw = din("cw", [4, 2 * DFF])
    w_down = din("w_down", [DFF, D])
    c_ident = din("c_ident", [128, 128]); c_gm = din("c_gm", [128, 6, 128]); c_seq = din("c_seq", [64, 2, 16])
    c_bm = din("c_bm", [128, 16, 64]); c_ev = din("c_ev", [128, 34]); c_sg = din("c_sg", [128, 2])
    c_cm = din("c_cm", [128, 128])

    yp = dout("yp", [TM, D]); ys = dout("ys", [TS, D]); glap = dout("glap", [4, 128, 256])
    s5p = dout("s5p", [64, 128]); convp = dout("convp", [2, 2 * DFF])
    glas = dout("glas", [16, 4, 128, 256]); s5s = dout("s5s", [16, 64, 128]); convs = dout("convs", [64, 2 * DFF])
    dbg = dout("dbg", [128, 16, TT]) if debug else None

    modd = nc.dram_tensor("modd", [17, 6 * D], F32).ap()
    dU = nc.dram_tensor("dU", [64, 16, 8, NCH], BF16).ap()
    dY = nc.dram_tensor("dY", [64, 16, 8, NCH], BF16).ap()
    dU2 = nc.dram_tensor("dU2", [64, 16, 8, NCP], BF16).ap()

    with ExitStack() as es:
        P = Prog(nc, es)

        def sb_t(stack, name, shape, dt=F32):
            return stack.enter_context(nc.sbuf_tensor(name, list(shape), dt))

        ps = [es.enter_context(nc.psum_tensor("ps%d" % i, [128, 512], F32)) for i in range(8)]

        held = set()

        def nps():
            while P.psi in held:
                P.psi = (P.psi + 1) % 8
            i = P.psi
            P.psi = (P.psi + 1) % 8
            return ps[i], ('ps', i)

        ident = sb_t(es, "ident", [128, 128])
        seq = sb_t(es, "seq", [64, 2, 16])
        sgc = sb_t(es, "sgc", [128, 2])
        maskt = sb_t(es, "maskt", [128, 1])
        scT = sb_t(es, "scT", [128, KC, 17], BF16)
        nrmT = sb_t(es, "nrmT", [128, KC, 3])
        modT = sb_t(es, "modT", [128, 4, KC, 17])
        Amod = sb_t(es, "Amod", [128, 2, KC, 17])
        cwT = sb_t(es, "cwT", [128, 88, 4])
        bgT = sb_t(es, "bgT", [128, 8])
        gnT = sb_t(es, "gnT", [128, 2])
        wa2 = sb_t(es, "wa2", [16, 512], BF16)
        ba2 = sb_t(es, "ba2", [1, 512], BF16)
        ones = sb_t(es, "ones", [1, 128], BF16)
        actT = sb_t(es, "actT", [128, KC, TT], BF16)
        wr = [None] * 4

        nring = [4]

        def alloc_ring(stack, n=4, width=256):
            nring[0] = n
            wri[0] = 0
            for i in range(n):
                wr[i] = sb_t(stack, "wr%d_%d" % (i, P.nph), [128, KC, width], BF16)
        wri = [0]

        def nwr():
            i = wri[0]
            wri[0] = (i + 1) % nring[0]
            return wr[i], ('wr', i)

        def mm(out, lhsT, rhs, start, stop, r, w, inc=True):
            P.op('pe', lambda e, o=out, l=lhsT, rr=rhs, s=start, t=stop: e.matmul(o, lhsT=l, rhs=rr, start=s, stop=t), r, w, inc)

        def tr(out, in_, r, w, inc=True):
            n = in_.shape[0]
            p0 = in_.base_partition()
            P.op('pe', lambda e, o=out, i=in_, n=n, p0=p0: e.transpose(o, i, ident[p0:p0 + n, p0:p0 + n]), r, w, inc)

        def act(out, in_, func, r, w, bias=None, scale=None, accum=None):
            kw = {}
            if bias is not None:
                kw['bias'] = bias
            if scale is not None:
                kw['scale'] = scale
            if accum is not None:
                kw['accum_out'] = accum
            P.op('act', lambda e, o=out, i=in_, f=func, kw=kw: e.activation(out=o, in_=i, func=f, **kw), r, w)

        def tt(eng, out, in0, in1, op, r, w):
            P.op(eng, lambda e, o=out, a=in0, b=in1, op=op: e.tensor_tensor(out=o, in0=a, in1=b, op=op), r, w)

        def stt(out, in0, scalar, in1, op0, op1, r, w):
            P.op('dve', lambda e, o=out, a=in0, s=scalar, b=in1, o0=op0, o1=op1:
                 e.scalar_tensor_tensor(out=o, in0=a, scalar=s, in1=b, op0=o0, op1=o1), r, w)

        def tsc(eng, out, in0, s1, s2, op0, op1, r, w):
            if s2 is None:
                P.op(eng, lambda e, o=out, a=in0, s1=s1, o0=op0: e.tensor_scalar(out=o, in0=a, scalar1=s1, scalar2=None, op0=o0), r, w)
            else:
                P.op(eng, lambda e, o=out, a=in0, s1=s1, s2=s2, o0=op0, o1=op1:
                     e.tensor_scalar(out=o, in0=a, scalar1=s1, scalar2=s2, op0=o0, op1=o1), r, w)

        def cp(eng, out, in_, r, w):
            if eng == 'act':
                P.op('act', lambda e, o=out, i=in_: e.copy(out=o, in_=i), r, w)
            else:
                P.op(eng, lambda e, o=out, i=in_: e.tensor_copy(out=o, in_=i), r, w)

        def recip(out, in_, r, w):
            P.op('dve', lambda e, o=out, i=in_: e.reciprocal(out=o, in_=i), r, w)

        def memset(eng, ap, val, w):
            P.op(eng, lambda e, a=ap, v=val: e.memset(a, v), (), w)

        def dma(eng, out, in_, r, w, sem):
            P.dma(eng, lambda e, o=out, i=in_: e.dma_start(out=o, in_=i), r, w, sem)

        evi = [0]

        def evac_eng():
            evi[0] ^= 1
            return 'act' if evi[0] else 'dve'

        def rstd_chain(ssq, tmp, out, n, scale, key):
            tsc('dve', tmp, ssq, scale, EPS, ALU.mult, ALU.add, [key], [key])
            P.op('act', lambda e, o=tmp, i=tmp: e.activation(out=o, in_=i, func=AF.Ln), [key], [key])
            P.op('act', lambda e, o=out, i=tmp: e.activation(out=o, in_=i, func=AF.Exp, scale=-0.5), [key], [key])

        def rows_to_fm(stack_rows, nrows, width, out_view_fn, rkey, wkey):
            nchunks = width // 128
            per = max(1, 512 // nrows)
            c = 0
            while c < nchunks:
                n = min(per, nchunks - c)
                pt, pk = nps()
                for i in range(n):
                    tr(pt[:, i * nrows:(i + 1) * nrows], stack_rows[0:nrows, (c + i) * 128:(c + i + 1) * 128], [rkey], [pk], inc=(i == n - 1))
                cp('dve', out_view_fn(c, n), pt[:, 0:n * nrows].rearrange("p (c r) -> p c r", r=nrows), [pk], [wkey])
                c += n

        class AdaWork:
            def __init__(self, stack, parts):
                self.modst = [sb_t(stack, "modst%d_%d" % (i, P.nph), [17, D]) for i in range(2)]
                self.bad = [sb_t(stack, "bad%d_%d" % (i, P.nph), [17, D]) for i in range(2)]
                self.todo = [(part, i8) for part in parts for i8 in range(8)]
                self.pt = None

            def step(self, k):
                for _ in range(k):
                    if not self.todo:
                        return
                    part, i8 = self.todo.pop(0)
                    pb = part % 2
                    modst, bad = self.modst[pb], self.bad[pb]
                    if i8 == 0:
                        dma('sp', bad[:], b_ada[:, part * D:(part + 1) * D].partition_broadcast(17), [], [('bad', pb)], 'bad%d' % pb)
                    i = part * 8 + i8
                    wt, wk = nwr()
                    dma('pool', wt[:], w_ada[:, i * 256:(i + 1) * 256].rearrange("(c p) n -> p c n", p=128), [], [wk], 'wr%d' % wk[1])
                    if i % 2 == 0:
                        self.pt = nps()
                    pt, pk = self.pt
                    for kc in range(KC):
                        mm(pt[0:17, (i % 2) * 256:(i % 2) * 256 + 256], scT[:, kc, :], wt[:, kc, :], kc == 0, kc == KC - 1,
                           ['scT', wk], [pk], inc=(kc == KC - 1))
                    if i % 2 == 1:
                        tt('dve', modst[:, (i8 - 1) * 256:(i8 + 1) * 256], pt[0:17, :], bad[:, (i8 - 1) * 256:(i8 + 1) * 256], ALU.add,
                           [pk, ('bad', pb)], [('modst', pb)])
                    if i8 == 7:
                        dma('sp', modd[:, part * D:(part + 1) * D], modst[:], [('modst', pb)], ['modd'], 'modd%d' % pb)
                        if part in (0, 1, 3, 4):
                            qi = {0: 0, 1: 1, 3: 2, 4: 3}[part]
                            rows_to_fm(modst, 17, D, lambda c, n, qi=qi: modT[:, qi, c:c + n, :], ('modst', pb), 'modT')
                        if part in (1, 4):
                            n_ = 0 if part == 1 else 1
                            tsc('dve', Amod[:, n_], modT[:, 1 + 2 * n_], 1.0, None, ALU.add, None, ['modT'], ['Amod'])
                            tt('dve', Amod[:, n_], Amod[:, n_], nrmT[:, :, n_:n_ + 1].to_broadcast([128, KC, 17]), ALU.mult,
                               ['Amod', 'nrmT'], ['Amod'])

        with ExitStack() as ph:
            cct = sb_t(ph, "cct", [17, D]); sct = sb_t(ph, "sct", [17, D])
            nrmt = sb_t(ph, "nrmt", [3, D])
            cwr = sb_t(ph, "cwr", [4, DFF]); smr = sb_t(ph, "smr", [1, 1280])
            smT = sb_t(ph, "smT", [128, 10, 1])
            alloc_ring(ph)
            init_keys = ['ident', 'seq', 'sgc', 'cct', 'nrmt', 'smr', 'mask']
            for (o, i) in [(ident[:], c_ident), (seq[:], c_seq), (maskt[:], c_mask), (sgc[:], c_sg), (cct[:], cc),
                           (nrmt[:], nrm), (smr[:, 0:1024], b_glu), (smr[:, 1024:1280], gla_norm)]:
                dma('sp', o, i, [], [], 'init')
            P.settle(init_keys, 'init')
            dma('pool', wa2[:], w_a2, [], ['wa2'], 'wa2')
            dma('pool', ba2[:], b_a2, [], ['wa2'], 'wa2')
            memset('dve', ones[:], 1.0, ['ones'])
            act(sct[:], cct[:], AF.Silu, ['cct'], ['sct'])
            rows_to_fm(sct, 17, D, lambda c, n: scT[:, c:c + n, :], 'sct', 'scT')
            rows_to_fm(nrmt, 3, D, lambda c, n: nrmT[:, c:c + n, :], 'nrmt', 'nrmT')
            ada = AdaWork(ph, [0, 1])
            ada.step(16)
            for blk in range(2):
                dma('sp', cwr[:], cw[:, blk * DFF:(blk + 1) * DFF], [], ['cwr'], 'cwr')
                rows_to_fm(cwr, 4, DFF, lambda c, n, blk=blk: cwT[:, blk * 44 + c:blk * 44 + c + n, :], 'cwr', 'cwT')
            rows_to_fm(smr, 1, 1280, lambda c, n: smT[:, c:c + n, :], 'smr', 'smT')
            cp('dve', bgT[:], smT[:, 0:8, 0], ['smT'], ['bgT'])
            cp('dve', gnT[:], smT[:, 8:10, 0], ['smT'], ['gnT'])
            P.emit(final=(STOP == 0))
        if STOP == 0:
            return nc

        def norm_phase(which, src_tile_fn, pid, tiles=None, dstT=None, kname='actT', post=None, ada_parts=None, ada_k=0):
            tiles = list(enumerate(TILES)) if tiles is None else tiles
            dstT = actT if dstT is None else dstT
            with ExitStack() as ph:
                xn = [sb_t(ph, "xn%d_%d" % (i, P.nph), [128, D]) for i in range(2)]
                xnb = [sb_t(ph, "xnb%d_%d" % (i, P.nph), [128, D], BF16) for i in range(2)]
                nsm = [sb_t(ph, "nsm%d_%d" % (i, P.nph), [128, 16, 4]) for i in range(2)]
                identb = sb_t(ph, "identb_%d" % P.nph, [128, 128], BF16)
                cp('dve', identb[:], ident[:], ['ident'], ['identb'])
                st = [sb_t(ph, "nst%d_%d" % (i, P.nph), [128, 4]) for i in range(2)]
                adaw = None
                if ada_parts:
                    alloc_ring(ph)
                    adaw = AdaWork(ph, ada_parts)
                for li, (ti, (t0, nt)) in enumerate(tiles):
                    b = li % 2
                    if adaw is not None:
                        adaw.step(ada_k)
                    xt, xk = src_tile_fn(ph, ti, t0, nt)
                    act(xn[b][0:nt, :], xt, AF.Square, [xk], [('xn', b), ('st', b)], accum=st[b][0:nt, 0:1])
                    rstd_chain(st[b][0:nt, 0:1], st[b][0:nt, 1:2], st[b][0:nt, 2:3], nt, 1.0 / D, ('st', b))
                    act(xnb[b][0:nt, :], xt, AF.Copy, [xk, ('st', b)], [('xnb', b)], scale=st[b][0:nt, 2:3])
                    for k4 in range(4):
                        pt32, pk = nps()
                        pt = pt32[:, :].bitcast(BF16)
                        for i in range(4):
                            kc = k4 * 4 + i
                            P.op('pe', lambda e, o=pt[:, i * nt:(i + 1) * nt], a=xnb[b][0:nt, kc * 128:(kc + 1) * 128], n_=nt:
                                 e.transpose(o, a, identb[0:n_, 0:n_]), [('xnb', b), 'identb'], [pk], inc=(i == 3))
                        for i in range(4):
                            kc = k4 * 4 + i
                            dst = dstT[:, kc, t0:t0 + nt]
                            src = pt[:, i * nt:(i + 1) * nt]
                            if ti != 8:
                                a_ap = Amod[:, which, kc, 0:1]
                                b_ap = modT[:, 2 * which, kc, 0:1]
                                if evac_eng() == 'act':
                                    act(dst, src, AF.Identity, [pk, 'Amod', 'modT'], [(kname, ti)], bias=b_ap, scale=a_ap)
                                else:
                                    tsc('dve', dst, src, a_ap, b_ap, ALU.mult, ALU.add, [pk, 'Amod', 'modT'], [(kname, ti)])
                            else:
                                s3 = src.rearrange("p (j i) -> p j i", i=4)
                                stmp = nsm[kc % 2]
                                tt('dve', stmp[:], s3, Amod[:, which, kc, 1:17].unsqueeze(2).to_broadcast([128, 16, 4]), ALU.mult,
                                   [pk, 'Amod'], [('nsm', kc % 2)])
                                s3 = stmp[:]
                                tt('dve', dst.rearrange("p (j i) -> p j i", i=4), s3,
                                   modT[:, 2 * which, kc, 1:17].unsqueeze(2).to_broadcast([128, 16, 4]), ALU.add,
                                   [('nsm', kc % 2), 'modT'], [(kname, ti)])
                if adaw is not None:
                    adaw.step(99)
                if post is not None:
                    post()
                P.emit(final=(STOP == pid))

        def x_src(ph, ti, t0, nt):
            if not hasattr(x_src, 'bufs'):
                x_src.bufs = [sb_t(ph, "xb%d" % i, [128, D]) for i in range(2)]
            b = x_src.n % 2
            x_src.n += 1
            src = xm[t0:t0 + nt, :] if ti < 8 else (xs if ti == 8 else xpre[896:1024, :])
            dma('sp', x_src.bufs[b][0:nt, :], src, [], [('xb', b)], 'xb%d' % b)
            return x_src.bufs[b][0:nt, :], ('xb', b)
        x_src.n = 0

        gx = ExitStack()
        gx.__enter__()
        gm = sb_t(gx, "gm", [128, 6, 128])
        Sst = sb_t(gx, "Sst", [128, 4, 256]); Sbf = sb_t(gx, "Sbf", [128, 4, 256], BF16)

        with ExitStack() as a0:
            hTp = sb_t(a0, "hTp", [128, KC, 896], BF16)
            PT = [(i, (i * 128, 128)) for i in range(7)]

            def xpre_src(ph, ti, t0, nt):
                if not hasattr(xpre_src, 'bufs'):
                    xpre_src.bufs = [sb_t(ph, "xpb%d" % i, [128, D]) for i in range(2)]
                b = ti % 2
                dma('sp', xpre_src.bufs[b][:], xpre[t0:t0 + nt, :], [], [('xpb', b)], 'xb%d' % b)
                return xpre_src.bufs[b][:], ('xpb', b)
            norm_phase(0, xpre_src, -1, tiles=PT, dstT=hTp, kname='hTp', ada_parts=[2, 3], ada_k=3)
            with ExitStack() as ph:
                alloc_ring(ph)
                ktp = sb_t(ph, "ktp", [128, 7, 512], BF16); vtp = sb_t(ph, "vtp", [128, 7, 1024], BF16)
                alp = sb_t(ph, "alp", [16, 896], BF16); uTq = sb_t(ph, "uTq", [128, 8, 8, NCP], BF16)
                lap = [sb_t(ph, "lap%d" % i, [128, 512]) for i in range(2)]
                erp = [sb_t(ph, "erp%d" % i, [128, 512]) for i in range(2)]
                khp = [sb_t(ph, "khp%d" % i, [128, 512], BF16) for i in range(2)]
                dcp = [sb_t(ph, "dcp%d" % i, [128, 4]) for i in range(2)]
                dma('sp', gm[:], c_gm, [], ['gm'], 'gm')
                memset('dve', Sst[:], 0.0, ['S'])
                hreads = [('hTp', i) for i in range(7)]
                pieces = [('k', 512), ('k', 768)] + [('v', 1024 + 256 * i) for i in range(4)] + [('a', 3072)] + [('u', 3088 + 256 * i) for i in range(4)]
                for (kind, c0) in pieces:
                    ncol = 16 if kind == 'a' else 256
                    wt, wk = nwr()
                    dma('pool', wt[:, :, 0:ncol], w_in[:, c0:c0 + ncol].rearrange("(c p) n -> p c n", p=128), [], [wk], 'wr%d' % wk[1])
                    if kind in ('k', 'v'):
                        for ti in range(7):
                            if ti % 2 == 0:
                                pt, pk = nps()
                            o0 = (ti % 2) * 256
                            for kc in range(KC):
                                mm(pt[:, o0:o0 + 256], hTp[:, kc, ti * 128:(ti + 1) * 128], wt[:, kc, :], kc == 0, kc == KC - 1, [wk, ('hTp', ti)], [pk], inc=(kc == KC - 1))
                            dst = ktp if kind == 'k' else vtp
                            dc0 = (c0 - 512) if kind == 'k' else (c0 - 1024)
                            cp(evac_eng(), dst[:, ti, dc0:dc0 + 256], pt[:, o0:o0 + 256], [pk], [kind + 'tp'])
                    elif kind == 'a':
                        for (g0, gn) in [(0, 512), (512, 384)]:
                            pt, pk = nps()
                            for kc in range(KC):
                                mm(pt[0:16, 0:gn], wt[:, kc, 0:16], hTp[:, kc, g0:g0 + gn], kc == 0, kc == KC - 1, [wk] + hreads, [pk], inc=(kc == KC - 1))
                            cp(evac_eng(), alp[:, g0:g0 + gn], pt[0:16, 0:gn], [pk], ['alp'])
                    else:
                        idx = (c0 - 3088) // 128
                        for half in range(2):
                            ch = idx + half
                            for (g0, gn) in [(0, 512), (512, 384)]:
                                pt, pk = nps()
                                for kc in range(KC):
                                    mm(pt[:, 0:gn], wt[:, kc, half * 128:(half + 1) * 128], hTp[:, kc, g0:g0 + gn], kc == 0, kc == KC - 1, [wk] + hreads, [pk], inc=(kc == KC - 1))
                                cp(evac_eng(), uTq[:, ch, :, g0 // 8:(g0 + gn) // 8], pt[:, 0:gn].rearrange("p (n s) -> p s n", s=8), [pk], ['uTq'])
                for ch in range(8):
                    dma('sp', dU2[ch * 8:(ch + 1) * 8].rearrange("g k s n -> (g k) s n"), uTq[:, ch], ['uTq'], ['dU2'], 'dU')
                for ti in range(7):
                    b = ti % 2
                    K = lambda n, b=b: (n, b)
                    t0 = ti * 128
                    pz, pzk = nps()
                    mm(pz[:, :], alp[:, t0:t0 + 128], wa2[:], True, False, ['alp', 'wa2'], [pzk], inc=False)
                    mm(pz[:, :], ones[0:1, 0:128], ba2[:], False, True, ['ones', 'wa2'], [pzk])
                    act(lap[b][:], pz[:, :], AF.Exp, [pzk], [K('lap')], scale=-1.0)
                    act(lap[b][:], lap[b][:], AF.Ln, [K('lap')], [K('lap')], bias=1.0)
                    prl, prk = nps()
                    mm(prl[:, :], gm[:, 1, :], lap[b][:], True, True, [K('lap'), 'gm'], [prk])
                    pc, pck = nps()
                    for h in range(4):
                        mm(pc[:, h:h + 1], lap[b][:, h * 128:(h + 1) * 128], gm[:, 0, 127:128], True, True, [K('lap'), 'gm'], [pck], inc=(h == 3))
                    act(dcp[b][:], pc[:, 0:4], AF.Exp, [pck], [K('dcp')])
                    act(erp[b][:], prl[:, :], AF.Exp, [prk], [K('erp')])
                    tt('pool', khp[b][:], erp[b][:], ktp[:, ti, :], ALU.mult, [K('erp'), 'ktp'], [K('khp')])
                    for hp in range(2):
                        pss, psk = nps()
                        for hh in range(2):
                            h = hp * 2 + hh
                            mm(pss[:, hh * 256:(hh + 1) * 256], khp[b][:, h * 128:(h + 1) * 128], vtp[:, ti, h * 256:(h + 1) * 256], True, True,
                               [K('khp'), 'vtp'], [psk], inc=(hh == 1))
                        for hh in range(2):
                            h = hp * 2 + hh
                            stt(Sst[:, h, :], Sst[:, h, :], dcp[b][:, h:h + 1], pss[:, hh * 256:(hh + 1) * 256], ALU.mult, ALU.add, ['S', K('dcp'), psk], ['S'])
                P.emit()

        norm_phase(0, x_src, 1, ada_parts=[4, 5], ada_k=2)
        if STOP == 1:
            return nc

        with ExitStack() as mx:
            qT = sb_t(mx, "qT", [128, 4, TT], BF16); kT = sb_t(mx, "kT", [128, 4, TT], BF16)
            ktok = sb_t(mx, "ktok", [128, NTILE, 512], BF16); vtok = sb_t(mx, "vtok", [128, NTILE, 1024], BF16)
            sgT = sb_t(mx, "sgT", [128, 8, TT], BF16); alT = sb_t(mx, "alT", [16, TT], BF16)

            with ExitStack() as ph:
                uTp = sb_t(ph, "uTp", [128, 8, 8, NCH], BF16)
                alloc_ring(ph)
                memset('pool', uTp[:, :, 4:8, 144:160], 0.0, ['uTp'])
                pieces = [('q', 0, 0), ('q', 256, 2), ('k', 512, 0), ('k', 768, 2)]
                pieces += [('v', 1024 + 256 * i, i) for i in range(4)]
                pieces += [('g', 2048 + 256 * i, 2 * i) for i in range(4)]
                pieces += [('a', 3072, 0)]
                pieces += [('u', 3088 + 256 * i, 2 * i) for i in range(4)]
                for (kind, c0, idx) in pieces:
                    ncol = 16 if kind == 'a' else 256
                    wt, wk = nwr()
                    dma('pool', wt[:, :, 0:ncol], w_in[:, c0:c0 + ncol].rearrange("(c p) n -> p c n", p=128), [], [wk], 'wr%d' % wk[1])
                    areads = [('actT', ti) for ti in range(NTILE)]
                    if kind in ('q', 'k', 'g', 'u'):
                        for half in range(2):
                            ch = idx + half
                            for (g0, gn) in CGRP:
                                pt, pk = nps()
                                for kc in range(KC):
                                    mm(pt[:, 0:gn], wt[:, kc, half * 128:(half + 1) * 128], actT[:, kc, g0:g0 + gn], kc == 0, kc == KC - 1,
                                       [wk] + areads, [pk], inc=(kc == KC - 1))
                                if kind == 'q':
                                    cp(evac_eng(), qT[:, ch, g0:g0 + gn], pt[:, 0:gn], [pk], ['qT'])
                                elif kind == 'k':
                                    cp(evac_eng(), kT[:, ch, g0:g0 + gn], pt[:, 0:gn], [pk], ['kT'])
                                elif kind == 'g':
                                    act(sgT[:, ch, g0:g0 + gn], pt[:, 0:gn], AF.Silu, [pk], ['sgT'])
                                else:
                                    if g0 != 1024:
                                        n0 = 16 + g0 // 8 if g0 < 1024 else 0
                                        cp(evac_eng(), uTp[:, ch, :, n0:n0 + gn // 8], pt[:, 0:gn].rearrange("p (n s) -> p s n", s=8), [pk], ['uTp'])
                                    else:
                                        cp(evac_eng(), uTp[:, ch, 0:4, 144:160], pt[:, 0:64].rearrange("p (j i) -> p i j", i=4), [pk], ['uTp'])
                    if kind in ('k', 'v'):
                        for ti, (t0, nt) in enumerate(TILES):
                            if ti % 2 == 0:
                                pt, pk = nps()
                            o0 = (ti % 2) * 256
                            for kc in range(KC):
                                mm(pt[0:nt, o0:o0 + 256], actT[:, kc, t0:t0 + nt], wt[:, kc, :], kc == 0, kc == KC - 1,
                                   [wk, ('actT', ti)], [pk], inc=(kc == KC - 1))
                            dst = ktok if kind == 'k' else vtok
                            dc0 = (c0 - 512) if kind == 'k' else (c0 - 1024)
                            cp(evac_eng(), dst[0:nt, ti, dc0:dc0 + 256], pt[0:nt, o0:o0 + 256], [pk], [kind + 'tok'])
                    if kind == 'a':
                        for (g0, gn) in CGRP:
                            pt, pk = nps()
                            for kc in range(KC):
                                mm(pt[0:16, 0:gn], wt[:, kc, 0:16], actT[:, kc, g0:g0 + gn], kc == 0, kc == KC - 1,
                                   [wk] + areads, [pk], inc=(kc == KC - 1))
                            cp(evac_eng(), alT[:, g0:g0 + gn], pt[0:16, 0:gn], [pk], ['alT'])
                for ch in range(8):
                    dma('sp', dU[ch * 8:(ch + 1) * 8].rearrange("g k s n -> (g k) s n"), uTp[:, ch], ['uTp'], ['dU'], 'dU')
                P.emit(final=(STOP == 2))
            if STOP == 2:
                return nc

            with ExitStack() as ph:
                NB = 2
                la = [sb_t(ph, "la%d" % i, [128, 512]) for i in range(NB)]
                eq = [sb_t(ph, "eq%d" % i, [128, 4, 128]) for i in range(NB)]
                ek = [sb_t(ph, "ek%d" % i, [128, 4, 128]) for i in range(NB)]
                er = [sb_t(ph, "er%d" % i, [128, 512]) for i in range(NB)]
                qtl = [sb_t(ph, "qtl%d" % i, [128, 4, 128], BF16) for i in range(NB)]
                ktl = [sb_t(ph, "ktl%d" % i, [128, 4, 128], BF16) for i in range(NB)]
                khat = [sb_t(ph, "khat%d" % i, [128, 512], BF16) for i in range(NB)]
                att = [sb_t(ph, "att%d" % i, [128, 4, 128], BF16) for i in range(NB)]
                on = [sb_t(ph, "on%d" % i, [128, 4, 256]) for i in range(NB)]
                hs = [sb_t(ph, "hs%d" % i, [128, 12]) for i in range(NB)]
                dec = [sb_t(ph, "dec%d" % i, [128, 4]) for i in range(NB)]
                decs = sb_t(ph, "decs", [128, 4, 16])
                qf = sb_t(ph, "qf", [128, 4, 64])
                qexp = [sb_t(ph, "qexp%d" % i, [128, 4, 64]) for i in range(2)]
                kexp = [sb_t(ph, "kexp%d" % i, [64, 4, 128], BF16) for i in range(2)]
                bm = sb_t(ph, "bm", [128, 16, 64])
                sst = [sb_t(ph, "sst%d" % i, [128, 4, 256]) for i in range(2)]
                sno = [sb_t(ph, "sno%d" % i, [128, 4, 256]) for i in range(2)]
                dma('sp', bm[:], c_bm, [], ['bm'], 'bm')
                cp('act', Sbf[:], Sst[:], ['S'], ['Sbf'])
                for oi, ti in enumerate([9, 0, 1, 2, 3, 4, 5, 6, 7, 8]):
                    t0, nt = TILES[ti]
                    b = oi % NB
                    smp = ti == 8
                    K = lambda n, b=b: (n, b)
                    TRI = gm[0:nt, 3 if smp else 0, 0:nt]
                    UP = gm[0:nt, 4 if smp else 1, 0:nt]
                    M01 = gm[0:nt, 5 if smp else 2, 0:nt]
                    pz, pzk = nps()
                    mm(pz[0:nt, :], alT[:, t0:t0 + nt], wa2[:], True, False, ['alT', 'wa2'], [pzk], inc=False)
                    mm(pz[0:nt, :], ones[0:1, 0:nt], ba2[:], False, True, ['ones', 'wa2'], [pzk])
                    act(la[b][0:nt, :], pz[0:nt, :], AF.Exp, [pzk], [K('la')], scale=-1.0)
                    act(la[b][0:nt, :], la[b][0:nt, :], AF.Ln, [K('la')], [K('la')], bias=1.0)
                    pc, pck = nps()
                    for h in range(4):
                        mm(pc[:, h * nt:(h + 1) * nt], la[b][0:nt, h * 128:(h + 1) * 128], TRI, True, True, [K('la'), 'gm'], [pck], inc=(h == 3))
                    prl, prk = nps()
                    mm(prl[0:nt, :], UP, la[b][0:nt, :], True, True, [K('la'), 'gm'], [prk])
                    pc3 = pc[:, 0:4 * nt].rearrange("p (h t) -> p h t", h=4)
                    act(eq[b][:, :, 0:nt], pc3, AF.Exp, [pck], [K('eq')])
                    act(ek[b][:, :, 0:nt], pc3, AF.Exp, [pck], [K('ek')], scale=-1.0)
                    if not smp:
                        act(dec[b][:], pc3[:, :, nt - 1], AF.Exp, [pck], [K('dec')])
                    act(er[b][0:nt, :], prl[0:nt, :], AF.Exp, [prk], [K('er')])
                    stt(qtl[b][:, :, 0:nt], eq[b][:, :, 0:nt], 128 ** -0.5, qT[:, :, t0:t0 + nt], ALU.mult, ALU.mult, [K('eq'), 'qT'], [K('qtl')])
                    tt('dve', ktl[b][:, :, 0:nt], ek[b][:, :, 0:nt], kT[:, :, t0:t0 + nt], ALU.mult, [K('ek'), 'kT'], [K('ktl')])
                    tt('pool', khat[b][0:nt, :], er[b][0:nt, :], ktok[0:nt, ti, :], ALU.mult, [K('er'), 'ktok'], [K('khat')])
                    pa, pak = nps()
                    for h in range(4):
                        mm(pa[0:nt, h * nt:(h + 1) * nt], ktl[b][:, h, 0:nt], qtl[b][:, h, 0:nt], True, True, [K('ktl'), K('qtl')], [pak], inc=(h == 3))
                    tt('dve', att[b][0:nt, :, 0:nt], pa[0:nt, 0:4 * nt].rearrange("p (h t) -> p h t", h=4),
                       M01.unsqueeze(1).to_broadcast([nt, 4, nt]), ALU.mult, [pak, 'gm'], [K('att')])
                    po = [nps(), nps()]
                    if smp:
                        po4 = [po[0], po[1], nps(), nps()]
                        for (_, pk_) in po4:
                            held.add(pk_[1])
                        obank = lambda h: (po4[h][0][0:nt, 0:256], po4[h][1])
                    else:
                        obank = lambda h: (po[h // 2][0][0:nt, (h % 2) * 256:(h % 2) * 256 + 256], po[h // 2][1])
                    if not smp:
                        for h in range(4):
                            oo, pk = obank(h)
                            mm(oo, att[b][0:nt, h, 0:nt], vtok[0:nt, ti, h * 256:(h + 1) * 256], True, False, [K('att'), 'vtok'], [pk], inc=False)
                            mm(oo, qtl[b][:, h, 0:nt], Sbf[:, h, :], False, True, [K('qtl'), 'Sbf'], [pk], inc=(h % 2 == 1))
                    else:
                        pd, pdk = nps()
                        for h in range(4):
                            mm(pd[:, h * 16:(h + 1) * 16], la[b][0:nt, h * 128:(h + 1) * 128], seq[:, 1, :], True, True, [K('la'), 'seq'], [pdk], inc=(h == 3))
                        act(decs[:], pd[:, 0:64].rearrange("p (h j) -> p h j", h=4), AF.Exp, [pdk], ['decs'])
                        stt(qf[:], eq[b][:, :, 0:nt], 128 ** -0.5, qT[:, :, t0:t0 + nt], ALU.mult, ALU.mult, [K('eq'), 'qT'], ['qf'])
                        for h in range(4):
                            oo, pk = obank(h)
                            mm(oo, att[b][0:nt, h, 0:nt], vtok[0:nt, ti, h * 256:(h + 1) * 256], True, False, [K('att'), 'vtok'], [pk], inc=False)
                        for j in range(16):
                            sbi = j % 2
                            tt('dve', qexp[sbi][:], qf[:], bm[:, j, :].unsqueeze(1).to_broadcast([128, 4, 64]), ALU.mult, ['qf', 'bm'], [('qexp', sbi)])
                            tsc('pool', kexp[sbi][:].rearrange("p h d -> p (h d)"), khat[b][0:64, :], seq[:, 0, j:j + 1], None, ALU.mult, None,
                                [K('khat'), 'seq'], [('kexp', sbi)])
                            dma('sp', sst[sbi][:], sgla[j].rearrange("h d v -> d h v"), [], [('sst', sbi)], 'sst%d' % sbi)
                            for h in range(4):
                                oo, pk = obank(h)
                                mm(oo, qexp[sbi][:, h, :], sst[sbi][:, h, :], False, j == 15, [('qexp', sbi), ('sst', sbi)], [pk],
                                   inc=(j == 15 or h == 3))
                            for hp in range(2):
                                pss, psk = nps()
                                for hh in range(2):
                                    h = hp * 2 + hh
                                    mm(pss[:, hh * 256:(hh + 1) * 256], kexp[sbi][:, h, :], vtok[0:64, ti, h * 256:(h + 1) * 256], True, True,
                                       [('kexp', sbi), 'vtok'], [psk], inc=(hh == 1))
                                for hh in range(2):
                                    h = hp * 2 + hh
                                    stt(sno[sbi][:, h, :], sst[sbi][:, h, :], decs[:, h, j:j + 1], pss[:, hh * 256:(hh + 1) * 256],
                                        ALU.mult, ALU.add, [('sst', sbi), 'decs', psk], [('sno', sbi)])
                            dma('sp', glas[j].rearrange("h d v -> d h v"), sno[sbi][:], [('sno', sbi)], [], 'sno%d' % sbi)
                    held.clear()
                    for h in range(4):
                        oo, pk = obank(h)
                        act(on[b][0:nt, h, :], oo, AF.Square, [pk], [K('on'), K('hs')], accum=hs[b][0:nt, h:h + 1])
                    rstd_chain(hs[b][0:nt, 0:4], hs[b][0:nt, 4:8], hs[b][0:nt, 8:12], nt, 1.0 / 256, K('hs'))
                    for h in range(4):
                        oo, pk = obank(h)
                        act(on[b][0:nt, h, :], oo, AF.Copy, [pk, K('hs')], [K('on')], scale=hs[b][0:nt, 8 + h:9 + h])
                    for c4 in range(2):
                        pt, pk = nps()
                        for i in range(4):
                            ch = c4 * 4 + i
                            tr(pt[:, i * nt:(i + 1) * nt], on[b][0:nt, ch // 2, (ch % 2) * 128:(ch % 2) * 128 + 128], [K('on')], [pk], inc=(i == 3))
                        for i in range(4):
                            ch = c4 * 4 + i
                            stt(actT[:, ch, t0:t0 + nt], pt[:, i * nt:(i + 1) * nt], gnT[:, ch % 2:ch % 2 + 1], sgT[:, ch, t0:t0 + nt],
                                ALU.mult, ALU.mult, [pk, 'gnT', 'sgT'], [('actT', ti)])
                    if not smp:
                        for hp in range(2):
                            pss, psk = nps()
                            for hh in range(2):
                                h = hp * 2 + hh
                                mm(pss[:, hh * 256:(hh + 1) * 256], khat[b][0:nt, h * 128:(h + 1) * 128], vtok[0:nt, ti, h * 256:(h + 1) * 256],
                                   True, True, [K('khat'), 'vtok'], [psk], inc=(hh == 1))
                            for hh in range(2):
                                h = hp * 2 + hh
                                stt(Sst[:, h, :], Sst[:, h, :], dec[b][:, h:h + 1], pss[:, hh * 256:(hh + 1) * 256], ALU.mult, ALU.add,
                                    ['S', K('dec'), psk], ['S'])
                                cp('act', Sbf[:, h, :], Sst[:, h, :], ['S'], ['Sbf'])
                    if ti == 9:
                        tsc('dve', Sst[:].rearrange("p h v -> p (h v)"), Sst[:].rearrange("p h v -> p (h v)"), maskt[:, 0:1], None, ALU.mult, None, ['S', 'mask'], ['S'])
                        cp('act', Sbf[:], Sst[:], ['S'], ['Sbf'])
                    if ti == 7:
                        dma('sp', glap.rearrange("h d v -> d h v"), Sst[:], ['S'], [], 'glap')
                P.emit(final=(STOP == 3))
            if STOP == 3:
                return nc
        gx.close()

        def dbg_dump():
            if debug:
                dma('pool', dbg, actT[:], [('actT', ti) for ti in range(NTILE)], [], 'dbg')

        PI = math.pi
        with ExitStack() as s5x:
            A_su = sb_t(s5x, "A_su", [128, 64, 128], BF16)
            A_ys = sb_t(s5x, "A_ys", [128, 64, 128], BF16)
            A_yu = sb_t(s5x, "A_yu", [128, 64, 128], BF16)
            LL = sb_t(s5x, "LL", [128, 4, 64])
            s5y = ExitStack()
            s5y.__enter__()
            Pr2 = sb_t(s5y, "Pr2", [128, 64, 34]); Pi2 = sb_t(s5y, "Pi2", [128, 64, 34])
            Bbs = sb_t(s5y, "Bbs", [128, 64, 16]); Bbw = sb_t(s5y, "Bbw", [128, 64, 16])
            Cs = sb_t(s5y, "Cs", [128, 64, 16]); Cw = sb_t(s5y, "Cw", [128, 64, 16])
            Dcol = sb_t(s5y, "Dcol", [128, 64]); cm = sb_t(s5y, "cm", [128, 128])
            with ExitStack() as ph:
                lamt = sb_t(ph, "lamt", [64, 2, 128]); lre2 = sb_t(ph, "lre2", [128, 2, 64])
                dtb = sb_t(ph, "dtb", [128, 64]); ev = sb_t(ph, "ev", [128, 34])
                ld = sb_t(ph, "ld", [128, 2, 64])
                targ = sb_t(ph, "targ", [128, 64, 34]); tang = sb_t(ph, "tang", [128, 64, 34]); ttmp = sb_t(ph, "ttmp", [128, 64, 34])
                fw = sb_t(ph, "fw", [128, 10, 64]); tint = sb_t(ph, "tint", [128, 64, 34], I32)
                Bs = sb_t(ph, "Bs", [128, 64, 16]); Bw = sb_t(ph, "Bw", [128, 64, 16])
                bt1 = sb_t(ph, "bt1", [128, 64, 16]); bt2 = sb_t(ph, "bt2", [128, 64, 16])
                Cl = sb_t(ph, "Cl", [128, 8, 2, 64]); Clw = sb_t(ph, "Clw", [128, 8, 2, 64])
                sdt = sb_t(ph, "sdt", [64, 16]); sde = sb_t(ph, "sde", [64, 16, 8])
                ik = []
                for c_ in range(2):
                    for hh in range(2):
                        dma('sp', lamt[:, c_, hh * 64:(hh + 1) * 64], lam[c_], [], [], 'init2')
                dma('sp', dtb[:], lstep.partition_broadcast(128), [], [], 'init2')
                dma('sp', ev[:], c_ev, [], [], 'init2')
                dma('sp', cm[:], c_cm, [], [], 'init2')
                dma('sp', sdt[:], sd, [], [], 'init2')
                for q4 in range(4):
                    gs = slice(q4 * 16, (q4 + 1) * 16)
                    for half, (sa, sw) in enumerate([(0, 1), (1, 0)]):
                        ps_ = slice(half * 64, (half + 1) * 64)
                        dma('sp', Bs[ps_, gs, :], sb[sa, gs].rearrange("g p k -> p g k"), [], [], 'init2')
                        dma('sp', Bw[ps_, gs, :], sb[sw, gs].rearrange("g p k -> p g k"), [], [], 'init2')
                for c_ in range(2):
                    dma('sp', Cl[:, :, c_, :], sc[c_].rearrange("(gb gl) j p -> (gl j) gb p", gl=8), [], [], 'init2')
                    dma('sp', Clw[:, :, 1 - c_, :], sc[c_].rearrange("(gb gl) j p -> (gl j) gb p", gl=8), [], [], 'init2')
                P.settle(['lamt', 'dtb', 'ev', 'cm', 'sdt', 'Bs', 'Bw', 'Cl', 'Clw'], 'init2')
                for c_ in range(2):
                    pt, pk = nps()
                    tr(pt[:, 0:64], lamt[:, c_, :], ['lamt'], [pk])
                    cp('dve', lre2[:, c_, :], pt[:, 0:64], [pk], ['lre2'])
                act(dtb[:], dtb[:], AF.Exp, ['dtb'], ['dtb'])
                tt('dve', ld[:], lre2[:], dtb[:].unsqueeze(1).to_broadcast([128, 2, 64]), ALU.mult, ['lre2', 'dtb'], ['ld'])
                tt('dve', targ[:], ld[:, 0, :].unsqueeze(2).to_broadcast([128, 64, 34]), ev[:].unsqueeze(1).to_broadcast([128, 64, 34]), ALU.mult, ['ld', 'ev'], ['targ'])
                tt('pool', tang[:], ld[:, 1, :].unsqueeze(2).to_broadcast([128, 64, 34]), ev[:].unsqueeze(1).to_broadcast([128, 64, 34]), ALU.mult, ['ld', 'ev'], ['tang'])
                act(targ[:], targ[:], AF.Exp, ['targ'], ['targ'])
                for (dst, shift) in [(Pr2, PI / 2), (Pi2, 0.0)]:
                    tsc('dve', ttmp[:], tang[:], 1.0 / (2 * PI), (shift + PI) / (2 * PI) + 64.0, ALU.mult, ALU.add, ['tang'], ['ttmp'])
                    cp('dve', tint[:], ttmp[:], ['ttmp'], ['tint'])
                    cp('dve', Pi2[:], tint[:], ['tint'], ['Pi2'])
                    tt('dve', ttmp[:], ttmp[:], Pi2[:], ALU.subtract, ['ttmp', 'Pi2'], ['ttmp'])
                    tsc('dve', Pi2[:], ttmp[:], 0.0, None, ALU.is_lt, None, ['ttmp'], ['Pi2'])
                    tt('dve', ttmp[:], ttmp[:], Pi2[:], ALU.add, ['ttmp', 'Pi2'], ['ttmp'])
                    act(ttmp[:], ttmp[:], AF.Sin, ['ttmp'], ['ttmp'], bias=-PI, scale=2 * PI)
                    tt('dve', dst[:], ttmp[:], targ[:], ALU.mult, ['ttmp', 'targ'], [dst.name])
                lr, li = lre2[:, 0, :], lre2[:, 1, :]
                FK = ['fw']
                tsc('dve', fw[:, 0, :], Pr2[:, :, 8], -1.0, None, ALU.add, None, ['Pr2'], FK)
                cp('dve', fw[:, 1, :], Pi2[:, :, 8], ['Pi2'], FK)
                tt('dve', fw[:, 2, :], fw[:, 0, :], lr, ALU.mult, FK + ['lre2'], FK)
                tt('dve', fw[:, 3, :], fw[:, 1, :], li, ALU.mult, FK + ['lre2'], FK)
                tt('dve', fw[:, 4, :], fw[:, 2, :], fw[:, 3, :], ALU.add, FK, FK)
                tt('dve', fw[:, 7, :], fw[:, 1, :], lr, ALU.mult, FK, FK)
                tt('dve', fw[:, 8, :], fw[:, 0, :], li, ALU.mult, FK, FK)
                tt('dve', fw[:, 5, :], fw[:, 7, :], fw[:, 8, :], ALU.subtract, FK, FK)
                tt('dve', fw[:, 2, :], lr, lr, ALU.mult, FK, FK)
                tt('dve', fw[:, 3, :], li, li, ALU.mult, FK, FK)
                tt('dve', fw[:, 6, :], fw[:, 2, :], fw[:, 3, :], ALU.add, FK, FK)
                recip(fw[:, 6, :], fw[:, 6, :], FK, FK)
                tt('dve', fw[:, 4, :], fw[:, 4, :], fw[:, 6, :], ALU.mult, FK, FK)
                tt('dve', fw[:, 5, :], fw[:, 5, :], fw[:, 6, :], ALU.mult, FK, FK)
                Frb = fw[:, 4, :].unsqueeze(2).to_broadcast([128, 64, 16])
                Fib = fw[:, 5, :].unsqueeze(2).to_broadcast([128, 64, 16])
                fl = lambda t: t[:].rearrange("p g k -> p (g k)")
                tt('dve', bt1[:], Bw[:], Fib, ALU.mult, ['Bw'] + FK, ['bt1'])
                tt('dve', bt2[:], Bs[:], Frb, ALU.mult, ['Bs'] + FK, ['bt2'])
                stt(fl(Bbs), fl(bt1), sgc[:, 0:1], fl(bt2), ALU.mult, ALU.add, ['bt1', 'bt2', 'sgc'], ['Bbs'])
                tt('dve', bt1[:], Bs[:], Fib, ALU.mult, ['Bs', 'Bbs'] + FK, ['bt1'])
                tt('dve', bt2[:], Bw[:], Frb, ALU.mult, ['Bw', 'Bbs'] + FK, ['bt2'])
                stt(fl(Bbw), fl(bt1), sgc[:, 1:2], fl(bt2), ALU.mult, ALU.add, ['bt1', 'bt2', 'sgc'], ['Bbw'])
                for (src, dst, kname) in [(Cl, Cs, 'Cs'), (Clw, Cw, 'Cw')]:
                    for g4 in range(2):
                        pt, pk = nps()
                        for i in range(4):
                            gb = g4 * 4 + i
                            tr(pt[:, i * 128:(i + 1) * 128], src[:, gb].rearrange("p c q -> p (c q)"), ['Cl', 'Clw'], [pk], inc=(i == 3))
                        cp(evac_eng(), dst[:, g4 * 32:(g4 + 1) * 32, :].rearrange("p g j -> p (g j)"), pt[:, :], [pk], [kname])
                cp('dve', sde[:], sdt[:].unsqueeze(2).to_broadcast([64, 16, 8]), ['sdt'], ['sde'])
                pt, pk = nps()
                tr(pt[:, 0:64], sde[:].rearrange("g k s -> g (k s)"), ['sde'], [pk])
                cp('dve', Dcol[:], pt[:, 0:64], [pk], ['Dcol'])
                cp('dve', LL[:, 0, :], Pr2[:, :, 32], ['Pr2'], ['LL'])
                tsc('dve', LL[:, 1, :], Pi2[:, :, 32], sgc[:, 0:1], None, ALU.mult, None, ['Pi2', 'sgc'], ['LL'])
                cp('dve', LL[:, 2, :], Pr2[:, :, 33], ['Pr2'], ['LL'])
                tsc('dve', LL[:, 3, :], Pi2[:, :, 33], sgc[:, 0:1], None, ALU.mult, None, ['Pi2', 'sgc'], ['LL'])
                P.emit()
            with ExitStack() as ph:
                xa = [sb_t(ph, "xa%d" % i, [128, 8, 16, 8]) for i in range(2)]
                xb_ = [sb_t(ph, "xbb%d" % i, [128, 8, 16, 8]) for i in range(2)]
                X1 = sb_t(ph, "X1", [128, 8, 128]); BL = sb_t(ph, "BL", [128, 8, 128]); CL = sb_t(ph, "CL", [128, 8, 128])
                ytmp = [sb_t(ph, "ytmp%d" % i, [128, 128]) for i in range(2)]
                f4 = lambda t: t[:].rearrange("p a b c -> p (a b c)")
                for gb in range(8):
                    gs = slice(gb * 8, (gb + 1) * 8)
                    def bexp(T, e0):
                        return T[:, gs, e0:e0 + 8].unsqueeze(2).to_broadcast([128, 8, 16, 8])
                    def vexp(V):
                        return V[:, gs, :].unsqueeze(3).to_broadcast([128, 8, 16, 8])
                    b = gb % 2
                    tt('dve', xa[b][:], vexp(Bbs), bexp(Pr2, 0), ALU.mult, ['Bbs', 'Pr2'], [('xa', b)])
                    tt('pool', xb_[b][:], vexp(Bbw), bexp(Pi2, 0), ALU.mult, ['Bbw', 'Pi2'], [('xb_', b)])
                    stt(X1[:].rearrange("p g m -> p (g m)"), f4(xb_[b]), sgc[:, 0:1], f4(xa[b]), ALU.mult, ALU.add, [('xa', b), ('xb_', b), 'sgc'], ['X1'])
                    tt('dve', xa[b][:], vexp(Cs), bexp(Pr2, 8), ALU.mult, ['Cs', 'Pr2'], [('xa', b)])
                    tt('pool', xb_[b][:], vexp(Cw), bexp(Pi2, 8), ALU.mult, ['Cw', 'Pi2'], [('xb_', b)])
                    stt(A_ys[:, gs, :].rearrange("p g m -> p (g m)"), f4(xa[b]), sgc[:, 1:2], f4(xb_[b]), ALU.mult, ALU.subtract,
                        [('xa', b), ('xb_', b), 'sgc'], ['A_ys'])
                    tt('dve', xa[b][:], vexp(Bbs), bexp(Pr2, 16), ALU.mult, ['Bbs', 'Pr2'], [('xa', b)])
                    tt('pool', xb_[b][:], vexp(Bbw), bexp(Pi2, 16), ALU.mult, ['Bbw', 'Pi2'], [('xb_', b)])
                    stt(BL[:].rearrange("p g m -> p (g m)"), f4(xa[b]), sgc[:, 1:2], f4(xb_[b]), ALU.mult, ALU.subtract,
                        [('xa', b), ('xb_', b), 'sgc'], ['BL'])
                    tt('dve', xa[b][:], vexp(Cs), bexp(Pr2, 24), ALU.mult, ['Cs', 'Pr2'], [('xa', b)])
                    tt('pool', xb_[b][:], vexp(Cw), bexp(Pi2, 24), ALU.mult, ['Cw', 'Pi2'], [('xb_', b)])
                    stt(CL[:].rearrange("p g m -> p (g m)"), f4(xb_[b]), sgc[:, 0:1], f4(xa[b]), ALU.mult, ALU.add,
                        [('xa', b), ('xb_', b), 'sgc'], ['CL'])
                    for g4 in range(2):
                        pt, pk = nps()
                        for i in range(4):
                            gl = g4 * 4 + i
                            tr(pt[:, i * 128:(i + 1) * 128], X1[:, gl, :], ['X1'], [pk], inc=(i == 3))
                        cp(evac_eng(), A_su[:, gb * 8 + g4 * 4:gb * 8 + g4 * 4 + 4, :].rearrange("p g m -> p (g m)"), pt[:, :], [pk], ['A_su'])
                        pt, pk = nps()
                        for i in range(4):
                            gl = g4 * 4 + i
                            mm(pt[:, i * 128:(i + 1) * 128], BL[:, gl, :], CL[:, gl, :], True, True, ['BL', 'CL'], [pk], inc=(i == 3))
                        for i in range(4):
                            g = gb * 8 + g4 * 4 + i
                            yb = i % 2
                            tt('dve', ytmp[yb][:], pt[:, i * 128:(i + 1) * 128], cm[:], ALU.mult, [pk, 'cm'], [('ytmp', yb)])
                            stt(A_yu[:, g, :], ident[:], Dcol[:, g:g + 1], ytmp[yb][:], ALU.mult, ALU.add, [('ytmp', yb), 'Dcol', 'ident'], ['A_yu'])
                P.emit(final=(STOP == 4))
            s5y.close()
            if STOP == 4:
                return nc

            with ExitStack() as ph:
                U = sb_t(ph, "U", [128, 32, NCH], BF16)
                H12 = sb_t(ph, "H12", [128, 2, 64, NCH], BF16)
                S1bf = sb_t(ph, "S1bf", [128, 64, NCH + 2], BF16)
                Y = sb_t(ph, "Y", [128, 8, NCH], BF16)
                Z = [sb_t(ph, "Z%d" % i, [128, 3, 64]) for i in range(2)]
                T1 = sb_t(ph, "T1", [128, 2, 64]); T2 = sb_t(ph, "T2", [128, 2, 64])
                LN = sb_t(ph, "LN", [128, 2, 64])
                ssin = sb_t(ph, "ssin", [64, 16, 2, 64])
                Zs = sb_t(ph, "Zs", [128, 2, 32, 16]); Ts1 = sb_t(ph, "Ts1", [128, 2, 32, 16]); Ts2 = sb_t(ph, "Ts2", [128, 2, 32, 16])
                fin = sb_t(ph, "fin", [128, 64, 16])
                fo = ssin[:].rearrange("g s c p -> g s (c p)")
                SF = sb_t(ph, "SF", [128, 64]); sfo = sb_t(ph, "sfo", [64, 128])
                for c_ in range(2):
                    dma('sp', ssin[:, :, c_, :], s5in[c_].rearrange("s g p -> g s p"), [], ['ssin'], 'ssin')
                ssw8 = sb_t(ph, "ssw8", [64, 8, 2, 64])
                L1a = LL[:, 0, :]; L2a = LL[:, 1, :]
                cp('dve', LN[:, 0, :], L2a, ['LL'], ['LN'])
                tsc('dve', LN[:, 1, :], L2a, -1.0, None, ALU.mult, None, ['LL'], ['LN'])
                L1b = L1a.unsqueeze(1).to_broadcast([128, 2, 64])
                zc = [0]

                def rec_step(n, col):
                    cur, nxt = Z[zc[0] % 2], Z[(zc[0] + 1) % 2]
                    pc_, pn_ = zc[0] % 2, (zc[0] + 1) % 2
                    zc[0] += 1
                    H = [slice(0, 32), slice(32, 64)]
                    for h in range(2):
                        tt('dve', T1[:, :, H[h]], cur[:, 0:2, H[h]], L1a[:, H[h]].unsqueeze(1).to_broadcast([128, 2, 32]), ALU.mult, [('Z', pc_, h), 'LL'], [('T1', h)])
                    for h in range(2):
                        tt('dve', T2[:, :, H[h]], cur[:, 1:3, H[h]], LN[:, :, H[h]], ALU.mult, [('Z', pc_, h), 'LN'], [('T2', h)])
                    for h in range(2):
                        tt('dve', T1[:, :, H[h]], T1[:, :, H[h]], T2[:, :, H[h]], ALU.add, [('T1', h), ('T2', h)], [('T1', h)])
                    for h in range(2):
                        tt('dve', nxt[:, 0:2, H[h]], T1[:, :, H[h]], H12[:, :, H[h], n], ALU.add, [('T1', h), 'H12'], [('Z', pn_, h)])
                    for h in range(2):
                        tt('dve', nxt[:, 2, H[h]], T1[:, 0, H[h]], H12[:, 0, H[h], n], ALU.add, [('T1', h), 'H12'], [('Z', pn_, h)])
                    if col is not None:
                        cp('act', S1bf[:, :, col], nxt[:, 0, :], [('Z', pn_, 0), ('Z', pn_, 1)], ['S1bf'])

                def load_u(src, g0, nch):
                    for q4 in range(4):
                        dma('sp', U[:, q4 * 8:(q4 + 1) * 8, 0:nch], src[g0 + q4 * 8:g0 + (q4 + 1) * 8].rearrange("g k s n -> (k s) g n"), ['dU', 'dU2'], ['U'], 'U')

                def stage_a(g0, nch):
                    for g3 in range(0, 32, 3):
                        ng = min(3, 32 - g3)
                        pa_, pak_ = nps(); pb_, pbk_ = nps()
                        for i in range(ng):
                            gl = g3 + i; g = g0 + gl
                            cs = slice(i * nch, (i + 1) * nch)
                            mm(pa_[:, cs], A_su[:, g, :], U[:, gl, 0:nch], True, True, ['A_su', 'U'], [pak_], inc=(i == ng - 1))
                            mm(pb_[0:64, cs], A_su[:, g, 64:128], U[:, gl, 0:nch], True, True, ['A_su', 'U'], [pbk_], inc=False)
                            mm(pb_[64:128, cs], A_su[:, g, 0:64], U[:, gl, 0:nch], True, True, ['A_su', 'U'], [pbk_], inc=(i == ng - 1))
                        gsl = slice(g0 + g3, g0 + g3 + ng)
                        cp('dve', H12[:, 0, gsl, 0:nch], pa_[:, 0:ng * nch].rearrange("p (g n) -> p g n", n=nch), [pak_], ['H12'])
                        cp('dve', H12[:, 1, gsl, 0:nch], pb_[:, 0:ng * nch].rearrange("p (g n) -> p g n", n=nch), [pbk_], ['H12'])

                SK = os.environ.get('S5SKIP', '')
                for hf2 in (range(2) if 'p' not in SK else []):
                    load_u(dU2, hf2 * 32, NCP)
                    stage_a(hf2 * 32, NCP)
                memset('dve', Z[0][:], 0.0, [('Z', 0, 0), ('Z', 0, 1)])
                for n in (range(NCP) if 'p' not in SK else []):
                    rec_step(n, None)
                for hf2 in range(2):
                    load_u(dU, hf2 * 32, NCH)
                    stage_a(hf2 * 32, NCH)
                zk = lambda: [('Z', zc[0] % 2, 0), ('Z', zc[0] % 2, 1)]
                cp('act', S1bf[:, :, 0], Z[zc[0] % 2][:, 0, :], zk(), ['S1bf'])
                for n in (range(16) if 'r' not in SK else []):
                    rec_step(n, n + 1 if n < 15 else None)
                zcur = Z[zc[0] % 2]
                tsc('dve', zcur[:].rearrange("p a g -> p (a g)"), zcur[:].rearrange("p a g -> p (a g)"), maskt[:, 0:1], None, ALU.mult, None, zk() + ['mask'], zk())
                cp('act', S1bf[:, :, 16], zcur[:, 0, :], zk(), ['S1bf'])
                for n in (range(16, 144) if 'r' not in SK else []):
                    rec_step(n, n + 1 if n < 143 else None)
                cp('act', SF[:], Z[zc[0] % 2][:, 0, :], zk(), ['SF'])
                for hf2 in (range(2) if 's' not in SK else []):
                    g0 = hf2 * 32
                    L1h = LL[:, 0, g0:g0 + 32]
                    pt, pk = nps()
                    for sq in range(16):
                        tr(pt[:, sq * 32:(sq + 1) * 32], ssin[g0:g0 + 32, sq].rearrange("g c p -> g (c p)"), ['ssin'], [pk], inc=(sq == 15))
                    cp('dve', Zs[:, 0].rearrange("p g s -> p s g"), pt[:, :].rearrange("p (s g) -> p s g", g=32), [pk], ['Zs'])
                    cp('dve', S1bf[:, g0:g0 + 32, 144:160].rearrange("p g s -> p s g"), pt[:, :].rearrange("p (s g) -> p s g", g=32), [pk], ['S1bf'])
                    pt, pk = nps()
                    for s8 in range(2):
                        for c_ in range(2):
                            dma('sp', ssw8[:, :, 1 - c_, :], s5in[c_, s8 * 8:(s8 + 1) * 8].rearrange("s g p -> g s p"), [], ['ssw8'], 'ssw8')
                        for q8 in range(8):
                            sq = s8 * 8 + q8
                            tr(pt[:, sq * 32:(sq + 1) * 32], ssw8[g0:g0 + 32, q8].rearrange("g c p -> g (c p)"), ['ssw8'], [pk], inc=(q8 == 7))
                    cp('dve', Zs[:, 1].rearrange("p g s -> p s g"), pt[:, :].rearrange("p (s g) -> p s g", g=32), [pk], ['Zs'])
                    L1s = L1h.unsqueeze(1).unsqueeze(3).to_broadcast([128, 2, 32, 16])
                    tt('dve', Ts1[:], Zs[:, 0:2], L1s, ALU.mult, ['Zs', 'LL'], ['Ts1'])
                    tt('dve', Ts2[:, 0], Zs[:, 1], LN[:, 0, g0:g0 + 32].unsqueeze(2).to_broadcast([128, 32, 16]), ALU.mult, ['Zs', 'LN'], ['Ts2'])
                    tt('dve', Ts2[:, 1], Zs[:, 0], LN[:, 1, g0:g0 + 32].unsqueeze(2).to_broadcast([128, 32, 16]), ALU.mult, ['Zs', 'LN'], ['Ts2'])
                    tt('dve', Ts1[:], Ts1[:], Ts2[:], ALU.add, ['Ts1', 'Ts2'], ['Ts1'])
                    tt('dve', Ts1[:], Ts1[:], H12[:, :, g0:g0 + 32, 144:160], ALU.add, ['Ts1', 'H12'], ['Ts1'])
                    M1s = LL[:, 2, g0:g0 + 32].unsqueeze(2).to_broadcast([128, 32, 16])
                    M2s = LL[:, 3, g0:g0 + 32].unsqueeze(2).to_broadcast([128, 32, 16])
                    finh = fin[:, g0:g0 + 32, :]
                    tt('dve', finh, Ts1[:, 0], M1s, ALU.mult, ['Ts1', 'LL'], ['fin'])
                    tt('dve', Ts2[:, 0], Ts1[:, 1], M2s, ALU.mult, ['Ts1', 'LL'], ['Ts2'])
                    tt('dve', finh, finh, Ts2[:, 0], ALU.add, ['fin', 'Ts2'], ['fin'])
                for hf2 in (range(2) if 'c' not in SK else []):
                    g0 = hf2 * 32
                    if hf2 == 0:
                        load_u(dU, 0, NCH)
                    for gq in range(4):
                        for g3 in range(0, 8, 3):
                            ng = min(3, 8 - g3)
                            py, pyk = nps()
                            for i in range(ng):
                                gl = gq * 8 + g3 + i; g = g0 + gl
                                cs = slice(i * NCH, (i + 1) * NCH)
                                mm(py[:, cs], A_ys[:, g, :], S1bf[:, g, 0:NCH], True, False, ['A_ys', 'S1bf'], [pyk], inc=False)
                                mm(py[:, cs], A_yu[:, g, :], U[:, gl, :], False, True, ['A_yu', 'U'], [pyk], inc=(i == ng - 1))
                            cp('dve', Y[:, g3:g3 + ng, :], py[:, 0:ng * NCH].rearrange("p (g n) -> p g n", n=NCH), [pyk], ['Y'])
                        gb = g0 + gq * 8
                        dma('sp', dY[gb:gb + 8].rearrange("g j t n -> (j t) g n"), Y[:], ['Y'], ['dY'], 'Y')
                    if hf2 == 0:
                        load_u(dU, 32, NCH)
                for s4 in range(4):
                    pt, pk = nps()
                    for i in range(4):
                        sq = s4 * 4 + i
                        tr(pt[0:64, i * 128:(i + 1) * 128], fin[:, :, sq], ['fin'], [pk], inc=(i == 3))
                    cp(evac_eng(), fo[:, s4 * 4:s4 * 4 + 4, :].rearrange("g s m -> g (s m)"), pt[0:64, :], [pk], ['fo', 'ssin'])
                dma('sp', s5s.rearrange("s g m -> g s m"), fo, ['fo'], [], 'fo')
                pt, pk = nps()
                tr(pt[0:64, 0:128], SF[:], ['SF'], [pk])
                cp('dve', sfo[:], pt[0:64, 0:128], [pk], ['sfo'])
                dma('sp', s5p, sfo[:], ['sfo'], [], 'sfo')
                P.emit(final=(STOP == 5))
            if STOP == 5:
                return nc

        with ExitStack() as ph:
            yTp = sb_t(ph, "yTp", [128, 8, 8, NCH], BF16)
            yg = sb_t(ph, "yg", [128, 8, TT], BF16)
            sgm = [sb_t(ph, "sgm%d" % i, [128, 512], BF16) for i in range(2)]
            alloc_ring(ph)
            for ch in range(8):
                dma('sp', yTp[:, ch], dY[ch * 8:(ch + 1) * 8].rearrange("g j t n -> (g j) t n"), ['dY'], [('yTp', ch)], 'yTp')
            P.settle([('yTp', ch) for ch in range(8)], 'yTp')
            for ch in range(8):
                act(yg[:, ch, 0:TM].rearrange("p (n s) -> p n s", s=8), yTp[:, ch, :, 16:144].rearrange("p s n -> p n s"), AF.Gelu_apprx_tanh,
                    [('yTp', ch)], ['yg'])
                act(yg[:, ch, TM:TF].rearrange("p (j i) -> p j i", i=4), yTp[:, ch, 0:4, 144:160].rearrange("p i j -> p j i"), AF.Gelu_apprx_tanh,
                    [('yTp', ch)], ['yg'])
                act(yg[:, ch, TF:TT].rearrange("p (n s) -> p n s", s=8), yTp[:, ch, :, 0:16].rearrange("p s n -> p n s"), AF.Gelu_apprx_tanh,
                    [('yTp', ch)], ['yg'])
            if os.environ.get('DBGSEL') == 'yTp':
                dma('pool', dbg.rearrange("p c t -> p (c t)")[:, 0:8 * 8 * NCH], yTp[:].rearrange("p a b c -> p (a b c)"), [('yTp', ch) for ch in range(8)], [], 'dbg')
                P.emit(final=True)
                return nc
            if os.environ.get('DBGSEL') == 'yg':
                cp('dve', actT[:, 8:16, :], yg[:], ['yg'], [('actT', ti) for ti in range(NTILE)])
            for pi in (range(4) if os.environ.get('DBGSEL') != 'yg' else []):
                wt, wk = nwr()
                dma('pool', wt[:, 0:8, :], w_glu[:, pi * 256:(pi + 1) * 256].rearrange("(c p) n -> p c n", p=128), [], [wk], 'wr%d' % wk[1])
                for half in range(2):
                    ch = pi * 2 + half
                    for gi, (c0, gn) in enumerate(CGRP):
                        pt, pk = nps()
                        for kc in range(8):
                            mm(pt[:, 0:gn], wt[:, kc, half * 128:(half + 1) * 128], yg[:, kc, c0:c0 + gn], kc == 0, kc == 7, [wk, 'yg'], [pk], inc=(kc == 7))
                        sb_i = (ch * 3 + gi) % 2
                        act(sgm[sb_i][:, 0:gn], pt[:, 0:gn], AF.Sigmoid, [pk, 'bgT'], [('sgm', sb_i)], bias=bgT[:, ch:ch + 1])
                        tt('dve', actT[:, 8 + ch, c0:c0 + gn], sgm[sb_i][:, 0:gn], yg[:, ch, c0:c0 + gn], ALU.mult, [('sgm', sb_i), 'yg'],
                           [('actT', ti) for ti in range(NTILE)])
            if STOP == 6:
                dbg_dump()
            P.emit(final=(STOP == 6))
        if STOP == 6:
            return nc

        xmid = sb_t(es, "xmid", [128, NTO, D])
        px = ExitStack()
        px.__enter__()
        xpf = sb_t(px, "xpf", [128, D])
        xrow = lambda ti, nt: (xmid[0:nt, ti, :] if ti < NTO else xpf[0:nt, :])

        def gated_add(ph, name):
            tmp = [sb_t(ph, "%s_t%d" % (name, i), [128, 256]) for i in range(4)]
            cnt = [0]

            def f(pt_ap, pk, ti, nt, col0, gate_ap):
                b = cnt[0] % 4
                cnt[0] += 1
                tt('dve', tmp[b][0:nt, :], pt_ap, gate_ap, ALU.mult, [pk, 'gate'], [(name, b)])
                xs_ = xrow(ti, nt)[:, col0:col0 + 256]
                tt('pool', xs_, xs_, tmp[b][0:nt, :], ALU.add, [(name, b), ('xm', ti)], [('xm', ti)])
            return f

        def load_sample_gate(Gs, col0):
            for j in range(16):
                dma('sp', Gs[4 * j:4 * j + 4, :], modd[1 + j:2 + j, col0:col0 + D].partition_broadcast(4), ['modd'], ['gate'], 'gates')

        with ExitStack() as ph:
            alloc_ring(ph)
            G1 = sb_t(ph, "G1", [128, D]); G1s = sb_t(ph, "G1s", [64, D])
            gadd = gated_add(ph, 'wo')
            dma('sp', G1[:], modd[0:1, 2 * D:3 * D].partition_broadcast(128), ['modd'], ['gate'], 'gatep')
            load_sample_gate(G1s, 2 * D)
            for ti, (t0, nt) in enumerate(TILES):
                src = xm[t0:t0 + nt, :] if ti < 8 else (xs if ti == 8 else xpre[896:1024, :])
                dma('sp', xrow(ti, nt), src, [], [('xm', ti)], 'xm%d' % (ti % 4))
            wo_q = {}

            def wo_get(i):
                while len(wo_q) < min(8, i + 4):
                    j = len(wo_q)
                    wt_, wk_ = nwr()
                    dma('pool', wt_[:], w_out[:, j * 256:(j + 1) * 256].rearrange("(c p) n -> p c n", p=128), [], [wk_], 'wr%d' % wk_[1])
                    wo_q[j] = (wt_, wk_)
                return wo_q[i]
            for pi in range(8):
                wt, wk = wo_get(pi)
                for ti, (t0, nt) in enumerate(TILES):
                    pt, pk = nps()
                    o0 = 0
                    for kc in range(KC):
                        mm(pt[0:nt, o0:o0 + 256], actT[:, kc, t0:t0 + nt], wt[:, kc, :], kc == 0, kc == KC - 1, [wk, ('actT', ti)], [pk], inc=(kc == KC - 1))
                    G = G1[0:nt, pi * 256:(pi + 1) * 256] if ti != 8 else G1s[0:nt, pi * 256:(pi + 1) * 256]
                    gadd(pt[0:nt, o0:o0 + 256], pk, ti, nt, pi * 256, G)
            P.emit(final=(STOP == 7))
        if STOP == 7:
            return nc

        def halo_cols():
            tsc('dve', actT[:, :, TF:TF + 2], actT[:, :, TT - 2:TT], maskt[:, 0:1], None, ALU.mult, None, [('actT', 9), 'mask'], [('actT', 9)])
        norm_phase(1, lambda ph, ti, t0, nt: (xrow(ti, nt), ('xm', ti)), 8, post=halo_cols)
        px.close()
        if STOP == 8:
            return nc

        with ExitStack() as ph:
            alloc_ring(ph, 4, 128)
            G2 = sb_t(ph, "G2", [128, D]); G2s = sb_t(ph, "G2s", [64, D])
            gadd = gated_add(ph, 'wd')
            hid = sb_t(ph, "hid", [128, 8, TF], BF16)
            stgm = [sb_t(ph, "stgm%d" % i, [128, 2 + TM]) for i in range(2)]
            stgs = [sb_t(ph, "stgs%d" % i, [128, 16, 6]) for i in range(2)]
            acc = [sb_t(ph, "acc%d" % i, [128, TF]) for i in range(3)]
            cglob = [0]
            hal = sb_t(ph, "hal", [128, 2, 8, 32])
            scv = sb_t(ph, "scv", [64, 1024])
            UPS = sb_t(ph, "UPS", [128, 2, 8, 64])
            UPL = sb_t(ph, "UPL", [128, 2, 88]); uplT = sb_t(ph, "uplT", [128, 2, 128])
            dma('sp', G2[:], modd[0:1, 5 * D:6 * D].partition_broadcast(128), ['modd'], ['gate'], 'gatep')
            load_sample_gate(G2s, 5 * D)
            FG = [(2, 1024, 66), (0, 0, 512), (1, 512, 512)]
            wlist = []
            c0_ = 0
            for nk_ in [8, 8, 8, 8, 8, 4]:
                for cl_ in range(nk_):
                    for part_ in range(2):
                        wlist.append(('up', part_ * DFF + (c0_ + cl_) * 128, 0, 0))
                for pi_ in range(8):
                    wlist.append(('down', c0_, nk_, pi_))
                c0_ += nk_
            wst = {'n': 0, 'got': {}}

            def wget(i):
                while wst['n'] < min(len(wlist), i + 3):
                    j = wst['n']
                    kind_, a_, b_, c_ = wlist[j]
                    wt_, wk_ = nwr()
                    if kind_ == 'up':
                        dma('pool', wt_[:], w_up[:, a_:a_ + 128].rearrange("(c p) n -> p c n", p=128), [], [wk_], 'wr%d' % wk_[1])
                    else:
                        wd_ = wt_[:].rearrange("p c n -> p (c n)").rearrange("p (c n) -> p c n", n=256)
                        dma('pool', wd_[:, 0:b_, :], w_down[a_ * 128:(a_ + b_) * 128, c_ * 256:(c_ + 1) * 256].rearrange("(c p) n -> p c n", p=128),
                            [], [wk_], 'wr%d' % wk_[1])
                    wst['got'][j] = (wt_, wk_)
                    wst['n'] += 1
                return wst['got'].pop(i)
            wi = [0]
            c0 = 0
            for nk in [8, 8, 8, 8, 8, 4]:
                for part in range(2):
                    dma('sp', scv[0:32, 0:nk * 128], sconv[:, part * DFF + c0 * 128:part * DFF + (c0 + nk) * 128], [], ['scv'], 'scv')
                    pt, pk = nps()
                    for cl in range(nk):
                        tr(pt[:, cl * 32:(cl + 1) * 32], scv[0:32, cl * 128:(cl + 1) * 128], ['scv'], [pk], inc=(cl == nk - 1))
                    cp('dve', hal[:, part, 0:nk, :].rearrange("p c r -> p (c r)"), pt[:, 0:nk * 32], [pk], ['hal'])
                for cl in range(nk):
                    wts = [wget(wi[0]), wget(wi[0] + 1)]
                    wi[0] += 2
                    if True:
                        c = c0 + cl
                        par = cglob[0] % 2
                        cglob[0] += 1
                        for part in range(2):
                            wt, wk = wts[part]
                            cidx = c + 44 * part
                            ai = par if part == 0 else 2
                            sm, ss, ac = stgm[part], stgs[part], acc[ai]
                            AK = ('acc', ai)
                            for (gi, g0, gn) in FG:
                                pt, pk = nps()
                                for kc in range(KC):
                                    mm(pt[:, 0:gn], wt[:, kc, :], actT[:, kc, g0:g0 + gn], kc == 0, kc == KC - 1,
                                       [wk] + [('actT', ti) for ti in range(NTILE)], [pk], inc=(kc == KC - 1))
                                if gi < 2:
                                    cp('act', sm[:, 2 + g0:2 + g0 + gn], pt[:, 0:gn], [pk], [('stgm', part)])
                                else:
                                    cp('act', ss[:, :, 2:6], pt[:, 0:64].rearrange("p (j i) -> p j i", i=4), [pk], [('stgs', part)])
                                    cp('act', sm[:, 0:2], pt[:, 64:66], [pk], [('stgm', part)])
                            cp('pool', ss[:, :, 0:2], hal[:, part, cl, :].rearrange("p (j i) -> p j i", i=2), ['hal'], [('stgs', part)])
                            w0 = cwT[:, cidx, 0:1]; w1 = cwT[:, cidx, 1:2]; w2 = cwT[:, cidx, 2:3]; bb = cwT[:, cidx, 3:4]
                            acs = ac[:, TM:TF].rearrange("p (j i) -> p j i", i=4)
                            act(ac[:, 0:TM], sm[:, 2:2 + TM], AF.Identity, [('stgm', part), 'cwT'], [AK], bias=bb, scale=w2)
                            act(acs, ss[:, :, 2:6], AF.Identity, [('stgs', part), 'cwT'], [AK], bias=bb, scale=w2)
                            stt(ac[:, 0:TM], sm[:, 1:1 + TM], w1, ac[:, 0:TM], ALU.mult, ALU.add, [('stgm', part), 'cwT', AK], [AK])
                            stt(ac[:, 0:TM], sm[:, 0:TM], w0, ac[:, 0:TM], ALU.mult, ALU.add, [('stgm', part), 'cwT', AK], [AK])
                            stt(acs, ss[:, :, 1:5], w1, acs, ALU.mult, ALU.add, [('stgs', part), 'cwT', AK], [AK])
                            stt(acs, ss[:, :, 0:4], w0, acs, ALU.mult, ALU.add, [('stgs', part), 'cwT', AK], [AK])
                            cp('pool', UPL[:, :, cidx], sm[:, TM:TM + 2], [('stgm', part)], ['UPL'])
                            cp('pool', UPS[:, part, cl, :].rearrange("p (j i) -> p j i", i=4), ss[:, :, 2:6], [('stgs', part)], ['UPS'])
                            if part == 0:
                                act(ac[:, :], ac[:, :], AF.Gelu_apprx_tanh, [AK], [AK])
                        tt('dve', hid[:, cl, :], acc[par][:, :], acc[2][:, :], ALU.mult, [('acc', par), ('acc', 2)], ['hid'])
                for part in range(2):
                    for c4 in range(0, nk, 4):
                        pt, pk = nps()
                        for i in range(4):
                            tr(pt[0:64, i * 128:(i + 1) * 128], UPS[:, part, c4 + i, :], ['UPS'], [pk], inc=(i == 3))
                        cp('dve', scv[:, c4 * 128:(c4 + 4) * 128], pt[0:64, :], [pk], ['scv'])
                    dma('sp', convs[:, part * DFF + c0 * 128:part * DFF + (c0 + nk) * 128], scv[:, 0:nk * 128], ['scv'], [], 'scv')
                for pi in range(8):
                    wt, wk = wget(wi[0])
                    wi[0] += 1
                    wd = wt[:].rearrange("p c n -> p (c n)").rearrange("p (c n) -> p c n", n=256)
                    for ti, (t0, nt) in enumerate(TILES[0:NTO]):
                        pt, pk = nps()
                        o0 = 0
                        for cl in range(nk):
                            mm(pt[0:nt, o0:o0 + 256], hid[:, cl, t0:t0 + nt], wd[:, cl, :], cl == 0, cl == nk - 1, [wk, 'hid'], [pk], inc=(cl == nk - 1))
                        G = G2[0:nt, pi * 256:(pi + 1) * 256] if ti < 8 else G2s[0:nt, pi * 256:(pi + 1) * 256]
                        gadd(pt[0:nt, o0:o0 + 256], pk, ti, nt, pi * 256, G)
                c0 += nk
            pt, pk = nps()
            UPLf = UPL[:].rearrange("p i c -> p (i c)")
            tr(pt[:, 0:128], UPLf[:, 0:128], ['UPL'], [pk], inc=False)
            tr(pt[0:48, 128:256], UPLf[:, 128:176], ['UPL'], [pk])
            cp('dve', uplT[:, 0, :], pt[:, 0:128], [pk], ['uplT'])
            cp('dve', uplT[0:48, 1, :], pt[0:48, 128:256], [pk], ['uplT'])
            cpv = convp.rearrange("i (c p) -> (i c) p", p=128)
            dma('sp', cpv[0:128, :], uplT[:, 0, :], ['uplT'], [], 'convp')
            dma('sp', cpv[128:176, :], uplT[0:48, 1, :], ['uplT'], [], 'convp')
            P.emit(final=(STOP == 9))
        if STOP == 9:
            return nc

        with ExitStack() as ph:
            fnb = sb_t(ph, "fnb", [128, D])
            yo = [sb_t(ph, "yo%d" % i, [128, D]) for i in range(2)]
            st = [sb_t(ph, "fst%d" % i, [128, 4]) for i in range(2)]
            dma('sp', fnb[:], nrm[2:3, :].partition_broadcast(128), [], ['fnb'], 'fnb')
            for ti, (t0, nt) in enumerate(TILES[0:NTO]):
                b = ti % 2
                xt = xmid[0:nt, ti, :]
                act(yo[b][0:nt, :], xt, AF.Square, [('xm', ti)], [('yo', b), ('fst', b)], accum=st[b][0:nt, 0:1])
                rstd_chain(st[b][0:nt, 0:1], st[b][0:nt, 1:2], st[b][0:nt, 2:3], nt, 1.0 / D, ('fst', b))
                stt(yo[b][0:nt, :], xt, st[b][0:nt, 2:3], fnb[0:nt, :], ALU.mult, ALU.mult, [('xm', ti), ('fst', b), 'fnb'], [('yo', b)])
                dst = yp[t0:t0 + nt, :] if ti < 8 else ys
                dma('sp', dst, yo[b][0:nt, :], [('yo', b)], [], 'yo%d' % b)
            P.emit(final=True)
    return nc


_CACHE = {}


def _consts():
    c = {}
    c["c_ident"] = np.eye(128, dtype=np.float32)
    gmk = np.zeros((128, 6, 128), np.float32)
    s = np.arange(128)[:, None]; t = np.arange(128)[None, :]
    gmk[:, 0, :] = np.where(s <= t, -1.0 / 16, 0.0)
    gmk[:, 1, :] = np.where(s > t, -1.0 / 16, 0.0)
    gmk[:, 2, :] = np.where(s <= t, 1.0, 0.0)
    same = (s // 4) == (t // 4)
    gmk[:, 3, :] = np.where(same & (s <= t), -1.0 / 16, 0.0)
    gmk[:, 4, :] = np.where(same & (s > t), -1.0 / 16, 0.0)
    gmk[:, 5, :] = np.where(same & (s <= t), 1.0, 0.0)
    c["c_gm"] = gmk
    sq = np.zeros((64, 2, 16), np.float32)
    for tkn in range(64):
        sq[tkn, 0, tkn // 4] = 1.0
        sq[tkn, 1, tkn // 4] = -1.0 / 16
    c["c_seq"] = sq
    bmk = np.zeros((128, 16, 64), np.float32)
    for j in range(16):
        bmk[:, j, 4 * j:4 * j + 4] = 1.0
    c["c_bm"] = bmk
    ev = np.array(list(range(7, -1, -1)) + list(range(1, 9)) + list(range(0, -8, -1)) + list(range(0, 8)) + [8, -4], np.float32)
    c["c_ev"] = np.tile(ev[None, :], (128, 1)).astype(np.float32)
    sg = np.zeros((128, 2), np.float32)
    sg[:64, 0] = -1; sg[64:, 0] = 1; sg[:64, 1] = 1; sg[64:, 1] = -1
    c["c_sg"] = sg
    ks = np.arange(128)
    c["c_cm"] = ((ks[None, :] % 8) >= (ks[:, None] % 8)).astype(np.float32)
    return c


def kernel(_cores=None, _debug=False, **inp):
    f = lambda a: np.ascontiguousarray(np.asarray(a, dtype=np.float32))
    cores = list(range(NCORES)) if _cores is None else _cores
    key = ('nc', _debug)
    if key not in _CACHE:
        _CACHE[key] = build_program(_debug)
    nc = _CACHE[key]
    consts = _consts()
    shared = {
        "w_ada": f(inp["w_ada"][0]), "b_ada": f(inp["b_ada"][0][None, :]),
        "nrm": f(np.stack([inp["norm1"][0], inp["norm2"][0], inp["final_norm"]])),
        "w_in": f(inp["w_in"][0]), "w_a2": f(inp["w_a2"][0]), "b_a2": f(inp["b_a2"][0][None, :]),
        "gla_norm": f(inp["gla_norm"][0][None, :]),
        "lam": f(np.stack([inp["s5_lam_re"][0], inp["s5_lam_im"][0]])), "lstep": f(inp["s5_log_step"][0][None, :]),
        "sb": f(np.stack([inp["s5_b_re"][0], inp["s5_b_im"][0]])), "sc": f(np.stack([inp["s5_c_re"][0], inp["s5_c_im"][0]])),
        "sd": f(inp["s5_d"][0]), "w_glu": f(inp["w_glu"][0]), "b_glu": f(inp["b_glu"][0][None, :]),
        "w_out": f(inp["w_out"][0]), "w_up": f(inp["w_up"][0]),
        "cw": f(np.concatenate([inp["conv_w"][0], inp["conv_b"][0][None, :]], 0)), "w_down": f(inp["w_down"][0]),
    }
    shared.update(consts)
    in_maps = []
    for c in cores:
        s, hf = c // 2, c % 2
        m = dict(shared)
        m["xm"] = f(inp["x_prompt"][s, hf * TM:(hf + 1) * TM])
        m["xpre"] = f(inp["x_prompt"][s, 0:TM])
        m["c_mask"] = np.full((128, 1), float(hf), np.float32)
        m["xs"] = f(inp["x_sample"][16 * c:16 * c + 16].reshape(64, D))
        m["cc"] = f(np.concatenate([inp["c_prompt"][s:s + 1], inp["c_sample"][16 * c:16 * c + 16]], 0))
        m["sgla"] = f(inp["state_gla"][0, 16 * c:16 * c + 16])
        m["s5in"] = f(np.stack([inp["state_s5_re"][0, 16 * c:16 * c + 16], inp["state_s5_im"][0, 16 * c:16 * c + 16]]))
        m["sconv"] = f(inp["state_conv"][0, 16 * c:16 * c + 16].reshape(32, 2 * DFF))
        in_maps.append(m)
    res = run_bass_kernel_spmd(nc, in_maps, core_ids=list(range(len(cores))))
    R = res.results
    if _cores is not None:
        return R
    y_prompt = np.zeros((4, 2048, D), np.float32); y_sample = np.zeros((128, 4, D), np.float32)
    gla_p = np.zeros((1, 4, 4, 128, 256), np.float32); s5re_p = np.zeros((1, 4, 64, 64), np.float32)
    s5im_p = np.zeros((1, 4, 64, 64), np.float32); conv_p = np.zeros((1, 4, 2, 2 * DFF), np.float32)
    gla_s = np.zeros((1, 128, 4, 128, 256), np.float32); s5re_s = np.zeros((1, 128, 64, 64), np.float32)
    s5im_s = np.zeros((1, 128, 64, 64), np.float32); conv_s = np.zeros((1, 128, 2, 2 * DFF), np.float32)
    for c in range(NCORES):
        s, hf = c // 2, c % 2
        r = R[c]
        y_prompt[s, hf * TM:(hf + 1) * TM] = r["yp"]
        y_sample[16 * c:16 * c + 16] = r["ys"].reshape(16, 4, D)
        gla_s[0, 16 * c:16 * c + 16] = r["glas"]
        s5 = r["s5s"].reshape(16, 64, 2, 64)
        s5re_s[0, 16 * c:16 * c + 16] = s5[:, :, 0]; s5im_s[0, 16 * c:16 * c + 16] = s5[:, :, 1]
        conv_s[0, 16 * c:16 * c + 16] = r["convs"].reshape(16, 4, 2 * DFF)[:, 2:4]
        if hf == 1:
            gla_p[0, s] = r["glap"]
            sp5 = r["s5p"].reshape(64, 2, 64)
            s5re_p[0, s] = sp5[:, 0]; s5im_p[0, s] = sp5[:, 1]
            conv_p[0, s] = r["convp"]
    return (y_prompt, y_sample, gla_p, s5re_p, s5im_p, conv_p, gla_s, s5re_s, s5im_s, conv_s)
```

```python
import math
from contextlib import ExitStack
import numpy as np
import concourse.bass as bass
import concourse.mybir as mybir
from concourse.bass_utils import run_bass_kernel_spmd

F32 = mybir.dt.float32
BF16 = mybir.dt.bfloat16
I32 = mybir.dt.int32
AF = mybir.ActivationFunctionType
ALU = mybir.AluOpType

D = 2048
KC = 16
TM = 1024
TS = 64
TF = TM + TS
TT = TF + 128
NTILE = 10
NTO = 9
TILES = [(i * 128, 128) for i in range(8)] + [(1024, 64), (1088, 128)]
CGRP = [(0, 512), (512, 512), (1024, 64), (1088, 128)]
NCH = 160
NCP = 112
DFF = 5632
EPS = 1e-6
NCORES = 8


class Prog:
    ENGS = ['pe', 'act', 'dve', 'pool', 'sp']

    def __init__(self, nc, es):
        self.nc = nc
        self.es = es
        self.sem = {e: es.enter_context(nc.semaphore("s_" + e)) for e in ['pe', 'act', 'dve', 'pool']}
        self.cnt = {e: 0 for e in ['pe', 'act', 'dve', 'pool']}
        self.dsem = {}
        self.dcnt = {}
        self.dpool = [es.enter_context(nc.semaphore("dq%d" % i)) for i in range(64)]
        self.seen = {e: {} for e in self.ENGS}
        self.psi = 0
        self.nph = 0
        self.reset()

    def reset(self):
        self.q = {e: [] for e in self.ENGS}
        self.lw = {k: v for k, v in getattr(self, 'lw', {}).items() if v[0].startswith('d:')}
        self.rd = {k: [t for t in v if t[0].startswith('d:')] for k, v in getattr(self, 'rd', {}).items()}

    def _waits(self, e, r, w):
        deps = {}
        def add(tok):
            k, v = tok
            if k == 'pe' and e == 'pe':
                return
            if deps.get(k, 0) < v:
                deps[k] = v
        for k in r:
            if k in self.lw:
                add(self.lw[k])
        for k in w:
            if k in self.lw:
                add(self.lw[k])
            for t in self.rd.get(k, ()):
                add(t)
        out = []
        for k, v in deps.items():
            if self.seen[e].get(k, 0) < v:
                self.seen[e][k] = v
                out.append((k, v))
        return out

    def _commit(self, tok, r, w):
        for k in w:
            self.lw[k] = tok
            self.rd[k] = []
        for k in r:
            self.rd.setdefault(k, []).append(tok)

    def op(self, e, fn, r=(), w=(), inc=True):
        assert inc or e == 'pe'
        waits = self._waits(e, r, w)
        if inc:
            self.cnt[e] += 1
            tok = (e, self.cnt[e])
        else:
            tok = (e, self.cnt[e] + 1)
        self.q[e].append((waits, fn, e if inc else None))
        self._commit(tok, r, w)

    def dma(self, e, fn, r=(), w=(), sem='x'):
        waits = self._waits(e, r, w)
        if sem not in self.dsem:
            self.dsem[sem] = self.dpool.pop()
            self.dcnt[sem] = 0
        self.dcnt[sem] += 16
        tok = ('d:' + sem, self.dcnt[sem])
        self.q[e].append((waits, fn, 'd:' + sem))
        self._commit(tok, r, w)

    def settle(self, keys, sem):
        tok = ('d:' + sem, self.dcnt[sem])
        for k in keys:
            self.lw[k] = tok

    def semobj(self, k):
        return self.dsem[k[2:]] if k.startswith('d:') else self.sem[k]

    def emit(self, final=False):
        nc = self.nc
        if final:
            for s, v in self.dcnt.items():
                k = 'd:' + s
                if self.seen['sp'].get(k, 0) < v:
                    self.seen['sp'][k] = v
                    self.q['sp'].append(([(k, v)], None, None))
        q = self.q
        me = self

        def replay(eng, items):
            for waits, fn, kind in items:
                for k, v in waits:
                    eng.wait_ge(me.semobj(k), v)
                if fn is None:
                    continue
                ins = fn(eng)
                if kind is None:
                    continue
                if kind.startswith('d:'):
                    ins.then_inc(me.dsem[kind[2:]], 16)
                else:
                    ins.then_inc(me.sem[kind], 1)

        self.nph += 1
        with nc.Block() as block:
            @block.tensor
            def _(eng):
                replay(eng, q['pe'])

            @block.scalar
            def _(eng):
                replay(eng, q['act'])

            @block.vector
            def _(eng):
                replay(eng, q['dve'])

            @block.gpsimd
            def _(eng):
                replay(eng, q['pool'])

            @block.sync
            def _(eng):
                replay(eng, q['sp'])
        self.reset()


import os
STOP = int(os.environ.get('KSTOP', '99'))


def build_program(debug=False):
    nc = bass.Bass("TRN2", target_bir_lowering=False)

    def din(name, shape, dt=F32):
        return nc.dram_tensor(name, list(shape), dt, kind="ExternalInput").ap()

    def dout(name, shape):
        return nc.dram_tensor(name, list(shape), F32, kind="ExternalOutput").ap()

    xm = din("xm", [TM, D]); xs = din("xs", [TS, D]); cc = din("cc", [17, D])
    xpre = din("xpre", [TM, D]); c_mask = din("c_mask", [128, 1])
    sgla = din("sgla", [16, 4, 128, 256]); s5in = din("s5in", [2, 16, 64, 64]); sconv = din("sconv", [32, 2 * DFF])
    w_ada = din("w_ada", [D, 6 * D]); b_ada = din("b_ada", [1, 6 * D]); nrm = din("nrm", [3, D])
    w_in = din("w_in", [D, 4112]); w_a2 = din("w_a2", [16, 512]); b_a2 = din("b_a2", [1, 512])
    gla_norm = din("gla_norm", [1, 256])
    lam = din("lam", [2, 64, 64]); lstep = din("lstep", [1, 64])
    sb = din("sb", [2, 64, 64, 16]); sc = din("sc", [2, 64, 16, 64]); sd = din("sd", [64, 16])
    w_glu = din("w_glu", [1024, 1024]); b_glu = din("b_glu", [1, 1024])
    w_out = din("w_out", [D, D]); w_up = din("w_up", [D, 2 * DFF]); cw = din("cw", [4, 2 * DFF])
    w_down = din("w_down", [DFF, D])
    c_ident = din("c_ident", [128, 128]); c_gm = din("c_gm", [128, 6, 128]); c_seq = din("c_seq", [64, 2, 16])
    c_bm = din("c_bm", [128, 16, 64]); c_ev = din("c_ev", [128, 34]); c_sg = din("c_sg", [128, 2])
    c_cm = din("c_cm", [128, 128])

    yp = dout("yp", [TM, D]); ys = dout("ys", [TS, D]); glap = dout("glap", [4, 128, 256])
    s5p = dout("s5p", [64, 128]); convp = dout("convp", [2, 2 * DFF])
    glas = dout("glas", [16, 4, 128, 256]); s5s = dout("s5s", [16, 64, 128]); convs = dout("convs", [64, 2 * DFF])
    dbg = dout("dbg", [128, 16, TT]) if debug else None

    modd = nc.dram_tensor("modd", [17, 6 * D], F32).ap()
    dU = nc.dram_tensor("dU", [64, 16, 8, NCH], BF16).ap()
    dY = nc.dram_tensor("dY", [64, 16, 8, NCH], BF16).ap()
    dU2 = nc.dram_tensor("dU2", [64, 16, 8, NCP], BF16).ap()

    with ExitStack() as es:
        P = Prog(nc, es)

        def sb_t(stack, name, shape, dt=F32):
            return stack.enter_context(nc.sbuf_tensor(name, list(shape), dt))

        ps = [es.enter_context(nc.psum_tensor("ps%d" % i, [128, 512], F32)) for i in range(8)]

        held = set()

        def nps():
            while P.psi in held:
                P.psi = (P.psi + 1) % 8
            i = P.psi
            P.psi = (P.psi + 1) % 8
            return ps[i], ('ps', i)

        ident = sb_t(es, "ident", [128, 128])
        seq = sb_t(es, "seq", [64, 2, 16])
        sgc = sb_t(es, "sgc", [128, 2])
        maskt = sb_t(es, "maskt", [128, 1])
        scT = sb_t(es, "scT", [128, KC, 17], BF16)
        nrmT = sb_t(es, "nrmT", [128, KC, 3])
        modT = sb_t(es, "modT", [128, 4, KC, 17])
        Amod = sb_t(es, "Amod", [128, 2, KC, 17])
        cwT = sb_t(es, "cwT", [128, 88, 4])
        bgT = sb_t(es, "bgT", [128, 8])
        gnT = sb_t(es, "gnT", [128, 2])
        wa2 = sb_t(es, "wa2", [16, 512], BF16)
        ba2 = sb_t(es, "ba2", [1, 512], BF16)
        ones = sb_t(es, "ones", [1, 128], BF16)
        actT = sb_t(es, "actT", [128, KC, TT], BF16)
        wr = [None] * 4

        nring = [4]

        def alloc_ring(stack, n=4, width=256):
            nring[0] = n
            wri[0] = 0
            for i in range(n):
                wr[i] = sb_t(stack, "wr%d_%d" % (i, P.nph), [128, KC, width], BF16)
        wri = [0]

        def nwr():
            i = wri[0]
            wri[0] = (i + 1) % nring[0]
            return wr[i], ('wr', i)

        def mm(out, lhsT, rhs, start, stop, r, w, inc=True):
            P.op('pe', lambda e, o=out, l=lhsT, rr=rhs, s=start, t=stop: e.matmul(o, lhsT=l, rhs=rr, start=s, stop=t), r, w, inc)

        def tr(out, in_, r, w, inc=True):
            n = in_.shape[0]
            p0 = in_.base_partition()
            P.op('pe', lambda e, o=out, i=in_, n=n, p0=p0: e.transpose(o, i, ident[p0:p0 + n, p0:p0 + n]), r, w, inc)

        def act(out, in_, func, r, w, bias=None, scale=None, accum=None):
            kw = {}
            if bias is not None:
                kw['bias'] = bias
            if scale is not None:
                kw['scale'] = scale
            if accum is not None:
                kw['accum_out'] = accum
            P.op('act', lambda e, o=out, i=in_, f=func, kw=kw: e.activation(out=o, in_=i, func=f, **kw), r, w)

        def tt(eng, out, in0, in1, op, r, w):
            P.op(eng, lambda e, o=out, a=in0, b=in1, op=op: e.tensor_tensor(out=o, in0=a, in1=b, op=op), r, w)

        def stt(out, in0, scalar, in1, op0, op1, r, w):
            P.op('dve', lambda e, o=out, a=in0, s=scalar, b=in1, o0=op0, o1=op1:
                 e.scalar_tensor_tensor(out=o, in0=a, scalar=s, in1=b, op0=o0, op1=o1), r, w)

        def tsc(eng, out, in0, s1, s2, op0, op1, r, w):
            if s2 is None:
                P.op(eng, lambda e, o=out, a=in0, s1=s1, o0=op0: e.tensor_scalar(out=o, in0=a, scalar1=s1, scalar2=None, op0=o0), r, w)
            else:
                P.op(eng, lambda e, o=out, a=in0, s1=s1, s2=s2, o0=op0, o1=op1:
                     e.tensor_scalar(out=o, in0=a, scalar1=s1, scalar2=s2, op0=o0, op1=o1), r, w)

        def cp(eng, out, in_, r, w):
            if eng == 'act':
                P.op('act', lambda e, o=out, i=in_: e.copy(out=o, in_=i), r, w)
            else:
                P.op(eng, lambda e, o=out, i=in_: e.tensor_copy(out=o, in_=i), r, w)

        def recip(out, in_, r, w):
            P.op('dve', lambda e, o=out, i=in_: e.reciprocal(out=o, in_=i), r, w)

        def memset(eng, ap, val, w):
            P.op(eng, lambda e, a=ap, v=val: e.memset(a, v), (), w)

        def dma(eng, out, in_, r, w, sem):
            P.dma(eng, lambda e, o=out, i=in_: e.dma_start(out=o, in_=i), r, w, sem)

        evi = [0]

        def evac_eng():
            evi[0] ^= 1
            return 'act' if evi[0] else 'dve'

        def rstd_chain(ssq, tmp, out, n, scale, key):
            tsc('dve', tmp, ssq, scale, EPS, ALU.mult, ALU.add, [key], [key])
            P.op('act', lambda e, o=tmp, i=tmp: e.activation(out=o, in_=i, func=AF.Ln), [key], [key])
            P.op('act', lambda e, o=out, i=tmp: e.activation(out=o, in_=i, func=AF.Exp, scale=-0.5), [key], [key])

        def rows_to_fm(stack_rows, nrows, width, out_view_fn, rkey, wkey):
            nchunks = width // 128
            per = max(1, 512 // nrows)
            c = 0
            while c < nchunks:
                n = min(per, nchunks - c)
                pt, pk = nps()
                for i in range(n):
                    tr(pt[:, i * nrows:(i + 1) * nrows], stack_rows[0:nrows, (c + i) * 128:(c + i + 1) * 128], [rkey], [pk], inc=(i == n - 1))
                cp('dve', out_view_fn(c, n), pt[:, 0:n * nrows].rearrange("p (c r) -> p c r", r=nrows), [pk], [wkey])
                c += n

        class AdaWork:
            def __init__(self, stack, parts):
                self.modst = [sb_t(stack, "modst%d_%d" % (i, P.nph), [17, D]) for i in range(2)]
                self.bad = [sb_t(stack, "bad%d_%d" % (i, P.nph), [17, D]) for i in range(2)]
                self.todo = [(part, i8) for part in parts for i8 in range(8)]
                self.pt = None

            def step(self, k):
                for _ in range(k):
                    if not self.todo:
                        return
                    part, i8 = self.todo.pop(0)
                    pb = part % 2
                    modst, bad = self.modst[pb], self.bad[pb]
                    if i8 == 0:
                        dma('sp', bad[:], b_ada[:, part * D:(part + 1) * D].partition_broadcast(17), [], [('bad', pb)], 'bad%d' % pb)
                    i = part * 8 + i8
                    wt, wk = nwr()
                    dma('pool', wt[:], w_ada[:, i * 256:(i + 1) * 256].rearrange("(c p) n -> p c n", p=128), [], [wk], 'wr%d' % wk[1])
                    if i % 2 == 0:
                        self.pt = nps()
                    pt, pk = self.pt
                    for kc in range(KC):
                        mm(pt[0:17, (i % 2) * 256:(i % 2) * 256 + 256], scT[:, kc, :], wt[:, kc, :], kc == 0, kc == KC - 1,
                           ['scT', wk], [pk], inc=(kc == KC - 1))
                    if i % 2 == 1:
                        tt('dve', modst[:, (i8 - 1) * 256:(i8 + 1) * 256], pt[0:17, :], bad[:, (i8 - 1) * 256:(i8 + 1) * 256], ALU.add,
                           [pk, ('bad', pb)], [('modst', pb)])
                    if i8 == 7:
                        dma('sp', modd[:, part * D:(part + 1) * D], modst[:], [('modst', pb)], ['modd'], 'modd%d' % pb)
                        if part in (0, 1, 3, 4):
                            qi = {0: 0, 1: 1, 3: 2, 4: 3}[part]
                            rows_to_fm(modst, 17, D, lambda c, n, qi=qi: modT[:, qi, c:c + n, :], ('modst', pb), 'modT')
                        if part in (1, 4):
                            n_ = 0 if part == 1 else 1
                            tsc('dve', Amod[:, n_], modT[:, 1 + 2 * n_], 1.0, None, ALU.add, None, ['modT'], ['Amod'])
                            tt('dve', Amod[:, n_], Amod[:, n_], nrmT[:, :, n_:n_ + 1].to_broadcast([128, KC, 17]), ALU.mult,
                               ['Amod', 'nrmT'], ['Amod'])

        with ExitStack() as ph:
            cct = sb_t(ph, "cct", [17, D]); sct = sb_t(ph, "sct", [17, D])
            nrmt = sb_t(ph, "nrmt", [3, D])
            cwr = sb_t(ph, "cwr", [4, DFF]); smr = sb_t(ph, "smr", [1, 1280])
            smT = sb_t(ph, "smT", [128, 10, 1])
            alloc_ring(ph)
            init_keys = ['ident', 'seq', 'sgc', 'cct', 'nrmt', 'smr', 'mask']
            for (o, i) in [(ident[:], c_ident), (seq[:], c_seq), (maskt[:], c_mask), (sgc[:], c_sg), (cct[:], cc),
                           (nrmt[:], nrm), (smr[:, 0:1024], b_glu), (smr[:, 1024:1280], gla_norm)]:
                dma('sp', o, i, [], [], 'init')
            P.settle(init_keys, 'init')
            dma('pool', wa2[:], w_a2, [], ['wa2'], 'wa2')
            dma('pool', ba2[:], b_a2, [], ['wa2'], 'wa2')
            memset('dve', ones[:], 1.0, ['ones'])
            act(sct[:], cct[:], AF.Silu, ['cct'], ['sct'])
            rows_to_fm(sct, 17, D, lambda c, n: scT[:, c:c + n, :], 'sct', 'scT')
            rows_to_fm(nrmt, 3, D, lambda c, n: nrmT[:, c:c + n, :], 'nrmt', 'nrmT')
            ada = AdaWork(ph, [0, 1])
            ada.step(16)
            for blk in range(2):
                dma('sp', cwr[:], cw[:, blk * DFF:(blk + 1) * DFF], [], ['cwr'], 'cwr')
                rows_to_fm(cwr, 4, DFF, lambda c, n, blk=blk: cwT[:, blk * 44 + c:blk * 44 + c + n, :], 'cwr', 'cwT')
            rows_to_fm(smr, 1, 1280, lambda c, n: smT[:, c:c + n, :], 'smr', 'smT')
            cp('dve', bgT[:], smT[:, 0:8, 0], ['smT'], ['bgT'])
            cp('dve', gnT[:], smT[:, 8:10, 0], ['smT'], ['gnT'])
            P.emit(final=(STOP == 0))
        if STOP == 0:
            return nc

        def norm_phase(which, src_tile_fn, pid, tiles=None, dstT=None, kname='actT', post=None, ada_parts=None, ada_k=0):
            tiles = list(enumerate(TILES)) if tiles is None else tiles
            dstT = actT if dstT is None else dstT
            with ExitStack() as ph:
                xn = [sb_t(ph, "xn%d_%d" % (i, P.nph), [128, D]) for i in range(2)]
                xnb = [sb_t(ph, "xnb%d_%d" % (i, P.nph), [128, D], BF16) for i in range(2)]
                nsm = [sb_t(ph, "nsm%d_%d" % (i, P.nph), [128, 16, 4]) for i in range(2)]
                identb = sb_t(ph, "identb_%d" % P.nph, [128, 128], BF16)
                cp('dve', identb[:], ident[:], ['ident'], ['identb'])
                st = [sb_t(ph, "nst%d_%d" % (i, P.nph), [128, 4]) for i in range(2)]
                adaw = None
                if ada_parts:
                    alloc_ring(ph)
                    adaw = AdaWork(ph, ada_parts)
                for li, (ti, (t0, nt)) in enumerate(tiles):
                    b = li % 2
                    if adaw is not None:
                        adaw.step(ada_k)
                    xt, xk = src_tile_fn(ph, ti, t0, nt)
                    act(xn[b][0:nt, :], xt, AF.Square, [xk], [('xn', b), ('st', b)], accum=st[b][0:nt, 0:1])
                    rstd_chain(st[b][0:nt, 0:1], st[b][0:nt, 1:2], st[b][0:nt, 2:3], nt, 1.0 / D, ('st', b))
                    act(xnb[b][0:nt, :], xt, AF.Copy, [xk, ('st', b)], [('xnb', b)], scale=st[b][0:nt, 2:3])
                    for k4 in range(4):
                        pt32, pk = nps()
                        pt = pt32[:, :].bitcast(BF16)
                        for i in range(4):
                            kc = k4 * 4 + i
                            P.op('pe', lambda e, o=pt[:, i * nt:(i + 1) * nt], a=xnb[b][0:nt, kc * 128:(kc + 1) * 128], n_=nt:
                                 e.transpose(o, a, identb[0:n_, 0:n_]), [('xnb', b), 'identb'], [pk], inc=(i == 3))
                        for i in range(4):
                            kc = k4 * 4 + i
                            dst = dstT[:, kc, t0:t0 + nt]
                            src = pt[:, i * nt:(i + 1) * nt]
                            if ti != 8:
                                a_ap = Amod[:, which, kc, 0:1]
                                b_ap = modT[:, 2 * which, kc, 0:1]
                                if evac_eng() == 'act':
                                    act(dst, src, AF.Identity, [pk, 'Amod', 'modT'], [(kname, ti)], bias=b_ap, scale=a_ap)
                                else:
                                    tsc('dve', dst, src, a_ap, b_ap, ALU.mult, ALU.add, [pk, 'Amod', 'modT'], [(kname, ti)])
                            else:
                                s3 = src.rearrange("p (j i) -> p j i", i=4)
                                stmp = nsm[kc % 2]
                                tt('dve', stmp[:], s3, Amod[:, which, kc, 1:17].unsqueeze(2).to_broadcast([128, 16, 4]), ALU.mult,
                                   [pk, 'Amod'], [('nsm', kc % 2)])
                                s3 = stmp[:]
                                tt('dve', dst.rearrange("p (j i) -> p j i", i=4), s3,
                                   modT[:, 2 * which, kc, 1:17].unsqueeze(2).to_broadcast([128, 16, 4]), ALU.add,
                                   [('nsm', kc % 2), 'modT'], [(kname, ti)])
                if adaw is not None:
                    adaw.step(99)
                if post is not None:
                    post()
                P.emit(final=(STOP == pid))

        def x_src(ph, ti, t0, nt):
            if not hasattr(x_src, 'bufs'):
                x_src.bufs = [sb_t(ph, "xb%d" % i, [128, D]) for i in range(2)]
            b = x_src.n % 2
            x_src.n += 1
            src = xm[t0:t0 + nt, :] if ti < 8 else (xs if ti == 8 else xpre[896:1024, :])
            dma('sp', x_src.bufs[b][0:nt, :], src, [], [('xb', b)], 'xb%d' % b)
            return x_src.bufs[b][0:nt, :], ('xb', b)
        x_src.n = 0

        gx = ExitStack()
        gx.__enter__()
        gm = sb_t(gx, "gm", [128, 6, 128])
        Sst = sb_t(gx, "Sst", [128, 4, 256]); Sbf = sb_t(gx, "Sbf", [128, 4, 256], BF16)

        with ExitStack() as a0:
            hTp = sb_t(a0, "hTp", [128, KC, 896], BF16)
            PT = [(i, (i * 128, 128)) for i in range(7)]

            def xpre_src(ph, ti, t0, nt):
                if not hasattr(xpre_src, 'bufs'):
                    xpre_src.bufs = [sb_t(ph, "xpb%d" % i, [128, D]) for i in range(2)]
                b = ti % 2
                dma('sp', xpre_src.bufs[b][:], xpre[t0:t0 + nt, :], [], [('xpb', b)], 'xb%d' % b)
                return xpre_src.bufs[b][:], ('xpb', b)
            norm_phase(0, xpre_src, -1, tiles=PT, dstT=hTp, kname='hTp', ada_parts=[2, 3], ada_k=3)
            with ExitStack() as ph:
                alloc_ring(ph)
                ktp = sb_t(ph, "ktp", [128, 7, 512], BF16); vtp = sb_t(ph, "vtp", [128, 7, 1024], BF16)
                alp = sb_t(ph, "alp", [16, 896], BF16); uTq = sb_t(ph, "uTq", [128, 8, 8, NCP], BF16)
                lap = [sb_t(ph, "lap%d" % i, [128, 512]) for i in range(2)]
                erp = [sb_t(ph, "erp%d" % i, [128, 512]) for i in range(2)]
                khp = [sb_t(ph, "khp%d" % i, [128, 512], BF16) for i in range(2)]
                dcp = [sb_t(ph, "dcp%d" % i, [128, 4]) for i in range(2)]
                dma('sp', gm[:], c_gm, [], ['gm'], 'gm')
                memset('dve', Sst[:], 0.0, ['S'])
                hreads = [('hTp', i) for i in range(7)]
                pieces = [('k', 512), ('k', 768)] + [('v', 1024 + 256 * i) for i in range(4)] + [('a', 3072)] + [('u', 3088 + 256 * i) for i in range(4)]
                for (kind, c0) in pieces:
                    ncol = 16 if kind == 'a' else 256
                    wt, wk = nwr()
                    dma('pool', wt[:, :, 0:ncol], w_in[:, c0:c0 + ncol].rearrange("(c p) n -> p c n", p=128), [], [wk], 'wr%d' % wk[1])
                    if kind in ('k', 'v'):
                        for ti in range(7):
                            if ti % 2 == 0:
                                pt, pk = nps()
                            o0 = (ti % 2) * 256
                            for kc in range(KC):
                                mm(pt[:, o0:o0 + 256], hTp[:, kc, ti * 128:(ti + 1) * 128], wt[:, kc, :], kc == 0, kc == KC - 1, [wk, ('hTp', ti)], [pk], inc=(kc == KC - 1))
                            dst = ktp if kind == 'k' else vtp
                            dc0 = (c0 - 512) if kind == 'k' else (c0 - 1024)
                            cp(evac_eng(), dst[:, ti, dc0:dc0 + 256], pt[:, o0:o0 + 256], [pk], [kind + 'tp'])
                    elif kind == 'a':
                        for (g0, gn) in [(0, 512), (512, 384)]:
                            pt, pk = nps()
                            for kc in range(KC):
                                mm(pt[0:16, 0:gn], wt[:, kc, 0:16], hTp[:, kc, g0:g0 + gn], kc == 0, kc == KC - 1, [wk] + hreads, [pk], inc=(kc == KC - 1))
                            cp(evac_eng(), alp[:, g0:g0 + gn], pt[0:16, 0:gn], [pk], ['alp'])
                    else:
                        idx = (c0 - 3088) // 128
                        for half in range(2):
                            ch = idx + half
                            for (g0, gn) in [(0, 512), (512, 384)]:
                                pt, pk = nps()
                                for kc in range(KC):
                                    mm(pt[:, 0:gn], wt[:, kc, half * 128:(half + 1) * 128], hTp[:, kc, g0:g0 + gn], kc == 0, kc == KC - 1, [wk] + hreads, [pk], inc=(kc == KC - 1))
                                cp(evac_eng(), uTq[:, ch, :, g0 // 8:(g0 + gn) // 8], pt[:, 0:gn].rearrange("p (n s) -> p s n", s=8), [pk], ['uTq'])
                for ch in range(8):
                    dma('sp', dU2[ch * 8:(ch + 1) * 8].rearrange("g k s n -> (g k) s n"), uTq[:, ch], ['uTq'], ['dU2'], 'dU')
                for ti in range(7):
                    b = ti % 2
                    K = lambda n, b=b: (n, b)
                    t0 = ti * 128
                    pz, pzk = nps()
                    mm(pz[:, :], alp[:, t0:t0 + 128], wa2[:], True, False, ['alp', 'wa2'], [pzk], inc=False)
                    mm(pz[:, :], ones[0:1, 0:128], ba2[:], False, True, ['ones', 'wa2'], [pzk])
                    act(lap[b][:], pz[:, :], AF.Exp, [pzk], [K('lap')], scale=-1.0)
                    act(lap[b][:], lap[b][:], AF.Ln, [K('lap')], [K('lap')], bias=1.0)
                    prl, prk = nps()
                    mm(prl[:, :], gm[:, 1, :], lap[b][:], True, True, [K('lap'), 'gm'], [prk])
                    pc, pck = nps()
                    for h in range(4):
                        mm(pc[:, h:h + 1], lap[b][:, h * 128:(h + 1) * 128], gm[:, 0, 127:128], True, True, [K('lap'), 'gm'], [pck], inc=(h == 3))
                    act(dcp[b][:], pc[:, 0:4], AF.Exp, [pck], [K('dcp')])
                    act(erp[b][:], prl[:, :], AF.Exp, [prk], [K('erp')])
                    tt('pool', khp[b][:], erp[b][:], ktp[:, ti, :], ALU.mult, [K('erp'), 'ktp'], [K('khp')])
                    for hp in range(2):
                        pss, psk = nps()
                        for hh in range(2):
                            h = hp * 2 + hh
                            mm(pss[:, hh * 256:(hh + 1) * 256], khp[b][:, h * 128:(h + 1) * 128], vtp[:, ti, h * 256:(h + 1) * 256], True, True,
                               [K('khp'), 'vtp'], [psk], inc=(hh == 1))
                        for hh in range(2):
                            h = hp * 2 + hh
                            stt(Sst[:, h, :], Sst[:, h, :], dcp[b][:, h:h + 1], pss[:, hh * 256:(hh + 1) * 256], ALU.mult, ALU.add, ['S', K('dcp'), psk], ['S'])
                P.emit()

        norm_phase(0, x_src, 1, ada_parts=[4, 5], ada_k=2)
        if STOP == 1:
            return nc

        with ExitStack() as mx:
            qT = sb_t(mx, "qT", [128, 4, TT], BF16); kT = sb_t(mx, "kT", [128, 4, TT], BF16)
            ktok = sb_t(mx, "ktok", [128, NTILE, 512], BF16); vtok = sb_t(mx, "vtok", [128, NTILE, 1024], BF16)
            sgT = sb_t(mx, "sgT", [128, 8, TT], BF16); alT = sb_t(mx, "alT", [16, TT], BF16)

            with ExitStack() as ph:
                uTp = sb_t(ph, "uTp", [128, 8, 8, NCH], BF16)
                alloc_ring(ph)
                memset('pool', uTp[:, :, 4:8, 144:160], 0.0, ['uTp'])
                pieces = [('q', 0, 0), ('q', 256, 2), ('k', 512, 0), ('k', 768, 2)]
                pieces += [('v', 1024 + 256 * i, i) for i in range(4)]
                pieces += [('g', 2048 + 256 * i, 2 * i) for i in range(4)]
                pieces += [('a', 3072, 0)]
                pieces += [('u', 3088 + 256 * i, 2 * i) for i in range(4)]
                for (kind, c0, idx) in pieces:
                    ncol = 16 if kind == 'a' else 256
                    wt, wk = nwr()
                    dma('pool', wt[:, :, 0:ncol], w_in[:, c0:c0 + ncol].rearrange("(c p) n -> p c n", p=128), [], [wk], 'wr%d' % wk[1])
                    areads = [('actT', ti) for ti in range(NTILE)]
                    if kind in ('q', 'k', 'g', 'u'):
                        for half in range(2):
                            ch = idx + half
                            for (g0, gn) in CGRP:
                                pt, pk = nps()
                                for kc in range(KC):
                                    mm(pt[:, 0:gn], wt[:, kc, half * 128:(half + 1) * 128], actT[:, kc, g0:g0 + gn], kc == 0, kc == KC - 1,
                                       [wk] + areads, [pk], inc=(kc == KC - 1))
                                if kind == 'q':
                                    cp(evac_eng(), qT[:, ch, g0:g0 + gn], pt[:, 0:gn], [pk], ['qT'])
                                elif kind == 'k':
                                    cp(evac_eng(), kT[:, ch, g0:g0 + gn], pt[:, 0:gn], [pk], ['kT'])
                                elif kind == 'g':
                                    act(sgT[:, ch, g0:g0 + gn], pt[:, 0:gn], AF.Silu, [pk], ['sgT'])
                                else:
                                    if g0 != 1024:
                                        n0 = 16 + g0 // 8 if g0 < 1024 else 0
                                        cp(evac_eng(), uTp[:, ch, :, n0:n0 + gn // 8], pt[:, 0:gn].rearrange("p (n s) -> p s n", s=8), [pk], ['uTp'])
                                    else:
                                        cp(evac_eng(), uTp[:, ch, 0:4, 144:160], pt[:, 0:64].rearrange("p (j i) -> p i j", i=4), [pk], ['uTp'])
                    if kind in ('k', 'v'):
                        for ti, (t0, nt) in enumerate(TILES):
                            if ti % 2 == 0:
                                pt, pk = nps()
                            o0 = (ti % 2) * 256
                            for kc in range(KC):
                                mm(pt[0:nt, o0:o0 + 256], actT[:, kc, t0:t0 + nt], wt[:, kc, :], kc == 0, kc == KC - 1,
                                   [wk, ('actT', ti)], [pk], inc=(kc == KC - 1))
                            dst = ktok if kind == 'k' else vtok
                            dc0 = (c0 - 512) if kind == 'k' else (c0 - 1024)
                            cp(evac_eng(), dst[0:nt, ti, dc0:dc0 + 256], pt[0:nt, o0:o0 + 256], [pk], [kind + 'tok'])
                    if kind == 'a':
                        for (g0, gn) in CGRP:
                            pt, pk = nps()
                            for kc in range(KC):
                                mm(pt[0:16, 0:gn], wt[:, kc, 0:16], actT[:, kc, g0:g0 + gn], kc == 0, kc == KC - 1,
                                   [wk] + areads, [pk], inc=(kc == KC - 1))
                            cp(evac_eng(), alT[:, g0:g0 + gn], pt[0:16, 0:gn], [pk], ['alT'])
                for ch in range(8):
                    dma('sp', dU[ch * 8:(ch + 1) * 8].rearrange("g k s n -> (g k) s n"), uTp[:, ch], ['uTp'], ['dU'], 'dU')
                P.emit(final=(STOP == 2))
            if STOP == 2:
                return nc

            with ExitStack() as ph:
                NB = 2
                la = [sb_t(ph, "la%d" % i, [128, 512]) for i in range(NB)]
                eq = [sb_t(ph, "eq%d" % i, [128, 4, 128]) for i in range(NB)]
                ek = [sb_t(ph, "ek%d" % i, [128, 4, 128]) for i in range(NB)]
                er = [sb_t(ph, "er%d" % i, [128, 512]) for i in range(NB)]
                qtl = [sb_t(ph, "qtl%d" % i, [128, 4, 128], BF16) for i in range(NB)]
                ktl = [sb_t(ph, "ktl%d" % i, [128, 4, 128], BF16) for i in range(NB)]
                khat = [sb_t(ph, "khat%d" % i, [128, 512], BF16) for i in range(NB)]
                att = [sb_t(ph, "att%d" % i, [128, 4, 128], BF16) for i in range(NB)]
                on = [sb_t(ph, "on%d" % i, [128, 4, 256]) for i in range(NB)]
                hs = [sb_t(ph, "hs%d" % i, [128, 12]) for i in range(NB)]
                dec = [sb_t(ph, "dec%d" % i, [128, 4]) for i in range(NB)]
                decs = sb_t(ph, "decs", [128, 4, 16])
                qf = sb_t(ph, "qf", [128, 4, 64])
                qexp = [sb_t(ph, "qexp%d" % i, [128, 4, 64]) for i in range(2)]
                kexp = [sb_t(ph, "kexp%d" % i, [64, 4, 128], BF16) for i in range(2)]
                bm = sb_t(ph, "bm", [128, 16, 64])
                sst = [sb_t(ph, "sst%d" % i, [128, 4, 256]) for i in range(2)]
                sno = [sb_t(ph, "sno%d" % i, [128, 4, 256]) for i in range(2)]
                dma('sp', bm[:], c_bm, [], ['bm'], 'bm')
                cp('act', Sbf[:], Sst[:], ['S'], ['Sbf'])
                for oi, ti in enumerate([9, 0, 1, 2, 3, 4, 5, 6, 7, 8]):
                    t0, nt = TILES[ti]
                    b = oi % NB
                    smp = ti == 8
                    K = lambda n, b=b: (n, b)
                    TRI = gm[0:nt, 3 if smp else 0, 0:nt]
                    UP = gm[0:nt, 4 if smp else 1, 0:nt]
                    M01 = gm[0:nt, 5 if smp else 2, 0:nt]
                    pz, pzk = nps()
                    mm(pz[0:nt, :], alT[:, t0:t0 + nt], wa2[:], True, False, ['alT', 'wa2'], [pzk], inc=False)
                    mm(pz[0:nt, :], ones[0:1, 0:nt], ba2[:], False, True, ['ones', 'wa2'], [pzk])
                    act(la[b][0:nt, :], pz[0:nt, :], AF.Exp, [pzk], [K('la')], scale=-1.0)
                    act(la[b][0:nt, :], la[b][0:nt, :], AF.Ln, [K('la')], [K('la')], bias=1.0)
                    pc, pck = nps()
                    for h in range(4):
                        mm(pc[:, h * nt:(h + 1) * nt], la[b][0:nt, h * 128:(h + 1) * 128], TRI, True, True, [K('la'), 'gm'], [pck], inc=(h == 3))
                    prl, prk = nps()
                    mm(prl[0:nt, :], UP, la[b][0:nt, :], True, True, [K('la'), 'gm'], [prk])
                    pc3 = pc[:, 0:4 * nt].rearrange("p (h t) -> p h t", h=4)
                    act(eq[b][:, :, 0:nt], pc3, AF.Exp, [pck], [K('eq')])
                    act(ek[b][:, :, 0:nt], pc3, AF.Exp, [pck], [K('ek')], scale=-1.0)
                    if not smp:
                        act(dec[b][:], pc3[:, :, nt - 1], AF.Exp, [pck], [K('dec')])
                    act(er[b][0:nt, :], prl[0:nt, :], AF.Exp, [prk], [K('er')])
                    stt(qtl[b][:, :, 0:nt], eq[b][:, :, 0:nt], 128 ** -0.5, qT[:, :, t0:t0 + nt], ALU.mult, ALU.mult, [K('eq'), 'qT'], [K('qtl')])
                    tt('dve', ktl[b][:, :, 0:nt], ek[b][:, :, 0:nt], kT[:, :, t0:t0 + nt], ALU.mult, [K('ek'), 'kT'], [K('ktl')])
                    tt('pool', khat[b][0:nt, :], er[b][0:nt, :], ktok[0:nt, ti, :], ALU.mult, [K('er'), 'ktok'], [K('khat')])
                    pa, pak = nps()
                    for h in range(4):
                        mm(pa[0:nt, h * nt:(h + 1) * nt], ktl[b][:, h, 0:nt], qtl[b][:, h, 0:nt], True, True, [K('ktl'), K('qtl')], [pak], inc=(h == 3))
                    tt('dve', att[b][0:nt, :, 0:nt], pa[0:nt, 0:4 * nt].rearrange("p (h t) -> p h t", h=4),
                       M01.unsqueeze(1).to_broadcast([nt, 4, nt]), ALU.mult, [pak, 'gm'], [K('att')])
                    po = [nps(), nps()]
                    if smp:
                        po4 = [po[0], po[1], nps(), nps()]
                        for (_, pk_) in po4:
                            held.add(pk_[1])
                        obank = lambda h: (po4[h][0][0:nt, 0:256], po4[h][1])
                    else:
                        obank = lambda h: (po[h // 2][0][0:nt, (h % 2) * 256:(h % 2) * 256 + 256], po[h // 2][1])
                    if not smp:
                        for h in range(4):
                            oo, pk = obank(h)
                            mm(oo, att[b][0:nt, h, 0:nt], vtok[0:nt, ti, h * 256:(h + 1) * 256], True, False, [K('att'), 'vtok'], [pk], inc=False)
                            mm(oo, qtl[b][:, h, 0:nt], Sbf[:, h, :], False, True, [K('qtl'), 'Sbf'], [pk], inc=(h % 2 == 1))
                    else:
                        pd, pdk = nps()
                        for h in range(4):
                            mm(pd[:, h * 16:(h + 1) * 16], la[b][0:nt, h * 128:(h + 1) * 128], seq[:, 1, :], True, True, [K('la'), 'seq'], [pdk], inc=(h == 3))
                        act(decs[:], pd[:, 0:64].rearrange("p (h j) -> p h j", h=4), AF.Exp, [pdk], ['decs'])
                        stt(qf[:], eq[b][:, :, 0:nt], 128 ** -0.5, qT[:, :, t0:t0 + nt], ALU.mult, ALU.mult, [K('eq'), 'qT'], ['qf'])
                        for h in range(4):
                            oo, pk = obank(h)
                            mm(oo, att[b][0:nt, h, 0:nt], vtok[0:nt, ti, h * 256:(h + 1) * 256], True, False, [K('att'), 'vtok'], [pk], inc=False)
                        for j in range(16):
                            sbi = j % 2
                            tt('dve', qexp[sbi][:], qf[:], bm[:, j, :].unsqueeze(1).to_broadcast([128, 4, 64]), ALU.mult, ['qf', 'bm'], [('qexp', sbi)])
                            tsc('pool', kexp[sbi][:].rearrange("p h d -> p (h d)"), khat[b][0:64, :], seq[:, 0, j:j + 1], None, ALU.mult, None,
                                [K('khat'), 'seq'], [('kexp', sbi)])
                            dma('sp', sst[sbi][:], sgla[j].rearrange("h d v -> d h v"), [], [('sst', sbi)], 'sst%d' % sbi)
                            for h in range(4):
                                oo, pk = obank(h)
                                mm(oo, qexp[sbi][:, h, :], sst[sbi][:, h, :], False, j == 15, [('qexp', sbi), ('sst', sbi)], [pk],
                                   inc=(j == 15 or h == 3))
                            for hp in range(2):
                                pss, psk = nps()
                                for hh in range(2):
                                    h = hp * 2 + hh
                                    mm(pss[:, hh * 256:(hh + 1) * 256], kexp[sbi][:, h, :], vtok[0:64, ti, h * 256:(h + 1) * 256], True, True,
                                       [('kexp', sbi), 'vtok'], [psk], inc=(hh == 1))
                                for hh in range(2):
                                    h = hp * 2 + hh
                                    stt(sno[sbi][:, h, :], sst[sbi][:, h, :], decs[:, h, j:j + 1], pss[:, hh * 256:(hh + 1) * 256],
                                        ALU.mult, ALU.add, [('sst', sbi), 'decs', psk], [('sno', sbi)])
                            dma('sp', glas[j].rearrange("h d v -> d h v"), sno[sbi][:], [('sno', sbi)], [], 'sno%d' % sbi)
                    held.clear()
                    for h in range(4):
                        oo, pk = obank(h)
                        act(on[b][0:nt, h, :], oo, AF.Square, [pk], [K('on'), K('hs')], accum=hs[b][0:nt, h:h + 1])
                    rstd_chain(hs[b][0:nt, 0:4], hs[b][0:nt, 4:8], hs[b][0:nt, 8:12], nt, 1.0 / 256, K('hs'))
                    for h in range(4):
                        oo, pk = obank(h)
                        act(on[b][0:nt, h, :], oo, AF.Copy, [pk, K('hs')], [K('on')], scale=hs[b][0:nt, 8 + h:9 + h])
                    for c4 in range(2):
                        pt, pk = nps()
                        for i in range(4):
                            ch = c4 * 4 + i
                            tr(pt[:, i * nt:(i + 1) * nt], on[b][0:nt, ch // 2, (ch % 2) * 128:(ch % 2) * 128 + 128], [K('on')], [pk], inc=(i == 3))
                        for i in range(4):
                            ch = c4 * 4 + i
                            stt(actT[:, ch, t0:t0 + nt], pt[:, i * nt:(i + 1) * nt], gnT[:, ch % 2:ch % 2 + 1], sgT[:, ch, t0:t0 + nt],
                                ALU.mult, ALU.mult, [pk, 'gnT', 'sgT'], [('actT', ti)])
                    if not smp:
                        for hp in range(2):
                            pss, psk = nps()
                            for hh in range(2):
                                h = hp * 2 + hh
                                mm(pss[:, hh * 256:(hh + 1) * 256], khat[b][0:nt, h * 128:(h + 1) * 128], vtok[0:nt, ti, h * 256:(h + 1) * 256],
                                   True, True, [K('khat'), 'vtok'], [psk], inc=(hh == 1))
                            for hh in range(2):
                                h = hp * 2 + hh
                                stt(Sst[:, h, :], Sst[:, h, :], dec[b][:, h:h + 1], pss[:, hh * 256:(hh + 1) * 256], ALU.mult, ALU.add,
                                    ['S', K('dec'), psk], ['S'])
                                cp('act', Sbf[:, h, :], Sst[:, h, :], ['S'], ['Sbf'])
                    if ti == 9:
                        tsc('dve', Sst[:].rearrange("p h v -> p (h v)"), Sst[:].rearrange("p h v -> p (h v)"), maskt[:, 0:1], None, ALU.mult, None, ['S', 'mask'], ['S'])
                        cp('act', Sbf[:], Sst[:], ['S'], ['Sbf'])
                    if ti == 7:
                        dma('sp', glap.rearrange("h d v -> d h v"), Sst[:], ['S'], [], 'glap')
                P.emit(final=(STOP == 3))
            if STOP == 3:
                return nc
        gx.close()

        def dbg_dump():
            if debug:
                dma('pool', dbg, actT[:], [('actT', ti) for ti in range(NTILE)], [], 'dbg')

        PI = math.pi
        with ExitStack() as s5x:
            A_su = sb_t(s5x, "A_su", [128, 64, 128], BF16)
            A_ys = sb_t(s5x, "A_ys", [128, 64, 128], BF16)
            A_yu = sb_t(s5x, "A_yu", [128, 64, 128], BF16)
            LL = sb_t(s5x, "LL", [128, 4, 64])
            s5y = ExitStack()
            s5y.__enter__()
            Pr2 = sb_t(s5y, "Pr2", [128, 64, 34]); Pi2 = sb_t(s5y, "Pi2", [128, 64, 34])
            Bbs = sb_t(s5y, "Bbs", [128, 64, 16]); Bbw = sb_t(s5y, "Bbw", [128, 64, 16])
            Cs = sb_t(s5y, "Cs", [128, 64, 16]); Cw = sb_t(s5y, "Cw", [128, 64, 16])
            Dcol = sb_t(s5y, "Dcol", [128, 64]); cm = sb_t(s5y, "cm", [128, 128])
            with ExitStack() as ph:
                lamt = sb_t(ph, "lamt", [64, 2, 128]); lre2 = sb_t(ph, "lre2", [128, 2, 64])
                dtb = sb_t(ph, "dtb", [128, 64]); ev = sb_t(ph, "ev", [128, 34])
                ld = sb_t(ph, "ld", [128, 2, 64])
                targ = sb_t(ph, "targ", [128, 64, 34]); tang = sb_t(ph, "tang", [128, 64, 34]); ttmp = sb_t(ph, "ttmp", [128, 64, 34])
                fw = sb_t(ph, "fw", [128, 10, 64]); tint = sb_t(ph, "tint", [128, 64, 34], I32)
                Bs = sb_t(ph, "Bs", [128, 64, 16]); Bw = sb_t(ph, "Bw", [128, 64, 16])
                bt1 = sb_t(ph, "bt1", [128, 64, 16]); bt2 = sb_t(ph, "bt2", [128, 64, 16])
                Cl = sb_t(ph, "Cl", [128, 8, 2, 64]); Clw = sb_t(ph, "Clw", [128, 8, 2, 64])
                sdt = sb_t(ph, "sdt", [64, 16]); sde = sb_t(ph, "sde", [64, 16, 8])
                ik = []
                for c_ in range(2):
                    for hh in range(2):
                        dma('sp', lamt[:, c_, hh * 64:(hh + 1) * 64], lam[c_], [], [], 'init2')
                dma('sp', dtb[:], lstep.partition_broadcast(128), [], [], 'init2')
                dma('sp', ev[:], c_ev, [], [], 'init2')
                dma('sp', cm[:], c_cm, [], [], 'init2')
                dma('sp', sdt[:], sd, [], [], 'init2')
                for q4 in range(4):
                    gs = slice(q4 * 16, (q4 + 1) * 16)
                    for half, (sa, sw) in enumerate([(0, 1), (1, 0)]):
                        ps_ = slice(half * 64, (half + 1) * 64)
                        dma('sp', Bs[ps_, gs, :], sb[sa, gs].rearrange("g p k -> p g k"), [], [], 'init2')
                        dma('sp', Bw[ps_, gs, :], sb[sw, gs].rearrange("g p k -> p g k"), [], [], 'init2')
                for c_ in range(2):
                    dma('sp', Cl[:, :, c_, :], sc[c_].rearrange("(gb gl) j p -> (gl j) gb p", gl=8), [], [], 'init2')
                    dma('sp', Clw[:, :, 1 - c_, :], sc[c_].rearrange("(gb gl) j p -> (gl j) gb p", gl=8), [], [], 'init2')
                P.settle(['lamt', 'dtb', 'ev', 'cm', 'sdt', 'Bs', 'Bw', 'Cl', 'Clw'], 'init2')
                for c_ in range(2):
                    pt, pk = nps()
                    tr(pt[:, 0:64], lamt[:, c_, :], ['lamt'], [pk])
                    cp('dve', lre2[:, c_, :], pt[:, 0:64], [pk], ['lre2'])
                act(dtb[:], dtb[:], AF.Exp, ['dtb'], ['dtb'])
                tt('dve', ld[:], lre2[:], dtb[:].unsqueeze(1).to_broadcast([128, 2, 64]), ALU.mult, ['lre2', 'dtb'], ['ld'])
                tt('dve', targ[:], ld[:, 0, :].unsqueeze(2).to_broadcast([128, 64, 34]), ev[:].unsqueeze(1).to_broadcast([128, 64, 34]), ALU.mult, ['ld', 'ev'], ['targ'])
                tt('pool', tang[:], ld[:, 1, :].unsqueeze(2).to_broadcast([128, 64, 34]), ev[:].unsqueeze(1).to_broadcast([128, 64, 34]), ALU.mult, ['ld', 'ev'], ['tang'])
                act(targ[:], targ[:], AF.Exp, ['targ'], ['targ'])
                for (dst, shift) in [(Pr2, PI / 2), (Pi2, 0.0)]:
                    tsc('dve', ttmp[:], tang[:], 1.0 / (2 * PI), (shift + PI) / (2 * PI) + 64.0, ALU.mult, ALU.add, ['tang'], ['ttmp'])
                    cp('dve', tint[:], ttmp[:], ['ttmp'], ['tint'])
                    cp('dve', Pi2[:], tint[:], ['tint'], ['Pi2'])
                    tt('dve', ttmp[:], ttmp[:], Pi2[:], ALU.subtract, ['ttmp', 'Pi2'], ['ttmp'])
                    tsc('dve', Pi2[:], ttmp[:], 0.0, None, ALU.is_lt, None, ['ttmp'], ['Pi2'])
                    tt('dve', ttmp[:], ttmp[:], Pi2[:], ALU.add, ['ttmp', 'Pi2'], ['ttmp'])
                    act(ttmp[:], ttmp[:], AF.Sin, ['ttmp'], ['ttmp'], bias=-PI, scale=2 * PI)
                    tt('dve', dst[:], ttmp[:], targ[:], ALU.mult, ['ttmp', 'targ'], [dst.name])
                lr, li = lre2[:, 0, :], lre2[:, 1, :]
                FK = ['fw']
                tsc('dve', fw[:, 0, :], Pr2[:, :, 8], -1.0, None, ALU.add, None, ['Pr2'], FK)
                cp('dve', fw[:, 1, :], Pi2[:, :, 8], ['Pi2'], FK)
                tt('dve', fw[:, 2, :], fw[:, 0, :], lr, ALU.mult, FK + ['lre2'], FK)
                tt('dve', fw[:, 3, :], fw[:, 1, :], li, ALU.mult, FK + ['lre2'], FK)
                tt('dve', fw[:, 4, :], fw[:, 2, :], fw[:, 3, :], ALU.add, FK, FK)
                tt('dve', fw[:, 7, :], fw[:, 1, :], lr, ALU.mult, FK, FK)
                tt('dve', fw[:, 8, :], fw[:, 0, :], li, ALU.mult, FK, FK)
                tt('dve', fw[:, 5, :], fw[:, 7, :], fw[:, 8, :], ALU.subtract, FK, FK)
                tt('dve', fw[:, 2, :], lr, lr, ALU.mult, FK, FK)
                tt('dve', fw[:, 3, :], li, li, ALU.mult, FK, FK)
                tt('dve', fw[:, 6, :], fw[:, 2, :], fw[:, 3, :], ALU.add, FK, FK)
                recip(fw[:, 6, :], fw[:, 6, :], FK, FK)
                tt('dve', fw[:, 4, :], fw[:, 4, :], fw[:, 6, :], ALU.mult, FK, FK)
                tt('dve', fw[:, 5, :], fw[:, 5, :], fw[:, 6, :], ALU.mult, FK, FK)
                Frb = fw[:, 4, :].unsqueeze(2).to_broadcast([128, 64, 16])
                Fib = fw[:, 5, :].unsqueeze(2).to_broadcast([128, 64, 16])
                fl = lambda t: t[:].rearrange("p g k -> p (g k)")
                tt('dve', bt1[:], Bw[:], Fib, ALU.mult, ['Bw'] + FK, ['bt1'])
                tt('dve', bt2[:], Bs[:], Frb, ALU.mult, ['Bs'] + FK, ['bt2'])
                stt(fl(Bbs), fl(bt1), sgc[:, 0:1], fl(bt2), ALU.mult, ALU.add, ['bt1', 'bt2', 'sgc'], ['Bbs'])
                tt('dve', bt1[:], Bs[:], Fib, ALU.mult, ['Bs', 'Bbs'] + FK, ['bt1'])
                tt('dve', bt2[:], Bw[:], Frb, ALU.mult, ['Bw', 'Bbs'] + FK, ['bt2'])
                stt(fl(Bbw), fl(bt1), sgc[:, 1:2], fl(bt2), ALU.mult, ALU.add, ['bt1', 'bt2', 'sgc'], ['Bbw'])
                for (src, dst, kname) in [(Cl, Cs, 'Cs'), (Clw, Cw, 'Cw')]:
                    for g4 in range(2):
                        pt, pk = nps()
                        for i in range(4):
                            gb = g4 * 4 + i
                            tr(pt[:, i * 128:(i + 1) * 128], src[:, gb].rearrange("p c q -> p (c q)"), ['Cl', 'Clw'], [pk], inc=(i == 3))
                        cp(evac_eng(), dst[:, g4 * 32:(g4 + 1) * 32, :].rearrange("p g j -> p (g j)"), pt[:, :], [pk], [kname])
                cp('dve', sde[:], sdt[:].unsqueeze(2).to_broadcast([64, 16, 8]), ['sdt'], ['sde'])
                pt, pk = nps()
                tr(pt[:, 0:64], sde[:].rearrange("g k s -> g (k s)"), ['sde'], [pk])
                cp('dve', Dcol[:], pt[:, 0:64], [pk], ['Dcol'])
                cp('dve', LL[:, 0, :], Pr2[:, :, 32], ['Pr2'], ['LL'])
                tsc('dve', LL[:, 1, :], Pi2[:, :, 32], sgc[:, 0:1], None, ALU.mult, None, ['Pi2', 'sgc'], ['LL'])
                cp('dve', LL[:, 2, :], Pr2[:, :, 33], ['Pr2'], ['LL'])
                tsc('dve', LL[:, 3, :], Pi2[:, :, 33], sgc[:, 0:1], None, ALU.mult, None, ['Pi2', 'sgc'], ['LL'])
                P.emit()
            with ExitStack() as ph:
                xa = [sb_t(ph, "xa%d" % i, [128, 8, 16, 8]) for i in range(2)]
                xb_ = [sb_t(ph, "xbb%d" % i, [128, 8, 16, 8]) for i in range(2)]
                X1 = sb_t(ph, "X1", [128, 8, 128]); BL = sb_t(ph, "BL", [128, 8, 128]); CL = sb_t(ph, "CL", [128, 8, 128])
                ytmp = [sb_t(ph, "ytmp%d" % i, [128, 128]) for i in range(2)]
                f4 = lambda t: t[:].rearrange("p a b c -> p (a b c)")
                for gb in range(8):
                    gs = slice(gb * 8, (gb + 1) * 8)
                    def bexp(T, e0):
                        return T[:, gs, e0:e0 + 8].unsqueeze(2).to_broadcast([128, 8, 16, 8])
                    def vexp(V):
                        return V[:, gs, :].unsqueeze(3).to_broadcast([128, 8, 16, 8])
                    b = gb % 2
                    tt('dve', xa[b][:], vexp(Bbs), bexp(Pr2, 0), ALU.mult, ['Bbs', 'Pr2'], [('xa', b)])
                    tt('pool', xb_[b][:], vexp(Bbw), bexp(Pi2, 0), ALU.mult, ['Bbw', 'Pi2'], [('xb_', b)])
                    stt(X1[:].rearrange("p g m -> p (g m)"), f4(xb_[b]), sgc[:, 0:1], f4(xa[b]), ALU.mult, ALU.add, [('xa', b), ('xb_', b), 'sgc'], ['X1'])
                    tt('dve', xa[b][:], vexp(Cs), bexp(Pr2, 8), ALU.mult, ['Cs', 'Pr2'], [('xa', b)])
                    tt('pool', xb_[b][:], vexp(Cw), bexp(Pi2, 8), ALU.mult, ['Cw', 'Pi2'], [('xb_', b)])
                    stt(A_ys[:, gs, :].rearrange("p g m -> p (g m)"), f4(xa[b]), sgc[:, 1:2], f4(xb_[b]), ALU.mult, ALU.subtract,
                        [('xa', b), ('xb_', b), 'sgc'], ['A_ys'])
                    tt('dve', xa[b][:], vexp(Bbs), bexp(Pr2, 16), ALU.mult, ['Bbs', 'Pr2'], [('xa', b)])
                    tt('pool', xb_[b][:], vexp(Bbw), bexp(Pi2, 16), ALU.mult, ['Bbw', 'Pi2'], [('xb_', b)])
                    stt(BL[:].rearrange("p g m -> p (g m)"), f4(xa[b]), sgc[:, 1:2], f4(xb_[b]), ALU.mult, ALU.subtract,
                        [('xa', b), ('xb_', b), 'sgc'], ['BL'])
                    tt('dve', xa[b][:], vexp(Cs), bexp(Pr2, 24), ALU.mult, ['Cs', 'Pr2'], [('xa', b)])
                    tt('pool', xb_[b][:], vexp(Cw), bexp(Pi2, 24), ALU.mult, ['Cw', 'Pi2'], [('xb_', b)])
                    stt(CL[:].rearrange("p g m -> p (g m)"), f4(xb_[b]), sgc[:, 0:1], f4(xa[b]), ALU.mult, ALU.add,
                        [('xa', b), ('xb_', b), 'sgc'], ['CL'])
                    for g4 in range(2):
                        pt, pk = nps()
                        for i in range(4):
                            gl = g4 * 4 + i
                            tr(pt[:, i * 128:(i + 1) * 128], X1[:, gl, :], ['X1'], [pk], inc=(i == 3))
                        cp(evac_eng(), A_su[:, gb * 8 + g4 * 4:gb * 8 + g4 * 4 + 4, :].rearrange("p g m -> p (g m)"), pt[:, :], [pk], ['A_su'])
                        pt, pk = nps()
                        for i in range(4):
                            gl = g4 * 4 + i
                            mm(pt[:, i * 128:(i + 1) * 128], BL[:, gl, :], CL[:, gl, :], True, True, ['BL', 'CL'], [pk], inc=(i == 3))
                        for i in range(4):
                            g = gb * 8 + g4 * 4 + i
                            yb = i % 2
                            tt('dve', ytmp[yb][:], pt[:, i * 128:(i + 1) * 128], cm[:], ALU.mult, [pk, 'cm'], [('ytmp', yb)])
                            stt(A_yu[:, g, :], ident[:], Dcol[:, g:g + 1], ytmp[yb][:], ALU.mult, ALU.add, [('ytmp', yb), 'Dcol', 'ident'], ['A_yu'])
                P.emit(final=(STOP == 4))
            s5y.close()
            if STOP == 4:
                return nc

            with ExitStack() as ph:
                U = sb_t(ph, "U", [128, 32, NCH], BF16)
                H12 = sb_t(ph, "H12", [128, 2, 64, NCH], BF16)
                S1bf = sb_t(ph, "S1bf", [128, 64, NCH + 2], BF16)
                Y = sb_t(ph, "Y", [128, 8, NCH], BF16)
                Z = [sb_t(ph, "Z%d" % i, [128, 3, 64]) for i in range(2)]
                T1 = sb_t(ph, "T1", [128, 2, 64]); T2 = sb_t(ph, "T2", [128, 2, 64])
                LN = sb_t(ph, "LN", [128, 2, 64])
                ssin = sb_t(ph, "ssin", [64, 16, 2, 64])
                Zs = sb_t(ph, "Zs", [128, 2, 32, 16]); Ts1 = sb_t(ph, "Ts1", [128, 2, 32, 16]); Ts2 = sb_t(ph, "Ts2", [128, 2, 32, 16])
                fin = sb_t(ph, "fin", [128, 64, 16])
                fo = ssin[:].rearrange("g s c p -> g s (c p)")
                SF = sb_t(ph, "SF", [128, 64]); sfo = sb_t(ph, "sfo", [64, 128])
                for c_ in range(2):
                    dma('sp', ssin[:, :, c_, :], s5in[c_].rearrange("s g p -> g s p"), [], ['ssin'], 'ssin')
                ssw8 = sb_t(ph, "ssw8", [64, 8, 2, 64])
                L1a = LL[:, 0, :]; L2a = LL[:, 1, :]
                cp('dve', LN[:, 0, :], L2a, ['LL'], ['LN'])
                tsc('dve', LN[:, 1, :], L2a, -1.0, None, ALU.mult, None, ['LL'], ['LN'])
                L1b = L1a.unsqueeze(1).to_broadcast([128, 2, 64])
                zc = [0]

                def rec_step(n, col):
                    cur, nxt = Z[zc[0] % 2], Z[(zc[0] + 1) % 2]
                    pc_, pn_ = zc[0] % 2, (zc[0] + 1) % 2
                    zc[0] += 1
                    H = [slice(0, 32), slice(32, 64)]
                    for h in range(2):
                        tt('dve', T1[:, :, H[h]], cur[:, 0:2, H[h]], L1a[:, H[h]].unsqueeze(1).to_broadcast([128, 2, 32]), ALU.mult, [('Z', pc_, h), 'LL'], [('T1', h)])
                    for h in range(2):
                        tt('dve', T2[:, :, H[h]], cur[:, 1:3, H[h]], LN[:, :, H[h]], ALU.mult, [('Z', pc_, h), ('Z3', pc_, h), 'LN'], [('T2', h)])
                    for h in range(2):
                        tt('dve', T1[:, :, H[h]], T1[:, :, H[h]], T2[:, :, H[h]], ALU.add, [('T1', h), ('T2', h)], [('T1', h)])
                    for h in range(2):
                        tt('dve', nxt[:, 0:2, H[h]], T1[:, :, H[h]], H12[:, :, H[h], n], ALU.add, [('T1', h), 'H12'], [('Z', pn_, h)])
                    for h in range(2):
                        cp('act', nxt[:, 2, H[h]], nxt[:, 0, H[h]], [('Z', pn_, h)], [('Z3', pn_, h)])
                    if col is not None:
                        cp('act', S1bf[:, :, col], nxt[:, 0, :], [('Z', pn_, 0), ('Z', pn_, 1)], ['S1bf'])

                def load_u(src, g0, nch):
                    for q4 in range(4):
                        dma('sp', U[:, q4 * 8:(q4 + 1) * 8, 0:nch], src[g0 + q4 * 8:g0 + (q4 + 1) * 8].rearrange("g k s n -> (k s) g n"), ['dU', 'dU2'], ['U'], 'U')

                def stage_a(g0, nch):
                    for g3 in range(0, 32, 3):
                        ng = min(3, 32 - g3)
                        pa_, pak_ = nps(); pb_, pbk_ = nps()
                        for i in range(ng):
                            gl = g3 + i; g = g0 + gl
                            cs = slice(i * nch, (i + 1) * nch)
                            mm(pa_[:, cs], A_su[:, g, :], U[:, gl, 0:nch], True, True, ['A_su', 'U'], [pak_], inc=(i == ng - 1))
                            mm(pb_[0:64, cs], A_su[:, g, 64:128], U[:, gl, 0:nch], True, True, ['A_su', 'U'], [pbk_], inc=False)
                            mm(pb_[64:128, cs], A_su[:, g, 0:64], U[:, gl, 0:nch], True, True, ['A_su', 'U'], [pbk_], inc=(i == ng - 1))
                        gsl = slice(g0 + g3, g0 + g3 + ng)
                        cp('dve', H12[:, 0, gsl, 0:nch], pa_[:, 0:ng * nch].rearrange("p (g n) -> p g n", n=nch), [pak_], ['H12'])
                        cp('dve', H12[:, 1, gsl, 0:nch], pb_[:, 0:ng * nch].rearrange("p (g n) -> p g n", n=nch), [pbk_], ['H12'])

                SK = os.environ.get('S5SKIP', '')
                for hf2 in (range(2) if 'p' not in SK else []):
                    load_u(dU2, hf2 * 32, NCP)
                    stage_a(hf2 * 32, NCP)
                memset('dve', Z[0][:], 0.0, [('Z', 0, 0), ('Z', 0, 1), ('Z3', 0, 0), ('Z3', 0, 1)])
                for n in (range(NCP) if 'p' not in SK else []):
                    rec_step(n, None)
                for hf2 in range(2):
                    load_u(dU, hf2 * 32, NCH)
                    stage_a(hf2 * 32, NCH)
                zk = lambda: [('Z', zc[0] % 2, 0), ('Z', zc[0] % 2, 1)]
                cp('act', S1bf[:, :, 0], Z[zc[0] % 2][:, 0, :], zk(), ['S1bf'])
                for n in (range(16) if 'r' not in SK else []):
                    rec_step(n, n + 1 if n < 15 else None)
                zcur = Z[zc[0] % 2]
                z3k = [('Z3', zc[0] % 2, 0), ('Z3', zc[0] % 2, 1)]
                tsc('dve', zcur[:].rearrange("p a g -> p (a g)"), zcur[:].rearrange("p a g -> p (a g)"), maskt[:, 0:1], None, ALU.mult, None, zk() + z3k + ['mask'], zk() + z3k)
                cp('act', S1bf[:, :, 16], zcur[:, 0, :], zk(), ['S1bf'])
                for n in (range(16, 144) if 'r' not in SK else []):
                    rec_step(n, n + 1 if n < 143 else None)
                cp('act', SF[:], Z[zc[0] % 2][:, 0, :], zk(), ['SF'])
                for hf2 in (range(2) if 's' not in SK else []):
                    g0 = hf2 * 32
                    L1h = LL[:, 0, g0:g0 + 32]
                    pt, pk = nps()
                    for sq in range(16):
                        tr(pt[:, sq * 32:(sq + 1) * 32], ssin[g0:g0 + 32, sq].rearrange("g c p -> g (c p)"), ['ssin'], [pk], inc=(sq == 15))
                    cp('dve', Zs[:, 0].rearrange("p g s -> p s g"), pt[:, :].rearrange("p (s g) -> p s g", g=32), [pk], ['Zs'])
                    cp('dve', S1bf[:, g0:g0 + 32, 144:160].rearrange("p g s -> p s g"), pt[:, :].rearrange("p (s g) -> p s g", g=32), [pk], ['S1bf'])
                    pt, pk = nps()
                    for s8 in range(2):
                        for c_ in range(2):
                            dma('sp', ssw8[:, :, 1 - c_, :], s5in[c_, s8 * 8:(s8 + 1) * 8].rearrange("s g p -> g s p"), [], ['ssw8'], 'ssw8')
                        for q8 in range(8):
                            sq = s8 * 8 + q8
                            tr(pt[:, sq * 32:(sq + 1) * 32], ssw8[g0:g0 + 32, q8].rearrange("g c p -> g (c p)"), ['ssw8'], [pk], inc=(q8 == 7))
                    cp('dve', Zs[:, 1].rearrange("p g s -> p s g"), pt[:, :].rearrange("p (s g) -> p s g", g=32), [pk], ['Zs'])
                    L1s = L1h.unsqueeze(1).unsqueeze(3).to_broadcast([128, 2, 32, 16])
                    tt('dve', Ts1[:], Zs[:, 0:2], L1s, ALU.mult, ['Zs', 'LL'], ['Ts1'])
                    tt('dve', Ts2[:, 0], Zs[:, 1], LN[:, 0, g0:g0 + 32].unsqueeze(2).to_broadcast([128, 32, 16]), ALU.mult, ['Zs', 'LN'], ['Ts2'])
                    tt('dve', Ts2[:, 1], Zs[:, 0], LN[:, 1, g0:g0 + 32].unsqueeze(2).to_broadcast([128, 32, 16]), ALU.mult, ['Zs', 'LN'], ['Ts2'])
                    tt('dve', Ts1[:], Ts1[:], Ts2[:], ALU.add, ['Ts1', 'Ts2'], ['Ts1'])
                    tt('dve', Ts1[:], Ts1[:], H12[:, :, g0:g0 + 32, 144:160], ALU.add, ['Ts1', 'H12'], ['Ts1'])
                    M1s = LL[:, 2, g0:g0 + 32].unsqueeze(2).to_broadcast([128, 32, 16])
                    M2s = LL[:, 3, g0:g0 + 32].unsqueeze(2).to_broadcast([128, 32, 16])
                    finh = fin[:, g0:g0 + 32, :]
                    tt('dve', finh, Ts1[:, 0], M1s, ALU.mult, ['Ts1', 'LL'], ['fin'])
                    tt('dve', Ts2[:, 0], Ts1[:, 1], M2s, ALU.mult, ['Ts1', 'LL'], ['Ts2'])
                    tt('dve', finh, finh, Ts2[:, 0], ALU.add, ['fin', 'Ts2'], ['fin'])
                for hf2 in (range(2) if 'c' not in SK else []):
                    g0 = hf2 * 32
                    if hf2 == 0:
                        load_u(dU, 0, NCH)
                    for gq in range(4):
                        for g3 in range(0, 8, 3):
                            ng = min(3, 8 - g3)
                            py, pyk = nps()
                            for i in range(ng):
                                gl = gq * 8 + g3 + i; g = g0 + gl
                                cs = slice(i * NCH, (i + 1) * NCH)
                                mm(py[:, cs], A_ys[:, g, :], S1bf[:, g, 0:NCH], True, False, ['A_ys', 'S1bf'], [pyk], inc=False)
                                mm(py[:, cs], A_yu[:, g, :], U[:, gl, :], False, True, ['A_yu', 'U'], [pyk], inc=(i == ng - 1))
                            cp('dve', Y[:, g3:g3 + ng, :], py[:, 0:ng * NCH].rearrange("p (g n) -> p g n", n=NCH), [pyk], ['Y'])
                        gb = g0 + gq * 8
                        dma('sp', dY[gb:gb + 8].rearrange("g j t n -> (j t) g n"), Y[:], ['Y'], ['dY'], 'Y')
                    if hf2 == 0:
                        load_u(dU, 32, NCH)
                for s4 in range(4):
                    pt, pk = nps()
                    for i in range(4):
                        sq = s4 * 4 + i
                        tr(pt[0:64, i * 128:(i + 1) * 128], fin[:, :, sq], ['fin'], [pk], inc=(i == 3))
                    cp(evac_eng(), fo[:, s4 * 4:s4 * 4 + 4, :].rearrange("g s m -> g (s m)"), pt[0:64, :], [pk], ['fo', 'ssin'])
                dma('sp', s5s.rearrange("s g m -> g s m"), fo, ['fo'], [], 'fo')
                pt, pk = nps()
                tr(pt[0:64, 0:128], SF[:], ['SF'], [pk])
                cp('dve', sfo[:], pt[0:64, 0:128], [pk], ['sfo'])
                dma('sp', s5p, sfo[:], ['sfo'], [], 'sfo')
                P.emit(final=(STOP == 5))
            if STOP == 5:
                return nc

        with ExitStack() as ph:
            yTp = sb_t(ph, "yTp", [128, 8, 8, NCH], BF16)
            yg = sb_t(ph, "yg", [128, 8, TT], BF16)
            sgm = [sb_t(ph, "sgm%d" % i, [128, 512], BF16) for i in range(2)]
            alloc_ring(ph)
            for ch in range(8):
                dma('sp', yTp[:, ch], dY[ch * 8:(ch + 1) * 8].rearrange("g j t n -> (g j) t n"), ['dY'], [('yTp', ch)], 'yTp')
            P.settle([('yTp', ch) for ch in range(8)], 'yTp')
            for ch in range(8):
                act(yg[:, ch, 0:TM].rearrange("p (n s) -> p n s", s=8), yTp[:, ch, :, 16:144].rearrange("p s n -> p n s"), AF.Gelu_apprx_tanh,
                    [('yTp', ch)], ['yg'])
                act(yg[:, ch, TM:TF].rearrange("p (j i) -> p j i", i=4), yTp[:, ch, 0:4, 144:160].rearrange("p i j -> p j i"), AF.Gelu_apprx_tanh,
                    [('yTp', ch)], ['yg'])
                act(yg[:, ch, TF:TT].rearrange("p (n s) -> p n s", s=8), yTp[:, ch, :, 0:16].rearrange("p s n -> p n s"), AF.Gelu_apprx_tanh,
                    [('yTp', ch)], ['yg'])
            if os.environ.get('DBGSEL') == 'yTp':
                dma('pool', dbg.rearrange("p c t -> p (c t)")[:, 0:8 * 8 * NCH], yTp[:].rearrange("p a b c -> p (a b c)"), [('yTp', ch) for ch in range(8)], [], 'dbg')
                P.emit(final=True)
                return nc
            if os.environ.get('DBGSEL') == 'yg':
                cp('dve', actT[:, 8:16, :], yg[:], ['yg'], [('actT', ti) for ti in range(NTILE)])
            for pi in (range(4) if os.environ.get('DBGSEL') != 'yg' else []):
                wt, wk = nwr()
                dma('pool', wt[:, 0:8, :], w_glu[:, pi * 256:(pi + 1) * 256].rearrange("(c p) n -> p c n", p=128), [], [wk], 'wr%d' % wk[1])
                for half in range(2):
                    ch = pi * 2 + half
                    for gi, (c0, gn) in enumerate(CGRP):
                        pt, pk = nps()
                        for kc in range(8):
                            mm(pt[:, 0:gn], wt[:, kc, half * 128:(half + 1) * 128], yg[:, kc, c0:c0 + gn], kc == 0, kc == 7, [wk, 'yg'], [pk], inc=(kc == 7))
                        sb_i = (ch * 3 + gi) % 2
                        act(sgm[sb_i][:, 0:gn], pt[:, 0:gn], AF.Sigmoid, [pk, 'bgT'], [('sgm', sb_i)], bias=bgT[:, ch:ch + 1])
                        tt('dve', actT[:, 8 + ch, c0:c0 + gn], sgm[sb_i][:, 0:gn], yg[:, ch, c0:c0 + gn], ALU.mult, [('sgm', sb_i), 'yg'],
                           [('actT', ti) for ti in range(NTILE)])
            if STOP == 6:
                dbg_dump()
            P.emit(final=(STOP == 6))
        if STOP == 6:
            return nc

        xmid = sb_t(es, "xmid", [128, NTO, D])
        px = ExitStack()
        px.__enter__()
        xpf = sb_t(px, "xpf", [128, D])
        xrow = lambda ti, nt: (xmid[0:nt, ti, :] if ti < NTO else xpf[0:nt, :])

        def gated_add(ph, name):
            tmp = [sb_t(ph, "%s_t%d" % (name, i), [128, 256]) for i in range(4)]
            cnt = [0]

            def f(pt_ap, pk, ti, nt, col0, gate_ap):
                b = cnt[0] % 4
                cnt[0] += 1
                tt('dve', tmp[b][0:nt, :], pt_ap, gate_ap, ALU.mult, [pk, 'gate'], [(name, b)])
                xs_ = xrow(ti, nt)[:, col0:col0 + 256]
                tt('pool', xs_, xs_, tmp[b][0:nt, :], ALU.add, [(name, b), ('xm', ti)], [('xm', ti)])
            return f

        def load_sample_gate(Gs, col0):
            for j in range(16):
                dma('sp', Gs[4 * j:4 * j + 4, :], modd[1 + j:2 + j, col0:col0 + D].partition_broadcast(4), ['modd'], ['gate'], 'gates')

        with ExitStack() as ph:
            alloc_ring(ph)
            G1 = sb_t(ph, "G1", [128, D]); G1s = sb_t(ph, "G1s", [64, D])
            gadd = gated_add(ph, 'wo')
            dma('sp', G1[:], modd[0:1, 2 * D:3 * D].partition_broadcast(128), ['modd'], ['gate'], 'gatep')
            load_sample_gate(G1s, 2 * D)
            for ti, (t0, nt) in enumerate(TILES):
                src = xm[t0:t0 + nt, :] if ti < 8 else (xs if ti == 8 else xpre[896:1024, :])
                dma('sp', xrow(ti, nt), src, [], [('xm', ti)], 'xm%d' % (ti % 4))
            wo_q = {}

            def wo_get(i):
                while len(wo_q) < min(8, i + 4):
                    j = len(wo_q)
                    wt_, wk_ = nwr()
                    dma('pool', wt_[:], w_out[:, j * 256:(j + 1) * 256].rearrange("(c p) n -> p c n", p=128), [], [wk_], 'wr%d' % wk_[1])
                    wo_q[j] = (wt_, wk_)
                return wo_q[i]
            for pi in range(8):
                wt, wk = wo_get(pi)
                for ti, (t0, nt) in enumerate(TILES):
                    pt, pk = nps()
                    o0 = 0
                    for kc in range(KC):
                        mm(pt[0:nt, o0:o0 + 256], actT[:, kc, t0:t0 + nt], wt[:, kc, :], kc == 0, kc == KC - 1, [wk, ('actT', ti)], [pk], inc=(kc == KC - 1))
                    G = G1[0:nt, pi * 256:(pi + 1) * 256] if ti != 8 else G1s[0:nt, pi * 256:(pi + 1) * 256]
                    gadd(pt[0:nt, o0:o0 + 256], pk, ti, nt, pi * 256, G)
            P.emit(final=(STOP == 7))
        if STOP == 7:
            return nc

        def halo_cols():
            tsc('dve', actT[:, :, TF:TF + 2], actT[:, :, TT - 2:TT], maskt[:, 0:1], None, ALU.mult, None, [('actT', 9), 'mask'], [('actT', 9)])
        norm_phase(1, lambda ph, ti, t0, nt: (xrow(ti, nt), ('xm', ti)), 8, post=halo_cols)
        px.close()
        if STOP == 8:
            return nc

        with ExitStack() as ph:
            alloc_ring(ph, 4, 128)
            G2 = sb_t(ph, "G2", [128, D]); G2s = sb_t(ph, "G2s", [64, D])
            gadd = gated_add(ph, 'wd')
            hid = sb_t(ph, "hid", [128, 8, TF], BF16)
            stgm = [sb_t(ph, "stgm%d" % i, [128, 2 + TM]) for i in range(2)]
            stgs = [sb_t(ph, "stgs%d" % i, [128, 16, 6]) for i in range(2)]
            acc = [sb_t(ph, "acc%d" % i, [128, TF]) for i in range(3)]
            cglob = [0]
            hal = sb_t(ph, "hal", [128, 2, 8, 32])
            scv = sb_t(ph, "scv", [64, 1024])
            UPS = sb_t(ph, "UPS", [128, 2, 8, 64])
            UPL = sb_t(ph, "UPL", [128, 2, 88]); uplT = sb_t(ph, "uplT", [128, 2, 128])
            dma('sp', G2[:], modd[0:1, 5 * D:6 * D].partition_broadcast(128), ['modd'], ['gate'], 'gatep')
            load_sample_gate(G2s, 5 * D)
            FG = [(2, 1024, 66), (0, 0, 512), (1, 512, 512)]
            wlist = []
            c0_ = 0
            for nk_ in [8, 8, 8, 8, 8, 4]:
                for cl_ in range(nk_):
                    for part_ in range(2):
                        wlist.append(('up', part_ * DFF + (c0_ + cl_) * 128, 0, 0))
                for pi_ in range(8):
                    wlist.append(('down', c0_, nk_, pi_))
                c0_ += nk_
            wst = {'n': 0, 'got': {}}

            def wget(i):
                while wst['n'] < min(len(wlist), i + 3):
                    j = wst['n']
                    kind_, a_, b_, c_ = wlist[j]
                    wt_, wk_ = nwr()
                    if kind_ == 'up':
                        dma('pool', wt_[:], w_up[:, a_:a_ + 128].rearrange("(c p) n -> p c n", p=128), [], [wk_], 'wr%d' % wk_[1])
                    else:
                        wd_ = wt_[:].rearrange("p c n -> p (c n)").rearrange("p (c n) -> p c n", n=256)
                        dma('pool', wd_[:, 0:b_, :], w_down[a_ * 128:(a_ + b_) * 128, c_ * 256:(c_ + 1) * 256].rearrange("(c p) n -> p c n", p=128),
                            [], [wk_], 'wr%d' % wk_[1])
                    wst['got'][j] = (wt_, wk_)
                    wst['n'] += 1
                return wst['got'].pop(i)
            wi = [0]
            c0 = 0
            for nk in [8, 8, 8, 8, 8, 4]:
                for part in range(2):
                    dma('sp', scv[0:32, 0:nk * 128], sconv[:, part * DFF + c0 * 128:part * DFF + (c0 + nk) * 128], [], ['scv'], 'scv')
                    pt, pk = nps()
                    for cl in range(nk):
                        tr(pt[:, cl * 32:(cl + 1) * 32], scv[0:32, cl * 128:(cl + 1) * 128], ['scv'], [pk], inc=(cl == nk - 1))
                    cp('dve', hal[:, part, 0:nk, :].rearrange("p c r -> p (c r)"), pt[:, 0:nk * 32], [pk], ['hal'])
                for cl in range(nk):
                    wts = [wget(wi[0]), wget(wi[0] + 1)]
                    wi[0] += 2
                    if True:
                        c = c0 + cl
                        par = cglob[0] % 2
                        cglob[0] += 1
                        for part in range(2):
                            wt, wk = wts[part]
                            cidx = c + 44 * part
                            ai = par if part == 0 else 2
                            sm, ss, ac = stgm[part], stgs[part], acc[ai]
                            AK = ('acc', ai)
                            for (gi, g0, gn) in FG:
                                pt, pk = nps()
                                for kc in range(KC):
                                    mm(pt[:, 0:gn], wt[:, kc, :], actT[:, kc, g0:g0 + gn], kc == 0, kc == KC - 1,
                                       [wk] + [('actT', ti) for ti in range(NTILE)], [pk], inc=(kc == KC - 1))
                                if gi < 2:
                                    cp('act', sm[:, 2 + g0:2 + g0 + gn], pt[:, 0:gn], [pk], [('stgm', part)])
                                else:
                                    cp('act', ss[:, :, 2:6], pt[:, 0:64].rearrange("p (j i) -> p j i", i=4), [pk], [('stgs', part)])
                                    cp('act', sm[:, 0:2], pt[:, 64:66], [pk], [('stgm', part)])
                            cp('pool', ss[:, :, 0:2], hal[:, part, cl, :].rearrange("p (j i) -> p j i", i=2), ['hal'], [('stgs', part)])
                            w0 = cwT[:, cidx, 0:1]; w1 = cwT[:, cidx, 1:2]; w2 = cwT[:, cidx, 2:3]; bb = cwT[:, cidx, 3:4]
                            acs = ac[:, TM:TF].rearrange("p (j i) -> p j i", i=4)
                            act(ac[:, 0:TM], sm[:, 2:2 + TM], AF.Identity, [('stgm', part), 'cwT'], [AK], bias=bb, scale=w2)
                            act(acs, ss[:, :, 2:6], AF.Identity, [('stgs', part), 'cwT'], [AK], bias=bb, scale=w2)
                            stt(ac[:, 0:TM], sm[:, 1:1 + TM], w1, ac[:, 0:TM], ALU.mult, ALU.add, [('stgm', part), 'cwT', AK], [AK])
                            stt(ac[:, 0:TM], sm[:, 0:TM], w0, ac[:, 0:TM], ALU.mult, ALU.add, [('stgm', part), 'cwT', AK], [AK])
                            stt(acs, ss[:, :, 1:5], w1, acs, ALU.mult, ALU.add, [('stgs', part), 'cwT', AK], [AK])
                            stt(acs, ss[:, :, 0:4], w0, acs, ALU.mult, ALU.add, [('stgs', part), 'cwT', AK], [AK])
                            cp('pool', UPL[:, :, cidx], sm[:, TM:TM + 2], [('stgm', part)], ['UPL'])
                            cp('pool', UPS[:, part, cl, :].rearrange("p (j i) -> p j i", i=4), ss[:, :, 2:6], [('stgs', part)], ['UPS'])
                            if part == 0:
                                act(ac[:, :], ac[:, :], AF.Gelu_apprx_tanh, [AK], [AK])
                        tt('dve', hid[:, cl, :], acc[par][:, :], acc[2][:, :], ALU.mult, [('acc', par), ('acc', 2)], ['hid'])
                for part in range(2):
                    for c4 in range(0, nk, 4):
                        pt, pk = nps()
                        for i in range(4):
                            tr(pt[0:64, i * 128:(i + 1) * 128], UPS[:, part, c4 + i, :], ['UPS'], [pk], inc=(i == 3))
                        cp('dve', scv[:, c4 * 128:(c4 + 4) * 128], pt[0:64, :], [pk], ['scv'])
                    dma('sp', convs[:, part * DFF + c0 * 128:part * DFF + (c0 + nk) * 128], scv[:, 0:nk * 128], ['scv'], [], 'scv')
                for pi in range(8):
                    wt, wk = wget(wi[0])
                    wi[0] += 1
                    wd = wt[:].rearrange("p c n -> p (c n)").rearrange("p (c n) -> p c n", n=256)
                    for ti, (t0, nt) in enumerate(TILES[0:NTO]):
                        pt, pk = nps()
                        o0 = 0
                        for cl in range(nk):
                            mm(pt[0:nt, o0:o0 + 256], hid[:, cl, t0:t0 + nt], wd[:, cl, :], cl == 0, cl == nk - 1, [wk, 'hid'], [pk], inc=(cl == nk - 1))
                        G = G2[0:nt, pi * 256:(pi + 1) * 256] if ti < 8 else G2s[0:nt, pi * 256:(pi + 1) * 256]
                        gadd(pt[0:nt, o0:o0 + 256], pk, ti, nt, pi * 256, G)
                c0 += nk
            pt, pk = nps()
            UPLf = UPL[:].rearrange("p i c -> p (i c)")
            tr(pt[:, 0:128], UPLf[:, 0:128], ['UPL'], [pk], inc=False)
            tr(pt[0:48, 128:256], UPLf[:, 128:176], ['UPL'], [pk])
            cp('dve', uplT[:, 0, :], pt[:, 0:128], [pk], ['uplT'])
            cp('dve', uplT[0:48, 1, :], pt[0:48, 128:256], [pk], ['uplT'])
            cpv = convp.rearrange("i (c p) -> (i c) p", p=128)
            dma('sp', cpv[0:128, :], uplT[:, 0, :], ['uplT'], [], 'convp')
            dma('sp', cpv[128:176, :], uplT[0:48, 1, :], ['uplT'], [], 'convp')
            P.emit(final=(STOP == 9))
        if STOP == 9:
            return nc

        with ExitStack() as ph:
            fnb = sb_t(ph, "fnb", [128, D])
            yo = [sb_t(ph, "yo%d" % i, [128, D]) for i in range(2)]
            st = [sb_t(ph, "fst%d" % i, [128, 4]) for i in range(2)]
            dma('sp', fnb[:], nrm[2:3, :].partition_broadcast(128), [], ['fnb'], 'fnb')
            for ti, (t0, nt) in enumerate(TILES[0:NTO]):
                b = ti % 2
                xt = xmid[0:nt, ti, :]
                act(yo[b][0:nt, :], xt, AF.Square, [('xm', ti)], [('yo', b), ('fst', b)], accum=st[b][0:nt, 0:1])
                rstd_chain(st[b][0:nt, 0:1], st[b][0:nt, 1:2], st[b][0:nt, 2:3], nt, 1.0 / D, ('fst', b))
                stt(yo[b][0:nt, :], xt, st[b][0:nt, 2:3], fnb[0:nt, :], ALU.mult, ALU.mult, [('xm', ti), ('fst', b), 'fnb'], [('yo', b)])
                dst = yp[t0:t0 + nt, :] if ti < 8 else ys
                dma('sp', dst, yo[b][0:nt, :], [('yo', b)], [], 'yo%d' % b)
            P.emit(final=True)
    return nc


_CACHE = {}


def _consts():
    c = {}
    c["c_ident"] = np.eye(128, dtype=np.float32)
    gmk = np.zeros((128, 6, 128), np.float32)
    s = np.arange(128)[:, None]; t = np.arange(128)[None, :]
    gmk[:, 0, :] = np.where(s <= t, -1.0 / 16, 0.0)
    gmk[:, 1, :] = np.where(s > t, -1.0 / 16, 0.0)
    gmk[:, 2, :] = np.where(s <= t, 1.0, 0.0)
    same = (s // 4) == (t // 4)
    gmk[:, 3, :] = np.where(same & (s <= t), -1.0 / 16, 0.0)
    gmk[:, 4, :] = np.where(same & (s > t), -1.0 / 16, 0.0)
    gmk[:, 5, :] = np.where(same & (s <= t), 1.0, 0.0)
    c["c_gm"] = gmk
    sq = np.zeros((64, 2, 16), np.float32)
    for tkn in range(64):
        sq[tkn, 0, tkn // 4] = 1.0
        sq[tkn, 1, tkn // 4] = -1.0 / 16
    c["c_seq"] = sq
    bmk = np.zeros((128, 16, 64), np.float32)
    for j in range(16):
        bmk[:, j, 4 * j:4 * j + 4] = 1.0
    c["c_bm"] = bmk
    ev = np.array(list(range(7, -1, -1)) + list(range(1, 9)) + list(range(0, -8, -1)) + list(range(0, 8)) + [8, -4], np.float32)
    c["c_ev"] = np.tile(ev[None, :], (128, 1)).astype(np.float32)
    sg = np.zeros((128, 2), np.float32)
    sg[:64, 0] = -1; sg[64:, 0] = 1; sg[:64, 1] = 1; sg[64:, 1] = -1
    c["c_sg"] = sg
    ks = np.arange(128)
    c["c_cm"] = ((ks[None, :] % 8) >= (ks[:, None] % 8)).astype(np.float32)
    return c


def kernel(_cores=None, _debug=False, **inp):
    f = lambda a: np.ascontiguousarray(np.asarray(a, dtype=np.float32))
    cores = list(range(NCORES)) if _cores is None else _cores
    key = ('nc', _debug)
    if key not in _CACHE:
        _CACHE[key] = build_program(_debug)
    nc = _CACHE[key]
    consts = _consts()
    shared = {
        "w_ada": f(inp["w_ada"][0]), "b_ada": f(inp["b_ada"][0][None, :]),
        "nrm": f(np.stack([inp["norm1"][0], inp["norm2"][0], inp["final_norm"]])),
        "w_in": f(inp["w_in"][0]), "w_a2": f(inp["w_a2"][0]), "b_a2": f(inp["b_a2"][0][None, :]),
        "gla_norm": f(inp["gla_norm"][0][None, :]),
        "lam": f(np.stack([inp["s5_lam_re"][0], inp["s5_lam_im"][0]])), "lstep": f(inp["s5_log_step"][0][None, :]),
        "sb": f(np.stack([inp["s5_b_re"][0], inp["s5_b_im"][0]])), "sc": f(np.stack([inp["s5_c_re"][0], inp["s5_c_im"][0]])),
        "sd": f(inp["s5_d"][0]), "w_glu": f(inp["w_glu"][0]), "b_glu": f(inp["b_glu"][0][None, :]),
        "w_out": f(inp["w_out"][0]), "w_up": f(inp["w_up"][0]),
        "cw": f(np.concatenate([inp["conv_w"][0], inp["conv_b"][0][None, :]], 0)), "w_down": f(inp["w_down"][0]),
    }
    shared.update(consts)
    in_maps = []
    for c in cores:
        s, hf = c // 2, c % 2
        m = dict(shared)
        m["xm"] = f(inp["x_prompt"][s, hf * TM:(hf + 1) * TM])
        m["xpre"] = f(inp["x_prompt"][s, 0:TM])
        m["c_mask"] = np.full((128, 1), float(hf), np.float32)
        m["xs"] = f(inp["x_sample"][16 * c:16 * c + 16].reshape(64, D))
        m["cc"] = f(np.concatenate([inp["c_prompt"][s:s + 1], inp["c_sample"][16 * c:16 * c + 16]], 0))
        m["sgla"] = f(inp["state_gla"][0, 16 * c:16 * c + 16])
        m["s5in"] = f(np.stack([inp["state_s5_re"][0, 16 * c:16 * c + 16], inp["state_s5_im"][0, 16 * c:16 * c + 16]]))
        m["sconv"] = f(inp["state_conv"][0, 16 * c:16 * c + 16].reshape(32, 2 * DFF))
        in_maps.append(m)
    res = run_bass_kernel_spmd(nc, in_maps, core_ids=list(range(len(cores))))
    R = res.results
    if _cores is not None:
        return R
    y_prompt = np.zeros((4, 2048, D), np.float32); y_sample = np.zeros((128, 4, D), np.float32)
    gla_p = np.zeros((1, 4, 4, 128, 256), np.float32); s5re_p = np.zeros((1, 4, 64, 64), np.float32)
    s5im_p = np.zeros((1, 4, 64, 64), np.float32); conv_p = np.zeros((1, 4, 2, 2 * DFF), np.float32)
    gla_s = np.zeros((1, 128, 4, 128, 256), np.float32); s5re_s = np.zeros((1, 128, 64, 64), np.float32)
    s5im_s = np.zeros((1, 128, 64, 64), np.float32); conv_s = np.zeros((1, 128, 2, 2 * DFF), np.float32)
    for c in range(NCORES):
        s, hf = c // 2, c % 2
        r = R[c]
        y_prompt[s, hf * TM:(hf + 1) * TM] = r["yp"]
        y_sample[16 * c:16 * c + 16] = r["ys"].reshape(16, 4, D)
        gla_s[0, 16 * c:16 * c + 16] = r["glas"]
        s5 = r["s5s"].reshape(16, 64, 2, 64)
        s5re_s[0, 16 * c:16 * c + 16] = s5[:, :, 0]; s5im_s[0, 16 * c:16 * c + 16] = s5[:, :, 1]
        conv_s[0, 16 * c:16 * c + 16] = r["convs"].reshape(16, 4, 2 * DFF)[:, 2:4]
        if hf == 1:
            gla_p[0, s] = r["glap"]
            sp5 = r["s5p"].reshape(64, 2, 64)
            s5re_p[0, s] = sp5[:, 0]; s5im_p[0, s] = sp5[:, 1]
            conv_p[0, s] = r["convp"]
    return (y_prompt, y_sample, gla_p, s5re_p, s5im_p, conv_p, gla_s, s5re_s, s5im_s, conv_s)
```
